# Optimizing a Trainium2 kernel written in Bass

```python
import jax, jax.numpy as jnp
from jax import lax
import numpy as np

D_MODEL = 1024
BATCH = 2
SEQ = 8192
DEPTH = 4

N_MIXERS = 2
N_CONV_LAYERS = (DEPTH + N_MIXERS - 1) // N_MIXERS
N_NSA_LAYERS = DEPTH // N_MIXERS
CONV_WIDTH = 31
N_HEADS = 16
HEAD_DIM = 64
N_KV = 4
GROUP = N_HEADS // N_KV
ROT_DIM = HEAD_DIM // 4
ROPE_THETA = 500000.0
CMP_LEN = 32
CMP_STRIDE = 16
CMP_HIDDEN = 256
SEL_LEN = 64
SEL_TOPK = 16
WINDOW = 512
Q_BLOCK = 128
D_FF = 2816
FFN_CONV_WIDTH = 3
EPS = 1e-6
PROJ_SIZES = (N_HEADS * HEAD_DIM,) + (N_KV * HEAD_DIM,) * 6 + (3 * N_HEADS,)
PROJ_COLS = sum(PROJ_SIZES)

kernel_name = "hybrid_conformer_nsa_convffn"


def rmsnorm(x, g):
    xf = x.astype(jnp.float32)
    y = xf * lax.rsqrt(jnp.mean(xf * xf, axis=-1, keepdims=True) + EPS)
    return (y * g.astype(jnp.float32)).astype(x.dtype)


def layernorm(x, g, b):
    xf = x.astype(jnp.float32)
    mu = jnp.mean(xf, axis=-1, keepdims=True)
    var = jnp.mean(jnp.square(xf - mu), axis=-1, keepdims=True)
    y = (xf - mu) * lax.rsqrt(var + EPS)
    return (y * g.astype(jnp.float32) + b.astype(jnp.float32)).astype(x.dtype)


def causal_dwconv(x, w, b):
    width, ch = w.shape
    y = lax.conv_general_dilated(
        x, w[:, None, :].astype(x.dtype), window_strides=(1,),
        padding=[(width - 1, 0)], dimension_numbers=("NWC", "WIO", "NWC"),
        feature_group_count=ch)
    return y + b.astype(x.dtype)


def partial_rope(x):
    seq = x.shape[1]
    half = ROT_DIM // 2
    inv_freq = ROPE_THETA ** (-jnp.arange(half, dtype=jnp.float32) * (2.0 / ROT_DIM))
    ang = jnp.arange(seq, dtype=jnp.float32)[:, None] * inv_freq[None, :]
    cos = jnp.cos(ang)[:, None, :]
    sin = jnp.sin(ang)[:, None, :]
    xf = x.astype(jnp.float32)
    x1 = xf[..., :half]
    x2 = xf[..., half:ROT_DIM]
    out = jnp.concatenate([x1 * cos - x2 * sin, x2 * cos + x1 * sin, xf[..., ROT_DIM:]], axis=-1)
    return out.astype(x.dtype)


def masked_softmax(s, mask):
    s = jnp.where(mask, s.astype(jnp.float32), -1e30)
    m = jnp.max(s, axis=-1, keepdims=True)
    e = jnp.where(mask, jnp.exp(s - m), 0.0)
    return e / jnp.maximum(jnp.sum(e, axis=-1, keepdims=True), 1e-30)


def cmp_to_sel_matrix(n_cmp, n_sel):
    start_c = np.arange(n_cmp)[:, None] * CMP_STRIDE
    start_s = np.arange(n_sel)[None, :] * SEL_LEN
    ov = np.minimum(start_c + CMP_LEN, start_s + SEL_LEN) - np.maximum(start_c, start_s)
    return jnp.asarray(np.maximum(ov, 0).astype(np.float32) / CMP_LEN)


def conformer_conv(h, w_pw1, b_pw1, w_dw, b_dw, ln_g, ln_b, w_pw2, b_pw2):
    u = h @ w_pw1 + b_pw1
    a, gate = jnp.split(u, 2, axis=-1)
    u = a * jax.nn.sigmoid(gate)
    u = causal_dwconv(u, w_dw, b_dw)
    u = jax.nn.silu(layernorm(u, ln_g, ln_b))
    return u @ w_pw2 + b_pw2


def compress(blocks, pe, w1, w2):
    b, g, n, l, d = blocks.shape
    z = (blocks + pe).reshape(b, g, n, l * d)
    return jax.nn.silu(z @ w1) @ w2


def nsa_attention(h, w_in, q_g, kc_g, ks_g, kw_g, pe_k, pe_v, ck_w1, ck_w2, cv_w1, cv_w2, w_out):
    bsz, seq, _ = h.shape
    n_cmp = (seq - CMP_LEN) // CMP_STRIDE + 1
    n_sel = seq // SEL_LEN
    k_top = min(SEL_TOPK, n_sel)
    n_qb = seq // Q_BLOCK
    scale = HEAD_DIM ** -0.5

    splits = [int(c) for c in np.cumsum(PROJ_SIZES)[:-1]]
    q, kc, vc, ks, vs, kw, vw, gl = jnp.split(h @ w_in, splits, axis=-1)
    heads = lambda t, n: t.reshape(bsz, seq, n, HEAD_DIM)
    to_groups = lambda t: t.reshape(bsz, seq, N_KV, GROUP, HEAD_DIM).transpose(0, 2, 3, 1, 4)
    q = rmsnorm(heads(q, N_HEADS), q_g)
    q_nope = to_groups(q)
    q_rope = to_groups(partial_rope(q))
    gates = jax.nn.sigmoid(gl.astype(jnp.float32)).astype(h.dtype)
    gates = gates.reshape(bsz, seq, N_KV, GROUP, 3).transpose(0, 2, 3, 1, 4)

    blk_idx = np.arange(n_cmp)[:, None] * CMP_STRIDE + np.arange(CMP_LEN)[None, :]
    kc_blk = heads(kc, N_KV)[:, blk_idx].transpose(0, 3, 1, 2, 4)
    vc_blk = heads(vc, N_KV)[:, blk_idx].transpose(0, 3, 1, 2, 4)
    k_cmp = rmsnorm(compress(kc_blk, pe_k, ck_w1, ck_w2), kc_g)
    v_cmp = compress(vc_blk, pe_v, cv_w1, cv_w2)
    cmp_end = jnp.asarray(np.arange(n_cmp) * CMP_STRIDE + CMP_LEN - 1)
    cmp_map = cmp_to_sel_matrix(n_cmp, n_sel)

    k_sel = partial_rope(rmsnorm(heads(ks, N_KV), ks_g)).transpose(0, 2, 1, 3)
    k_sel = k_sel.reshape(bsz, N_KV, n_sel, SEL_LEN, HEAD_DIM)
    v_sel = heads(vs, N_KV).transpose(0, 2, 1, 3).reshape(bsz, N_KV, n_sel, SEL_LEN, HEAD_DIM)

    pad = ((0, 0), (0, 0), (WINDOW, 0), (0, 0))
    k_win = jnp.pad(partial_rope(rmsnorm(heads(kw, N_KV), kw_g)).transpose(0, 2, 1, 3), pad)
    v_win = jnp.pad(heads(vw, N_KV).transpose(0, 2, 1, 3), pad)

    b_ix = jnp.arange(bsz)[:, None, None, None]
    g_ix = jnp.arange(N_KV)[None, :, None, None]
    sel_off = jnp.arange(SEL_LEN)
    win_off = jnp.arange(Q_BLOCK + WINDOW)
    sel_ids = jnp.arange(n_sel)

    def query_block(qb):
        t0 = qb * Q_BLOCK
        pos = t0 + jnp.arange(Q_BLOCK)
        qn = lax.dynamic_slice_in_dim(q_nope, t0, Q_BLOCK, axis=3)
        qr = lax.dynamic_slice_in_dim(q_rope, t0, Q_BLOCK, axis=3)
        gt = lax.dynamic_slice_in_dim(gates, t0, Q_BLOCK, axis=3)

        s_c = jnp.einsum("bgrtd,bgnd->bgrtn", qn, k_cmp) * scale
        p_c = masked_softmax(s_c, cmp_end[None, :] <= pos[:, None])
        o_c = jnp.einsum("bgrtn,bgnd->bgrtd", p_c.astype(v_cmp.dtype), v_cmp)

        cur = pos // SEL_LEN
        imp = jnp.einsum("bgrtn,nj->bgtj", p_c, cmp_map)
        forced = (sel_ids[None, :] == 0) | (sel_ids[None, :] == cur[:, None]) | (sel_ids[None, :] == cur[:, None] - 1)
        imp = jnp.where(sel_ids[None, :] > cur[:, None], -1.0, jnp.where(forced, 1e4, imp))
        _, idx = lax.top_k(imp, k_top)
        k_g = k_sel[b_ix, g_ix, idx]
        v_g = v_sel[b_ix, g_ix, idx]
        s_s = jnp.einsum("bgrtd,bgtkld->bgrtkl", qr, k_g) * scale
        key_pos = idx[..., None] * SEL_LEN + sel_off
        mask_s = (key_pos <= pos[:, None, None]).reshape(bsz, N_KV, 1, Q_BLOCK, k_top * SEL_LEN)
        p_s = masked_softmax(s_s.reshape(bsz, N_KV, GROUP, Q_BLOCK, k_top * SEL_LEN), mask_s)
        o_s = jnp.einsum("bgrtm,bgtmd->bgrtd", p_s.astype(v_g.dtype),
                         v_g.reshape(bsz, N_KV, Q_BLOCK, k_top * SEL_LEN, HEAD_DIM))

        kwb = lax.dynamic_slice_in_dim(k_win, t0, Q_BLOCK + WINDOW, axis=2)
        vwb = lax.dynamic_slice_in_dim(v_win, t0, Q_BLOCK + WINDOW, axis=2)
        kp = t0 - WINDOW + win_off
        mask_w = (kp[None, :] >= 0) & (kp[None, :] <= pos[:, None]) & (kp[None, :] > pos[:, None] - WINDOW)
        s_w = jnp.einsum("bgrtd,bgsd->bgrts", qr, kwb) * scale
        p_w = masked_softmax(s_w, mask_w)
        o_w = jnp.einsum("bgrts,bgsd->bgrtd", p_w.astype(vwb.dtype), vwb)

        return gt[..., 0:1] * o_c + gt[..., 1:2] * o_s + gt[..., 2:3] * o_w

    o = lax.map(query_block, jnp.arange(n_qb))
    o = o.transpose(1, 0, 4, 2, 3, 5).reshape(bsz, seq, N_HEADS * HEAD_DIM)
    return o @ w_out


def conv_ffn(h, w_up, w_dw, b_dw, w_down):
    a, v = jnp.split(h @ w_up, 2, axis=-1)
    return (jax.nn.silu(causal_dwconv(a, w_dw, b_dw)) * v) @ w_down


def setup_inputs(seed: int = 0) -> dict:
    key = jax.random.key(seed)
    ks = iter(jax.random.split(key, 40))
    f32 = jnp.float32
    D = D_MODEL
    hd = HEAD_DIM
    Lc = N_CONV_LAYERS
    Ln = N_NSA_LAYERS

    def w(shape, fan_in):
        return jax.random.normal(next(ks), shape, f32) * fan_in ** -0.5

    def gain(shape):
        return 1.0 + 0.05 * jax.random.normal(next(ks), shape, f32)

    def small(shape, s=0.02):
        return s * jax.random.normal(next(ks), shape, f32)

    return {
        "x": jax.random.normal(next(ks), (BATCH, SEQ, D), f32),
        "mix_norm_g": gain((DEPTH, D)),
        "ffn_norm_g": gain((DEPTH, D)),
        "conv_w_pw1": w((Lc, D, 2 * D), D),
        "conv_b_pw1": small((Lc, 2 * D)),
        "conv_w_dw": w((Lc, CONV_WIDTH, D), CONV_WIDTH),
        "conv_b_dw": small((Lc, D)),
        "conv_ln_g": gain((Lc, D)),
        "conv_ln_b": small((Lc, D)),
        "conv_w_pw2": w((Lc, D, D), D),
        "conv_b_pw2": small((Lc, D)),
        "nsa_w_in": w((Ln, D, PROJ_COLS), D),
        "nsa_q_norm": gain((Ln, hd)),
        "nsa_kc_norm": gain((Ln, hd)),
        "nsa_ks_norm": gain((Ln, hd)),
        "nsa_kw_norm": gain((Ln, hd)),
        "nsa_pe_k": small((Ln, CMP_LEN, hd), 0.1),
        "nsa_pe_v": small((Ln, CMP_LEN, hd), 0.1),
        "nsa_ck_w1": w((Ln, CMP_LEN * hd, CMP_HIDDEN), CMP_LEN * hd),
        "nsa_ck_w2": w((Ln, CMP_HIDDEN, hd), CMP_HIDDEN),
        "nsa_cv_w1": w((Ln, CMP_LEN * hd, CMP_HIDDEN), CMP_LEN * hd),
        "nsa_cv_w2": w((Ln, CMP_HIDDEN, hd), CMP_HIDDEN),
        "nsa_w_out": w((Ln, N_HEADS * hd, D), N_HEADS * hd),
        "ffn_w_up": w((DEPTH, D, 2 * D_FF), D),
        "ffn_w_dw": w((DEPTH, FFN_CONV_WIDTH, D_FF), FFN_CONV_WIDTH),
        "ffn_b_dw": small((DEPTH, D_FF)),
        "ffn_w_down": w((DEPTH, D_FF, D), D_FF),
    }


def reference(x, mix_norm_g, ffn_norm_g,
              conv_w_pw1, conv_b_pw1, conv_w_dw, conv_b_dw, conv_ln_g, conv_ln_b, conv_w_pw2, conv_b_pw2,
              nsa_w_in, nsa_q_norm, nsa_kc_norm, nsa_ks_norm, nsa_kw_norm, nsa_pe_k, nsa_pe_v,
              nsa_ck_w1, nsa_ck_w2, nsa_cv_w1, nsa_cv_w2, nsa_w_out,
              ffn_w_up, ffn_w_dw, ffn_b_dw, ffn_w_down):
    for i in range(DEPTH):
        h = rmsnorm(x, mix_norm_g[i])
        j = i // N_MIXERS
        if i % N_MIXERS == 0:
            x = x + conformer_conv(h, conv_w_pw1[j], conv_b_pw1[j], conv_w_dw[j], conv_b_dw[j],
                                   conv_ln_g[j], conv_ln_b[j], conv_w_pw2[j], conv_b_pw2[j])
        else:
            x = x + nsa_attention(h, nsa_w_in[j], nsa_q_norm[j], nsa_kc_norm[j], nsa_ks_norm[j], nsa_kw_norm[j],
                                  nsa_pe_k[j], nsa_pe_v[j], nsa_ck_w1[j], nsa_ck_w2[j], nsa_cv_w1[j], nsa_cv_w2[j],
                                  nsa_w_out[j])
        x = x + conv_ffn(rmsnorm(x, ffn_norm_g[i]), ffn_w_up[i], ffn_w_dw[i], ffn_b_dw[i], ffn_w_down[i])
    return x
```

```python
import numpy as np
from contextlib import ExitStack
import concourse.bass as bass
import concourse.mybir as mybir
from concourse.bass_utils import run_bass_kernel_spmd

F32 = mybir.dt.float32
BF16 = mybir.dt.bfloat16
ALU = mybir.AluOpType
AF = mybir.ActivationFunctionType
AX = mybir.AxisListType

D = 1024
B = 2
S = 8192
DEPTH = 4
NCORES = 8
TOK = 2048
HALO = 52
TT = TOK + HALO
TW = 420
NT = TT // TW
DFF = 2816
NFC = DFF // 128
CW = 31
EPS = 1e-6
GSZ = 4
GROUPS = [(j, min(GSZ, NFC - j)) for j in range(0, NFC, GSZ)]

HD = 64
NKV = 4
NH = 16
PROJ = 2608


class Sched:
    def __init__(self, nc, es):
        self.nc = nc
        self.es = es
        self.E = {"pe": nc.tensor, "act": nc.scalar, "dve": nc.vector, "pool": nc.gpsimd, "sp": nc.sync}
        self.semh = {}
        self.cnt = {}
        for e in self.E:
            self.semh[e] = es.enter_context(nc.semaphore("s_" + e))
            self.cnt[e] = 0
        self.waited = {e: {} for e in self.E}
        self.res = {}
        self.pending_noinc = {e: False for e in self.E}

    def dsem(self, name):
        k = "d_" + name
        if k not in self.semh:
            self.semh[k] = self.es.enter_context(self.nc.semaphore(k))
            self.cnt[k] = 0
        return k

    def _collect(self, eng, reads, writes):
        waits = {}

        def need(dep, same_ok):
            semk, val, deng = dep
            if deng == eng and (not same_ok or eng == "pe"):
                return
            if self.waited[eng].get(semk, 0) >= val:
                return
            if waits.get(semk, 0) < val:
                waits[semk] = val

        for r in reads:
            st = self.res.get(r)
            if st is not None and st[0] is not None:
                need(st[0], True)
        for w in writes:
            st = self.res.get(w)
            if st is not None:
                if st[0] is not None:
                    need(st[0], True)
                for rd in st[1].values():
                    need(rd, False)
        return waits

    def _emit_waits(self, eng, waits):
        E = self.E[eng]
        for semk, val in waits.items():
            E.wait_ge(self.semh[semk], val)
            self.waited[eng][semk] = val

    def _record(self, me, reads, writes):
        for r in reads:
            st = self.res.setdefault(r, [None, {}])
            st[1][me[2]] = me
        for w in writes:
            st = self.res.setdefault(w, [None, {}])
            st[0] = me
            st[1] = {}

    def op(self, eng, fn, reads=(), writes=(), inc=True):
        waits = self._collect(eng, reads, writes)
        self._emit_waits(eng, waits)
        inst = fn(self.E[eng])
        if inc:
            self.cnt[eng] += 1
            inst.then_inc(self.semh[eng], 1)
            me = (eng, self.cnt[eng], eng)
            self.pending_noinc[eng] = False
        else:
            me = (eng, self.cnt[eng] + 1, eng)
            self.pending_noinc[eng] = True
        self._record(me, reads, writes)
        return inst

    def mm(self, out, pairs, reads, writes):
        waits = self._collect("pe", reads, writes)
        self._emit_waits("pe", waits)
        n = len(pairs)
        inst = None
        for i, (lhsT, rhs) in enumerate(pairs):
            inst = self.nc.tensor.matmul(out, lhsT=lhsT, rhs=rhs, start=(i == 0), stop=(i == n - 1))
        self.cnt["pe"] += 1
        inst.then_inc(self.semh["pe"], 1)
        me = ("pe", self.cnt["pe"], "pe")
        self._record(me, reads, writes)

    def mm_part(self, out, lhsT, rhs, start, stop, reads, writes, last):
        waits = self._collect("pe", reads, writes)
        self._emit_waits("pe", waits)
        inst = self.nc.tensor.matmul(out, lhsT=lhsT, rhs=rhs, start=start, stop=stop)
        if last:
            self.cnt["pe"] += 1
            inst.then_inc(self.semh["pe"], 1)
            me = ("pe", self.cnt["pe"], "pe")
        else:
            me = ("pe", self.cnt["pe"] + 1, "pe")
        self._record(me, reads, writes)

    def dma(self, q, dname, items, reads=(), writes=()):
        semk = self.dsem(dname)
        waits = self._collect(q, reads, writes)
        if self.cnt[semk] > 0 and self.waited[q].get(semk, 0) < self.cnt[semk]:
            waits[semk] = self.cnt[semk]
        self._emit_waits(q, waits)
        for (o, i) in items:
            inst = self.E[q].dma_start(out=o, in_=i)
            self.cnt[semk] += 16
            inst.then_inc(self.semh[semk], 16)
        me = (semk, self.cnt[semk], "dma:" + semk)
        self._record(me, reads, writes)

    def barrier(self):
        comp = ["pe", "act", "dve", "pool", "sp"]
        for e in comp:
            waits = {}
            for k, v in self.cnt.items():
                if k == e or v == 0:
                    continue
                if self.waited[e].get(k, 0) < v:
                    waits[k] = v
            self._emit_waits(e, waits)
        self.res = {}

    def final_wait(self, eng="sp"):
        waits = {}
        for k, v in self.cnt.items():
            if k == eng or v == 0:
                continue
            if self.waited[eng].get(k, 0) < v:
                waits[k] = v
        self._emit_waits(eng, waits)


def fm(vec, nch):
    return np.ascontiguousarray(np.asarray(vec, np.float32).reshape(nch, 128).T)


class VecPack:
    def __init__(self):
        self.off = {}
        self.n = 0
        self.arrs = []

    def add(self, name, arr):
        arr = np.asarray(arr, np.float32)
        assert arr.shape[0] == 128
        arr = arr.reshape(128, -1)
        self.off[name] = (self.n, arr.shape[1])
        self.n += arr.shape[1]
        self.arrs.append(arr)

    def array(self):
        return np.ascontiguousarray(np.concatenate(self.arrs, axis=1))


def dense_vec_layout(phases):
    off = {}
    n = 0

    def add(name, w):
        nonlocal n
        off[name] = (n, w)
        n += w

    for pi, (kind, _) in enumerate(phases):
        p = "p%d_" % pi
        if kind == "conv":
            add(p + "g", 8); add(p + "b1", 16); add(p + "wdw", 8 * CW); add(p + "bdw", 8)
            add(p + "lng", 8); add(p + "lnb", 8); add(p + "b2", 8)
        elif kind == "ffn":
            add(p + "g", 8); add(p + "wdw", NFC * 3); add(p + "bdw", NFC)
        elif kind == "oproj":
            pass
    return off, n


def build_dense(phases):
    nc = bass.Bass("TRN2", target_bir_lowering=False)
    voff, nv = dense_vec_layout(phases)
    xin = nc.dram_tensor("xT", [D, TT], F32, kind="ExternalInput").ap()
    hm_in = nc.dram_tensor("hmask", [128, HALO], F32, kind="ExternalInput").ap()
    cv_in = nc.dram_tensor("cvec", [128, max(nv, 1)], F32, kind="ExternalInput").ap()
    yout = nc.dram_tensor("yT", [D, TOK], F32, kind="ExternalOutput").ap()
    wd = {}
    for pi, (kind, _) in enumerate(phases):
        p = "p%d_" % pi
        if kind == "conv":
            wd[p + "w1"] = nc.dram_tensor(p + "w1", [D, 2 * D], F32, kind="ExternalInput").ap()
            wd[p + "w2"] = nc.dram_tensor(p + "w2", [D, D], F32, kind="ExternalInput").ap()
        elif kind == "ffn":
            wd[p + "wup"] = nc.dram_tensor(p + "wup", [D, 2 * DFF], F32, kind="ExternalInput").ap()
            wd[p + "wdn"] = nc.dram_tensor(p + "wdn", [DFF, D], F32, kind="ExternalInput").ap()
        elif kind == "oproj":
            wd[p + "wo"] = nc.dram_tensor(p + "wo", [D, D], F32, kind="ExternalInput").ap()
            wd[p + "oT"] = nc.dram_tensor(p + "oT", [D, TT], F32, kind="ExternalInput").ap()

    with ExitStack() as es:
        sc = Sched(nc, es)
        xT = es.enter_context(nc.sbuf_tensor("xTs", [128, 8, TT], F32))
        cv = es.enter_context(nc.sbuf_tensor("cv", [128, max(nv, 1)], F32))
        hmk = es.enter_context(nc.sbuf_tensor("hmk", [128, HALO], F32))
        onesm = es.enter_context(nc.sbuf_tensor("onesm", [128, 128], BF16))
        epsb = es.enter_context(nc.sbuf_tensor("epsb", [128, 1], F32))
        ps = [es.enter_context(nc.psum_tensor("ps%d" % i, [128, 512], F32)) for i in range(8)]

        def V(name, a=0, b=None):
            o, w = voff[name]
            if b is None:
                b = w
            return cv[:, o + a:o + b]

        xin_v = xin.rearrange("(kc p) t -> p kc t", p=128)
        for t in range(NT):
            sc.dma("sp", "xin%d" % t, [(xT[:, :, t * TW:(t + 1) * TW], xin_v[:, :, t * TW:(t + 1) * TW])],
                   writes=[("x", t)])
        sc.dma("sp", "cvl", [(cv[:, :], cv_in[:, :]), (hmk[:, :], hm_in[:, :])], writes=["cv"])
        sc.op("dve", lambda e: e.memset(onesm[:, :], 1.0 / 1024.0), writes=["ones"])
        sc.op("dve", lambda e: e.memset(epsb[:, :], EPS), writes=["eps"])

        def rms_stats(x_ap, reads_x, sq, rs, k, psb):
            sc.op("act", lambda e: e.activation(out=sq[:, :, :], in_=x_ap, func=AF.Square),
                  reads=reads_x, writes=[("sq", k)])
            sc.mm(ps[psb][:, :TW], [(onesm[:, :], sq[:, kc, :]) for kc in range(8)],
                  reads=[("sq", k), "ones"], writes=[("ps", psb)])
            sc.op("act", lambda e: e.activation(out=rs[:, :], in_=ps[psb][:, :TW], func=AF.Sqrt,
                                                bias=epsb[:, 0:1], scale=1.0),
                  reads=[("ps", psb), "eps"], writes=[("rs", k)])
            sc.op("dve", lambda e: e.reciprocal(out=rs[:, :], in_=rs[:, :]), reads=[("rs", k)], writes=[("rs", k)])

        for pi, (kind, _) in enumerate(phases):
            p = "p%d_" % pi
            with ExitStack() as ph:
                if kind == "conv":
                    w1 = ph.enter_context(nc.sbuf_tensor(p + "w1s", [128, 8, 2 * D], BF16))
                    w2 = ph.enter_context(nc.sbuf_tensor(p + "w2s", [128, 8, D], BF16))
                    sqb = [ph.enter_context(nc.sbuf_tensor(p + "sq%d" % i, [128, 8, TW], BF16)) for i in range(2)]
                    rsb = [ph.enter_context(nc.sbuf_tensor(p + "rs%d" % i, [128, TW], F32)) for i in range(2)]
                    hb = [ph.enter_context(nc.sbuf_tensor(p + "hb%d" % i, [128, 8, TW], BF16)) for i in range(2)]
                    uG = ph.enter_context(nc.sbuf_tensor(p + "uG", [128, 8, TW + CW - 1], F32))
                    yb = ph.enter_context(nc.sbuf_tensor(p + "yb", [128, 8, TW], F32))
                    sgb = [ph.enter_context(nc.sbuf_tensor(p + "sg%d" % i, [128, TW], F32)) for i in range(2)]
                    ybf = [ph.enter_context(nc.sbuf_tensor(p + "ybf%d" % i, [128, TW], BF16)) for i in range(2)]
                    ysq = [ph.enter_context(nc.sbuf_tensor(p + "ysq%d" % i, [128, TW], BF16)) for i in range(2)]
                    zb = ph.enter_context(nc.sbuf_tensor(p + "zb", [128, 8, TW], BF16))
                    mu = ph.enter_context(nc.sbuf_tensor(p + "mu", [128, TW], F32))
                    var = ph.enter_context(nc.sbuf_tensor(p + "var", [128, TW], F32))
                    nb = ph.enter_context(nc.sbuf_tensor(p + "nb", [128, TW], F32))
                    ntm = [ph.enter_context(nc.sbuf_tensor(p + "ntm%d" % i, [128, TW], F32)) for i in range(2)]
                    w1v = wd[p + "w1"].rearrange("(kc p) n -> p kc n", p=128)
                    w2v = wd[p + "w2"].rearrange("(kc p) n -> p kc n", p=128)
                    sc.dma("pool", "w1a", [(w1[:, 0:4, :], w1v[:, 0:4, :])], writes=["w1a"])
                    sc.dma("pool", "w1b", [(w1[:, 4:8, :], w1v[:, 4:8, :])], writes=["w1b"])
                    sc.dma("pool", "w2", [(w2[:, :, :], w2v[:, :, :])], writes=["w2"])
                    sc.op("dve", lambda e: e.memset(uG[:, :, 0:CW - 1], 0.0), writes=[("uG", c) for c in range(8)])
                    PS_ST, PS_A, PS_G, PS_MU, PS_E2 = 0, (1, 2), (3, 4), 5, 6
                    for t in range(NT):
                        cols = slice(t * TW, (t + 1) * TW)
                        k = t % 2
                        rms_stats(xT[:, :, cols], [("x", t)], sqb[k], rsb[k], k, PS_ST)
                        for kc in range(8):
                            sc.op("dve", lambda e: e.scalar_tensor_tensor(
                                out=hb[k][:, kc, :], in0=xT[:, kc, cols], scalar=V(p + "g", kc, kc + 1),
                                in1=rsb[k][:, :], op0=ALU.mult, op1=ALU.mult),
                                reads=[("x", t), ("rs", k), "cv"], writes=[("hb", k)])
                        for c2 in range(0, 8, 2):
                            for c in (c2, c2 + 1):
                                pa, pg = PS_A[c % 2], PS_G[c % 2]
                                sc.mm(ps[pa][:, :TW], [(w1[:, kc, c * 128:(c + 1) * 128], hb[k][:, kc, :]) for kc in range(8)],
                                      reads=[("hb", k), "w1a", "w1b"], writes=[("ps", pa)])
                                sc.mm(ps[pg][:, :TW], [(w1[:, kc, D + c * 128:D + (c + 1) * 128], hb[k][:, kc, :]) for kc in range(8)],
                                      reads=[("hb", k), "w1a", "w1b"], writes=[("ps", pg)])
                                sc.op("act", lambda e: e.activation(out=sgb[c % 2][:, :], in_=ps[pg][:, :TW], func=AF.Sigmoid,
                                                                    bias=V(p + "b1", 8 + c, 9 + c), scale=1.0),
                                      reads=[("ps", pg), "cv"], writes=[("sg", c % 2)])
                                sc.op("dve", lambda e: e.scalar_tensor_tensor(
                                    out=uG[:, c, CW - 1:], in0=ps[pa][:, :TW], scalar=V(p + "b1", c, c + 1),
                                    in1=sgb[c % 2][:, :], op0=ALU.add, op1=ALU.mult),
                                    reads=[("ps", pa), ("sg", c % 2), "cv"], writes=[("uG", c)])
                                if t == 0:
                                    sc.op("dve", lambda e: e.tensor_tensor(
                                        out=uG[:, c, CW - 1:CW - 1 + HALO], in0=uG[:, c, CW - 1:CW - 1 + HALO],
                                        in1=hmk[:, :], op=ALU.mult), reads=[("uG", c), "cv"], writes=[("uG", c)])
                            for kk in range(CW):
                                for c in (c2, c2 + 1):
                                    wk = V(p + "wdw", c * CW + kk, c * CW + kk + 1)
                                    if kk == 0:
                                        sc.op("dve", lambda e: e.tensor_scalar(
                                            out=yb[:, c, :], in0=uG[:, c, 0:TW], scalar1=wk, scalar2=V(p + "bdw", c, c + 1),
                                            op0=ALU.mult, op1=ALU.add), reads=[("uG", c), "cv"], writes=[("y", c)])
                                    else:
                                        sc.op("dve", lambda e: e.scalar_tensor_tensor(
                                            out=yb[:, c, :], in0=uG[:, c, kk:kk + TW], scalar=wk, in1=yb[:, c, :],
                                            op0=ALU.mult, op1=ALU.add), reads=[("uG", c), ("y", c), "cv"], writes=[("y", c)])
                            for c in (c2, c2 + 1):
                                sc.op("act", lambda e: e.activation(out=ybf[c % 2][:, :], in_=yb[:, c, :], func=AF.Copy),
                                      reads=[("y", c)], writes=[("ybf", c % 2)])
                                sc.op("act", lambda e: e.activation(out=ysq[c % 2][:, :], in_=yb[:, c, :], func=AF.Square),
                                      reads=[("y", c)], writes=[("ysq", c % 2)])
                                sc.mm_part(ps[PS_MU][:, :TW], onesm[:, :], ybf[c % 2][:, :], c == 0, c == 7,
                                           reads=[("ybf", c % 2), "ones"], writes=[("ps", PS_MU)], last=True)
                                sc.mm_part(ps[PS_E2][:, :TW], onesm[:, :], ysq[c % 2][:, :], c == 0, c == 7,
                                           reads=[("ysq", c % 2), "ones"], writes=[("ps", PS_E2)], last=True)
                        sc.op("dve", lambda e: e.tensor_copy(out=uG[:, :, 0:CW - 1], in_=uG[:, :, TW:TW + CW - 1]),
                              reads=[("uG", c) for c in range(8)], writes=[("uG", c) for c in range(8)])
                        sc.op("act", lambda e: e.activation(out=mu[:, :], in_=ps[PS_MU][:, :TW], func=AF.Copy),
                              reads=[("ps", PS_MU)], writes=["mu"])
                        sc.op("dve", lambda e: e.tensor_tensor(out=var[:, :], in0=mu[:, :], in1=mu[:, :], op=ALU.mult),
                              reads=["mu"], writes=["var"])
                        sc.op("dve", lambda e: e.tensor_tensor(out=var[:, :], in0=ps[PS_E2][:, :TW], in1=var[:, :], op=ALU.subtract),
                              reads=[("ps", PS_E2), "var"], writes=["var"])
                        sc.op("dve", lambda e: e.tensor_scalar_max(out=var[:, :], in0=var[:, :], scalar1=0.0),
                              reads=["var"], writes=["var"])
                        sc.op("act", lambda e: e.activation(out=var[:, :], in_=var[:, :], func=AF.Sqrt, bias=epsb[:, 0:1], scale=1.0),
                              reads=["var", "eps"], writes=["var"])
                        sc.op("dve", lambda e: e.reciprocal(out=var[:, :], in_=var[:, :]), reads=["var"], writes=["var"])
                        sc.op("dve", lambda e: e.scalar_tensor_tensor(out=nb[:, :], in0=mu[:, :], scalar=-1.0, in1=var[:, :],
                                                                      op0=ALU.mult, op1=ALU.mult),
                              reads=["mu", "var"], writes=["nb"])
                        for c in range(8):
                            nt_ = ntm[c % 2]
                            sc.op("dve", lambda e: e.tensor_tensor(out=nt_[:, :], in0=yb[:, c, :], in1=var[:, :], op=ALU.mult),
                                  reads=[("y", c), "var"], writes=[("ntm", c % 2)])
                            sc.op("pool", lambda e: e.tensor_tensor(out=nt_[:, :], in0=nt_[:, :], in1=nb[:, :], op=ALU.add),
                                  reads=[("ntm", c % 2), "nb"], writes=[("ntm", c % 2)])
                            sc.op("act", lambda e: e.activation(out=zb[:, c, :], in_=nt_[:, :], func=AF.Silu,
                                                                bias=V(p + "lnb", c, c + 1), scale=V(p + "lng", c, c + 1)),
                                  reads=[("ntm", c % 2), "cv"], writes=[("z", c)])
                        for m in range(8):
                            po = PS_A[m % 2]
                            sc.mm(ps[po][:, :TW], [(w2[:, c, m * 128:(m + 1) * 128], zb[:, c, :]) for c in range(8)],
                                  reads=[("z", c) for c in range(8)] + ["w2"], writes=[("ps", po)])
                            sc.op("dve", lambda e: e.scalar_tensor_tensor(
                                out=xT[:, m, cols], in0=ps[po][:, :TW], scalar=V(p + "b2", m, m + 1), in1=xT[:, m, cols],
                                op0=ALU.add, op1=ALU.add), reads=[("ps", po), ("x", t), "cv"], writes=[("x", t)])

                elif kind == "oproj":
                    wo = ph.enter_context(nc.sbuf_tensor(p + "wos", [128, 8, D], BF16))
                    oT = ph.enter_context(nc.sbuf_tensor(p + "oTs", [128, 8, TT], BF16))
                    wov = wd[p + "wo"].rearrange("(kc p) n -> p kc n", p=128)
                    oTv = wd[p + "oT"].rearrange("(kc p) t -> p kc t", p=128)
                    sc.dma("pool", "wo", [(wo[:, :, :], wov[:, :, :])], writes=["wo"])
                    for t in range(NT):
                        cols = slice(t * TW, (t + 1) * TW)
                        sc.dma("pool", "oT%d" % t, [(oT[:, :, cols], oTv[:, :, cols])], writes=[("oT", t)])
                    for t in range(NT):
                        cols = slice(t * TW, (t + 1) * TW)
                        for m in range(8):
                            po = 1 + (m % 2)
                            sc.mm(ps[po][:, :TW], [(wo[:, kc, m * 128:(m + 1) * 128], oT[:, kc, cols]) for kc in range(8)],
                                  reads=[("oT", t), "wo"], writes=[("ps", po)])
                            sc.op("dve", lambda e: e.tensor_tensor(out=xT[:, m, cols], in0=ps[po][:, :TW], in1=xT[:, m, cols],
                                                                   op=ALU.add),
                                  reads=[("ps", po), ("x", t)], writes=[("x", t)])

                elif kind == "ffn":
                    hall = ph.enter_context(nc.sbuf_tensor(p + "hall", [128, 8, TT + 2], BF16))
                    wa = [ph.enter_context(nc.sbuf_tensor(p + "wa%d" % i, [128, 8, GSZ * 128], BF16)) for i in range(2)]
                    wv = [ph.enter_context(nc.sbuf_tensor(p + "wv%d" % i, [128, 8, GSZ * 128], BF16)) for i in range(2)]
                    wdn = [ph.enter_context(nc.sbuf_tensor(p + "wdn%d" % i, [128, GSZ, D], BF16)) for i in range(2)]
                    sqb = [ph.enter_context(nc.sbuf_tensor(p + "sq%d" % i, [128, 8, TW], BF16)) for i in range(2)]
                    rsb = [ph.enter_context(nc.sbuf_tensor(p + "rs%d" % i, [128, TW], F32)) for i in range(2)]
                    acc = [ph.enter_context(nc.sbuf_tensor(p + "acc%d" % i, [128, TW], F32)) for i in range(2)]
                    sil = [ph.enter_context(nc.sbuf_tensor(p + "sil%d" % i, [128, TW], F32)) for i in range(2)]
                    gb = [ph.enter_context(nc.sbuf_tensor(p + "gb%d" % i, [128, GSZ, TW], BF16)) for i in range(2)]
                    wupv = wd[p + "wup"].rearrange("(kc p) n -> p kc n", p=128)
                    wdnv = wd[p + "wdn"].rearrange("(jc p) n -> p jc n", p=128)

                    def load_group(gi):
                        j0, G = GROUPS[gi]
                        s = gi % 2
                        sc.dma("pool", "wg%d" % s, [
                            (wa[s][:, :, 0:G * 128], wupv[:, :, j0 * 128:(j0 + G) * 128]),
                            (wv[s][:, :, 0:G * 128], wupv[:, :, DFF + j0 * 128:DFF + (j0 + G) * 128]),
                            (wdn[s][:, 0:G, :], wdnv[:, j0:j0 + G, :]),
                        ], writes=[("wg", s)])

                    load_group(0)
                    sc.op("dve", lambda e: e.memset(hall[:, :, 0:2], 0.0), writes=[("h", -1)])
                    for t in range(NT):
                        cols = slice(t * TW, (t + 1) * TW)
                        k = t % 2
                        rms_stats(xT[:, :, cols], [("x", t)], sqb[k], rsb[k], k, 0)
                        for kc in range(8):
                            sc.op("dve", lambda e: e.scalar_tensor_tensor(
                                out=hall[:, kc, 2 + t * TW:2 + (t + 1) * TW], in0=xT[:, kc, cols], scalar=V(p + "g", kc, kc + 1),
                                in1=rsb[k][:, :], op0=ALU.mult, op1=ALU.mult),
                                reads=[("x", t), ("rs", k), "cv"], writes=[("h", t)])
                        if t == 0:
                            sc.op("dve", lambda e: e.tensor_tensor(
                                out=hall[:, :, 2:2 + HALO], in0=hall[:, :, 2:2 + HALO],
                                in1=hmk[:, :].unsqueeze(1).broadcast_to([128, 8, HALO]), op=ALU.mult),
                                reads=[("h", 0), "cv"], writes=[("h", 0)])
                    PA, PV, PD = (1, 2), (3, 4), (5, 6)
                    it = 0
                    pend = None

                    def emit_down(gi, t, gk):
                        j0, G = GROUPS[gi]
                        s = gi % 2
                        cols = slice(t * TW, (t + 1) * TW)
                        for m in range(8):
                            pd = PD[m % 2]
                            sc.mm(ps[pd][:, :TW], [(wdn[s][:, jj, m * 128:(m + 1) * 128], gb[gk][:, jj, :]) for jj in range(G)],
                                  reads=[("g", gk), ("wg", s)], writes=[("ps", pd)])
                            sc.op("dve", lambda e: e.tensor_tensor(out=xT[:, m, cols], in0=ps[pd][:, :TW], in1=xT[:, m, cols],
                                                                   op=ALU.add),
                                  reads=[("ps", pd), ("x", t)], writes=[("x", t)])

                    for gi, (j0, G) in enumerate(GROUPS):
                        s = gi % 2
                        for t in range(NT):
                            gk = it % 2
                            hreads = [("h", t), ("h", t - 1)]
                            for jj in range(G):
                                j = j0 + jj
                                pa, pv = PA[jj % 2], PV[jj % 2]
                                sc.mm(ps[pa][:, :TW + 2], [(wa[s][:, kc, jj * 128:(jj + 1) * 128], hall[:, kc, t * TW:t * TW + TW + 2])
                                                            for kc in range(8)],
                                      reads=hreads + [("wg", s)], writes=[("ps", pa)])
                                sc.mm(ps[pv][:, :TW], [(wv[s][:, kc, jj * 128:(jj + 1) * 128], hall[:, kc, 2 + t * TW:2 + (t + 1) * TW])
                                                       for kc in range(8)],
                                      reads=hreads + [("wg", s)], writes=[("ps", pv)])
                                a_ = acc[jj % 2]
                                sc.op("dve", lambda e: e.tensor_scalar(
                                    out=a_[:, :], in0=ps[pa][:, 2:TW + 2], scalar1=V(p + "wdw", j * 3 + 2, j * 3 + 3),
                                    scalar2=V(p + "bdw", j, j + 1), op0=ALU.mult, op1=ALU.add),
                                    reads=[("ps", pa), "cv"], writes=[("acc", jj % 2)])
                                sc.op("dve", lambda e: e.scalar_tensor_tensor(
                                    out=a_[:, :], in0=ps[pa][:, 1:TW + 1], scalar=V(p + "wdw", j * 3 + 1, j * 3 + 2), in1=a_[:, :],
                                    op0=ALU.mult, op1=ALU.add), reads=[("ps", pa), ("acc", jj % 2), "cv"], writes=[("acc", jj % 2)])
                                sc.op("dve", lambda e: e.scalar_tensor_tensor(
                                    out=a_[:, :], in0=ps[pa][:, 0:TW], scalar=V(p + "wdw", j * 3, j * 3 + 1), in1=a_[:, :],
                                    op0=ALU.mult, op1=ALU.add), reads=[("ps", pa), ("acc", jj % 2), "cv"], writes=[("acc", jj % 2)])
                                sc.op("act", lambda e: e.activation(out=sil[jj % 2][:, :], in_=a_[:, :], func=AF.Silu),
                                      reads=[("acc", jj % 2)], writes=[("sil", jj % 2)])
                                sc.op("dve", lambda e: e.tensor_tensor(out=gb[gk][:, jj, :], in0=ps[pv][:, :TW], in1=sil[jj % 2][:, :],
                                                                       op=ALU.mult),
                                      reads=[("ps", pv), ("sil", jj % 2)], writes=[("g", gk)])
                            if pend is not None:
                                emit_down(*pend)
                            pend = (gi, t, gk)
                            it += 1
                            if t == 0 and gi + 1 < len(GROUPS):
                                if pend is not None and pend[0] != gi:
                                    pass
                                load_group(gi + 1)
                    emit_down(*pend)
                sc.barrier()

        yv = yout.rearrange("(kc p) t -> p kc t", p=128)
        for t in range(NT):
            lo = max(t * TW, HALO)
            hi = (t + 1) * TW
            sc.dma("sp", "yout%d" % t, [(yv[:, :, lo - HALO:hi - HALO], xT[:, :, lo:hi])], reads=[("x", t)])
        sc.final_wait("sp")
    return nc


def shard_xT(xfull):
    outs = []
    for c in range(NCORES):
        b, q = divmod(c, 4)
        t0 = q * TOK
        buf = np.zeros((TT, xfull.shape[2]), np.float32)
        lo = t0 - HALO
        if lo < 0:
            buf[-lo:] = xfull[b, 0:t0 + TOK]
        else:
            buf[:] = xfull[b, lo:t0 + TOK]
        outs.append(np.ascontiguousarray(buf.T))
    return outs


def unshard_yT(ys):
    out = np.zeros((B, S, D), np.float32)
    for c in range(NCORES):
        b, q = divmod(c, 4)
        out[b, q * TOK:(q + 1) * TOK] = ys[c].T
    return out


def dense_host_inputs(phases, P, xfull, ofull=None):
    vp = VecPack()
    wmaps = {}
    for pi, (kind, idx) in enumerate(phases):
        p = "p%d_" % pi
        if kind == "conv":
            li, j = idx
            vp.add(p + "g", fm(P["mix_norm_g"][li], 8))
            vp.add(p + "b1", fm(P["conv_b_pw1"][j], 16))
            wdw = np.asarray(P["conv_w_dw"][j], np.float32)
            vp.add(p + "wdw", np.ascontiguousarray(wdw.reshape(CW, 8, 128).transpose(2, 1, 0)).reshape(128, 8 * CW))
            vp.add(p + "bdw", fm(P["conv_b_dw"][j], 8))
            vp.add(p + "lng", fm(P["conv_ln_g"][j], 8))
            vp.add(p + "lnb", fm(P["conv_ln_b"][j], 8))
            vp.add(p + "b2", fm(P["conv_b_pw2"][j], 8))
            wmaps[p + "w1"] = np.ascontiguousarray(P["conv_w_pw1"][j], np.float32)
            wmaps[p + "w2"] = np.ascontiguousarray(P["conv_w_pw2"][j], np.float32)
        elif kind == "ffn":
            li = idx
            vp.add(p + "g", fm(P["ffn_norm_g"][li], 8))
            wdw = np.asarray(P["ffn_w_dw"][li], np.float32)
            vp.add(p + "wdw", np.ascontiguousarray(wdw.reshape(3, NFC, 128).transpose(2, 1, 0)).reshape(128, NFC * 3))
            vp.add(p + "bdw", fm(P["ffn_b_dw"][li], NFC))
            wmaps[p + "wup"] = np.ascontiguousarray(P["ffn_w_up"][li], np.float32)
            wmaps[p + "wdn"] = np.ascontiguousarray(P["ffn_w_down"][li], np.float32)
        elif kind == "oproj":
            j = idx
            wmaps[p + "wo"] = np.ascontiguousarray(P["nsa_w_out"][j], np.float32)
    voff, nv = dense_vec_layout(phases)
    assert voff == vp.off, (voff, vp.off)
    cvec = vp.array() if vp.n > 0 else np.zeros((128, 1), np.float32)
    xs = shard_xT(xfull)
    oTs = shard_xT(ofull) if ofull is not None else None
    in_maps = []
    for c in range(NCORES):
        q = c % 4
        m = {"xT": xs[c], "cvec": cvec,
             "hmask": np.full((128, HALO), 0.0 if q == 0 else 1.0, np.float32)}
        m.update(wmaps)
        for pi, (kind, idx) in enumerate(phases):
            if kind == "oproj":
                m["p%d_oT" % pi] = oTs[c]
        in_maps.append(m)
    return in_maps


_NC_CACHE = {}


def run_dense(phases, P, xfull, ofull=None):
    key = ("dense", tuple(k for k, _ in phases))
    if key not in _NC_CACHE:
        _NC_CACHE[key] = build_dense(phases)
    nc = _NC_CACHE[key]
    in_maps = dense_host_inputs(phases, P, xfull, ofull)
    res = run_bass_kernel_spmd(nc, in_maps, core_ids=list(range(NCORES)))
    return unshard_yT([r["yT"] for r in res.results])


NTK = 512
NTT = S // NTK
NQT = S // 128
NCMP = 511
ROT = 16


def nsa_consts():
    c = {}
    half = ROT // 2
    inv_freq = (np.float32(500000.0) ** (-np.arange(half, dtype=np.float32) * np.float32(2.0 / ROT))).astype(np.float32)
    ang = (np.arange(S, dtype=np.float32)[None, :] * inv_freq[:, None]).astype(np.float32)
    cos = np.cos(ang).astype(np.float32)
    sin = np.sin(ang).astype(np.float32)
    c["ropeC"] = np.ascontiguousarray(np.concatenate([cos, cos], 0))
    c["ropeS"] = np.ascontiguousarray(np.concatenate([sin, sin], 0))
    prot = np.zeros((128, 128), np.float32)
    for i in range(half):
        prot[64 + i + half, 64 + i] = -1.0
        prot[64 + i, 64 + i + half] = 1.0
    c["prot"] = prot
    c["ident"] = np.eye(128, dtype=np.float32)
    p = np.arange(128)[:, None]
    t = np.arange(128)[None, :]
    c["tri"] = (p <= t).astype(np.float32)
    c["tric"] = (p > t).astype(np.float32)
    cm = np.zeros((17, 128, 128), np.float32)
    for mi in range(17):
        m = 8 * mi
        cm[mi] = (16 * (p - m) + 31 <= t).astype(np.float32)
    c["cmask"] = np.ascontiguousarray(cm.transpose(1, 0, 2))
    n = np.arange(512)[:, None] * 16
    s_ = np.arange(128)[None, :] * 64
    ov = np.minimum(n + 32, s_ + 64) - np.maximum(n, s_)
    cmap = np.maximum(ov, 0).astype(np.float32) / 32.0
    cmap[511] = 0.0
    c["cmap"] = np.ascontiguousarray(cmap.reshape(4, 128, 128).transpose(1, 0, 2))
    ext = np.zeros((128, S), np.float32)
    ext[np.arange(S) // 64, np.arange(S)] = 1.0
    c["ext"] = ext
    d = np.arange(-127, 129)[None, :]
    tt = np.arange(128)[:, None]
    lo = (tt < 64)
    t1 = np.where(lo, (d <= -2), (d <= -1)).astype(np.float32)
    t2 = np.where(lo,
                  np.where((d == -1) | (d == 0), 1e4, np.where(d > 0, -1.0, 0.0)),
                  np.where((d == 0) | (d == 1), 1e4, np.where(d > 1, -1.0, 0.0))).astype(np.float32)
    c["t1"] = np.ascontiguousarray(t1 * np.ones((128, 1), np.float32))
    c["t2"] = np.ascontiguousarray(t2)
    return c


def build_nsa():
    nc = bass.Bass("TRN2", target_bir_lowering=False)

    def din(name, shape):
        return nc.dram_tensor(name, shape, F32, kind="ExternalInput").ap()

    xin = din("xT", [D, S])
    wfm_in = din("wfm", [D, 768])
    wtm_in = din("wtm", [D, 140])
    cv_in = din("cvec", [128, 8 + 6 + 2])
    pek_in = din("pekT", [64, 32]); pev_in = din("pevT", [64, 32])
    ckw1_in = din("ck_w1", [2048, 256]); ckw2_in = din("ck_w2", [256, 64])
    cvw1_in = din("cv_w1", [2048, 256]); cvw2_in = din("cv_w2", [256, 64])
    ropeC_in = din("ropeC", [16, S]); ropeS_in = din("ropeS", [16, S])
    prot_in = din("prot", [128, 128]); ident_in = din("ident", [128, 128])
    tri_in = din("tri", [128, 128]); tric_in = din("tric", [128, 128])
    cmask_in = din("cmask", [128, 17, 128]); cmap_in = din("cmap", [128, 4, 128])
    ext_in = din("ext", [128, S]); t1_in = din("t1", [128, 256]); t2_in = din("t2", [128, 256])
    oout = nc.dram_tensor("o", [S, 256], F32, kind="ExternalOutput").ap()

    with ExitStack() as es:
        sc = Sched(nc, es)
        SB = lambda name, shape, dt: es.enter_context(nc.sbuf_tensor("sb_" + name, shape, dt))
        Q = SB("Q", [128, 4, S], BF16)
        KA = SB("KA", [128, S], BF16)
        KB = SB("KB", [128, S], BF16)
        V1 = SB("V1", [128, NQT, 2, 65], BF16)
        G = SB("G", [128, NQT, 12], F32)
        KCMP = SB("KCMP", [128, 512], BF16)
        VC1 = SB("VC1", [128, 4, 65], BF16)
        cv = SB("cv", [128, 16], F32)
        onesm = SB("onesm", [128, 128], BF16)
        BD = SB("BD", [128, 128], BF16)
        epsb = SB("epsb", [128, 1], F32)
        prot = SB("prot", [128, 128], F32)
        ps = [es.enter_context(nc.psum_tensor("ps%d" % i, [128, 512], F32)) for i in range(7)]
        psb = es.enter_context(nc.psum_tensor("psb", [128, 1024], BF16))

        sc.dma("sp", "c0", [(cv[:, :], cv_in[:, :]), (prot[:, :], prot_in[:, :])], writes=["cv", "prot"])
        sc.op("dve", lambda e: e.memset(onesm[:, :], 1.0 / 1024.0), writes=["ones"])
        sc.op("dve", lambda e: e.memset(epsb[:, :], EPS), writes=["eps"])
        sc.op("dve", lambda e: e.memset(BD[:, :], 0.0), writes=["BD"])
        sc.op("dve", lambda e: e.memset(BD[0:64, 0:64], 1.0 / 64.0), reads=["BD"], writes=["BD"])
        sc.op("dve", lambda e: e.memset(BD[64:128, 64:128], 1.0 / 64.0), reads=["BD"], writes=["BD"])
        sc.op("dve", lambda e: e.memset(V1[:, :, :, 64:65], 1.0), writes=["V1ones"])
        sc.op("dve", lambda e: e.memset(VC1[:, :, 64:65], 1.0), writes=["VC1ones"])

        with ExitStack() as ph:
            PB = lambda name, shape, dt: ph.enter_context(nc.sbuf_tensor("sb_" + name, shape, dt))
            wfm = PB("wfm", [128, 8, 768], BF16)
            wtm = PB("wtm", [128, 8, 140], BF16)
            xt_ = PB("xt_", [128, 8, NTK], F32)
            xt = [xt_, xt_]
            sq1_ = PB("sq_", [128, 8, NTK], BF16)
            sq = [sq1_, sq1_]
            rs = [PB("rs%d" % i, [128, NTK], F32) for i in range(2)]
            hb = [PB("hb%d" % i, [128, 8, NTK], BF16) for i in range(2)]
            csC = [PB("csC%d" % i, [128, NTK], F32) for i in range(2)]
            csS = [PB("csS%d" % i, [128, NTK], F32) for i in range(2)]
            sq2 = [PB("sq2%d" % i, [128, NTK], BF16) for i in range(2)]
            rt = [PB("rt%d" % i, [128, NTK], F32) for i in range(2)]
            qn = [PB("qn%d" % i, [128, NTK], F32) for i in range(2)]
            r1_ = PB("r1_", [128, NTK], F32)
            r2_ = PB("r2_", [128, NTK], F32)
            r1 = [r1_, r1_]
            r2 = [r2_, r2_]
            xv = xin.rearrange("(kc p) t -> p kc t", p=128)
            sc.dma("pool", "wfm", [(wfm[:, :, :], wfm_in.rearrange("(kc p) n -> p kc n", p=128)),
                                   (wtm[:, :, :], wtm_in.rearrange("(kc p) n -> p kc n", p=128))], writes=["wfm"])

            def load_x(tt):
                k = tt % 2
                tok = slice(tt * NTK, (tt + 1) * NTK)
                sc.dma("sp", "xt0", [(xt[k][:, :, :], xv[:, :, tok])], writes=[("xt", 0)])
                sc.dma("sp", "cs%d" % k, [(csC[k][64:80, :], ropeC_in[:, tok]), (csS[k][64:80, :], ropeS_in[:, tok])],
                       writes=[("cs", k)])

            load_x(0)
            cbi = 0
            for tt in range(NTT):
                k = tt % 2
                tok = slice(tt * NTK, (tt + 1) * NTK)
                sc.op("act", lambda e: e.activation(out=sq[k][:, :, :], in_=xt[k][:, :, :], func=AF.Square),
                      reads=[("xt", 0)], writes=[("sq", 0)])
                sc.mm(ps[0][:, :], [(onesm[:, :], sq[k][:, kc, :]) for kc in range(8)],
                      reads=[("sq", 0), "ones"], writes=[("ps", 0)])
                sc.op("act", lambda e: e.activation(out=rs[k][:, :], in_=ps[0][:, :], func=AF.Sqrt, bias=epsb[:, 0:1], scale=1.0),
                      reads=[("ps", 0), "eps"], writes=[("rs", k)])
                sc.op("dve", lambda e: e.reciprocal(out=rs[k][:, :], in_=rs[k][:, :]), reads=[("rs", k)], writes=[("rs", k)])
                for kc in range(8):
                    sc.op("dve", lambda e: e.scalar_tensor_tensor(
                        out=hb[k][:, kc, :], in0=xt[k][:, kc, :], scalar=cv[:, kc:kc + 1], in1=rs[k][:, :],
                        op0=ALU.mult, op1=ALU.mult), reads=[("xt", 0), ("rs", k), "cv"], writes=[("hb", k)])
                if tt + 1 < NTT:
                    load_x(tt + 1)
                for sub in range(4):
                    kt = tt * 4 + sub
                    pt = 1 + sub % 2
                    sc.mm(ps[pt][:, 0:140], [(hb[k][:, kc, sub * 128:(sub + 1) * 128], wtm[:, kc, :]) for kc in range(8)],
                          reads=[("hb", k), "wfm"], writes=[("ps", pt)])
                    sc.op("act", lambda e: e.activation(out=V1[:, kt, :, 0:64],
                                                        in_=ps[pt][:, 0:128].rearrange("p (a b) -> p a b", a=2), func=AF.Copy),
                          reads=[("ps", pt)], writes=[("V1", kt)])
                    sc.op("act", lambda e: e.activation(out=G[:, kt, :], in_=ps[pt][:, 128:140], func=AF.Sigmoid),
                          reads=[("ps", pt)], writes=[("G", kt)])
                for cb in range(6):
                    j = cbi % 2
                    cbi += 1
                    P1, P2 = 3 + j, 5 + j
                    sc.mm(ps[P1][:, :], [(wfm[:, kc, cb * 128:(cb + 1) * 128], hb[k][:, kc, :]) for kc in range(8)],
                          reads=[("hb", k), "wfm"], writes=[("ps", P1)])
                    sc.op("act", lambda e: e.activation(out=sq2[j][:, :], in_=ps[P1][:, :], func=AF.Square),
                          reads=[("ps", P1)], writes=[("sq2", j)])
                    sc.mm(ps[P2][:, :], [(BD[:, :], sq2[j][:, :])], reads=[("sq2", j), "BD"], writes=[("ps", P2)])
                    sc.op("act", lambda e: e.activation(out=rt[j][:, :], in_=ps[P2][:, :], func=AF.Sqrt, bias=epsb[:, 0:1], scale=1.0),
                          reads=[("ps", P2), "eps"], writes=[("rt", j)])
                    sc.op("dve", lambda e: e.reciprocal(out=rt[j][:, :], in_=rt[j][:, :]), reads=[("rt", j)], writes=[("rt", j)])
                    gcol = cv[:, 8 + cb:9 + cb]
                    if cb < 4:
                        sc.op("dve", lambda e: e.scalar_tensor_tensor(
                            out=qn[j][:, :], in0=ps[P1][:, :], scalar=gcol, in1=rt[j][:, :], op0=ALU.mult, op1=ALU.mult),
                            reads=[("ps", P1), ("rt", j), "cv"], writes=[("qn", j)])
                    else:
                        dst = KA if cb == 4 else KB
                        sc.op("act", lambda e: e.activation(out=dst[0:64, tok], in_=ps[P1][0:64, :], func=AF.Copy),
                              reads=[("ps", P1)], writes=[("KAB", cb, tt, 0)])
                        sc.op("dve", lambda e: e.memset(qn[j][0:64, :], 0.0), writes=[("qn", j)], reads=[("qn", j)])
                        sc.op("dve", lambda e: e.scalar_tensor_tensor(
                            out=qn[j][64:128, :], in0=ps[P1][64:128, :], scalar=gcol[64:128, :], in1=rt[j][64:128, :],
                            op0=ALU.mult, op1=ALU.mult),
                            reads=[("ps", P1), ("rt", j), "cv", ("qn", j)], writes=[("qn", j)])
                    PR = 0
                    sc.mm(ps[PR][:, :], [(prot[:, :], qn[j][:, :])], reads=[("qn", j), "prot"], writes=[("ps", PR)])
                    sc.op("dve", lambda e: e.tensor_tensor(out=r1[j][64:80, :], in0=qn[j][64:80, :], in1=csC[k][64:80, :], op=ALU.mult),
                          reads=[("qn", j), ("cs", k)], writes=[("r1", 0)])
                    sc.op("dve", lambda e: e.tensor_tensor(out=r2[j][64:80, :], in0=ps[PR][64:80, :], in1=csS[k][64:80, :], op=ALU.mult),
                          reads=[("ps", PR), ("cs", k)], writes=[("r2", 0)])
                    sc.op("dve", lambda e: e.tensor_tensor(out=qn[j][64:80, :], in0=r1[j][64:80, :], in1=r2[j][64:80, :], op=ALU.add),
                          reads=[("r1", 0), ("r2", 0), ("qn", j), ("ps", PR)], writes=[("qn", j)])
                    if cb < 4:
                        sc.op("act", lambda e: e.activation(out=Q[0:64, cb, tok], in_=qn[j][0:64, :], func=AF.Copy),
                              reads=[("qn", j)], writes=[("Q", cb, tt, 0)])
                        sc.op("pool", lambda e: e.tensor_copy(out=Q[64:128, cb, tok], in_=qn[j][64:128, :]),
                              reads=[("qn", j)], writes=[("Q", cb, tt, 1)])
                    else:
                        dst = KA if cb == 4 else KB
                        sc.op("pool", lambda e: e.tensor_copy(out=dst[64:128, tok], in_=qn[j][64:128, :]),
                              reads=[("qn", j)], writes=[("KAB", cb, tt, 1)])
            sc.barrier()

        with ExitStack() as ph:
            PB = lambda name, shape, dt: ph.enter_context(nc.sbuf_tensor("sb_" + name, shape, dt))
            w1 = [PB("cw1%d" % i, [64, 32, 256], BF16) for i in range(2)]
            w2 = [PB("cw2%d" % i, [128, 2, 64], BF16) for i in range(2)]
            peT = [PB("peT%d" % i, [64, 32], BF16) for i in range(2)]
            hid = [PB("hid%d" % i, [128, 2, 512], BF16) for i in range(2)]
            bia = [PB("bia%d" % i, [128, 2], F32) for i in range(2)]
            ksq = PB("ksq", [64, 512], BF16)
            krt = PB("krt", [64, 512], F32)
            for i, (a1, a2, ap_) in enumerate([(ckw1_in, ckw2_in, pek_in), (cvw1_in, cvw2_in, pev_in)]):
                sc.dma("pool", "cw%d" % i, [
                    (w1[i][:, :, :], a1.rearrange("(l d) h -> d l h", d=64)),
                    (w2[i][:, :, :], a2.rearrange("(hc p) d -> p hc d", p=128)),
                    (peT[i][:, :], ap_[:, :])], writes=[("cw", i)])
                sc.op("dve", lambda e: e.memset(hid[i][:, :, 511:512], 0.0), writes=[("hid", i)])
            for i in range(2):
                src = KA if i == 0 else KB
                for hc in range(2):
                    sc.mm(ps[0][:, hc:hc + 1], [(w1[i][:, l, hc * 128:(hc + 1) * 128], peT[i][:, l:l + 1]) for l in range(32)],
                          reads=[("cw", i)], writes=[("ps", 0, hc)])
                    sc.op("act", lambda e: e.activation(out=bia[i][:, hc:hc + 1], in_=ps[0][:, hc:hc + 1], func=AF.Copy),
                          reads=[("ps", 0, hc)], writes=[("bia", i, hc)])
                    pb_ = 1 + hc
                    sc.mm(ps[pb_][:, 0:NCMP],
                          [(w1[i][:, l, hc * 128:(hc + 1) * 128], src[0:64, l:l + 16 * (NCMP - 1) + 1:16]) for l in range(32)],
                          reads=[("cw", i)], writes=[("ps", pb_)])
                    sc.op("act", lambda e: e.activation(out=hid[i][:, hc, 0:NCMP], in_=ps[pb_][:, 0:NCMP], func=AF.Silu,
                                                        bias=bia[i][:, hc:hc + 1], scale=1.0),
                          reads=[("ps", pb_), ("bia", i, hc), ("hid", i)], writes=[("hid", i)])
            sc.mm(ps[3][0:64, :], [(w2[0][:, hc, :], hid[0][:, hc, :]) for hc in range(2)],
                  reads=[("hid", 0), ("cw", 0)], writes=[("ps", 3)])
            sc.op("act", lambda e: e.activation(out=ksq[:, :], in_=ps[3][0:64, :], func=AF.Square), reads=[("ps", 3)], writes=["ksq"])
            sc.mm(ps[4][0:64, :], [(BD[0:64, 0:64], ksq[:, :])], reads=["ksq", "BD"], writes=[("ps", 4)])
            sc.op("act", lambda e: e.activation(out=krt[:, :], in_=ps[4][0:64, :], func=AF.Sqrt, bias=epsb[0:64, 0:1], scale=1.0),
                  reads=[("ps", 4), "eps"], writes=["krt"])
            sc.op("dve", lambda e: e.reciprocal(out=krt[:, :], in_=krt[:, :]), reads=["krt"], writes=["krt"])
            sc.op("dve", lambda e: e.scalar_tensor_tensor(out=KCMP[0:64, :], in0=ps[3][0:64, :], scalar=cv[0:64, 14:15], in1=krt[:, :],
                                                          op0=ALU.mult, op1=ALU.mult),
                  reads=[("ps", 3), "krt", "cv"], writes=["KCMP"])
            for cc in range(4):
                pb_ = 5 + cc % 2
                sc.mm(ps[pb_][:, 0:64], [(hid[1][:, hc, cc * 128:(cc + 1) * 128], w2[1][:, hc, :]) for hc in range(2)],
                      reads=[("hid", 1), ("cw", 1)], writes=[("ps", pb_)])
                sc.op("act", lambda e: e.activation(out=VC1[:, cc, 0:64], in_=ps[pb_][:, 0:64], func=AF.Copy),
                      reads=[("ps", pb_)], writes=[("VC1", cc)])
            sc.barrier()

        with ExitStack() as ph:
            PB = lambda name, shape, dt: ph.enter_context(nc.sbuf_tensor("sb_" + name, shape, dt))
            ident = PB("ident", [128, 128], BF16)
            tri = PB("tri", [128, 128], BF16)
            tric = PB("tric", [128, 128], BF16)
            cmask = PB("cmask", [128, 17, 128], BF16)
            cmap = PB("cmap", [128, 4, 128], BF16)
            ext = PB("ext", [128, S], BF16)
            t1 = PB("t1", [128, 256], F32)
            t2 = PB("t2", [128, 256], F32)
            NE = 6
            Eb = [PB("E%d" % i, [128, 512], BF16) for i in range(NE)]
            impb = PB("impb", [128, 128], F32)
            imp2 = PB("imp2", [128, 128], F32)
            imp3 = PB("imp3", [128, 128], F32)
            m8 = PB("m8", [128, 16], F32)
            selb = PB("selb", [128, 128], BF16)
            selT = PB("selT", [128, 128], BF16)
            mkd = PB("mkd", [128, 128], BF16)
            rz = PB("rz", [128, 3, 4], F32)
            cf = PB("cf", [128, 3, 4], F32)
            osb = [PB("osb%d" % i, [128, 4, 64], F32) for i in range(2)]
            otm = PB("otm", [128, 4, 64], F32)
            sc.dma("pool", "c3", [(ident[:, :], ident_in[:, :]), (tri[:, :], tri_in[:, :]), (tric[:, :], tric_in[:, :]),
                                  (cmask[:, :, :], cmask_in[:, :, :]), (cmap[:, :, :], cmap_in[:, :, :])], writes=["c3"])
            sc.dma("pool", "c3b", [(ext[:, 0:4096], ext_in[:, 0:4096]), (ext[:, 4096:S], ext_in[:, 4096:S])], writes=["ext"])
            sc.dma("sp", "c3c", [(t1[:, :], t1_in[:, :]), (t2[:, :], t2_in[:, :])], writes=["t12"])
            PS_S = (0, 1)
            PS_OC, PS_OS, PS_OW, PS_IM, PS_MK = 2, 3, 4, 5, 6
            ecnt = [0]
            scnt = [0]

            def unit(lhsT, q_rhs, kreads, mask_fn, vrhs, vreads, po, first, last_):
                pS = PS_S[scnt[0] % 2]; scnt[0] += 1
                ei = ecnt[0] % NE; ecnt[0] += 1
                E = Eb[ei]
                sc.mm(ps[pS][:, :], [(lhsT, q_rhs)], reads=kreads, writes=[("ps", pS)])
                sc.op("act", lambda e: e.activation(out=E[:, :], in_=ps[pS][:, :], func=AF.Exp, scale=0.125),
                      reads=[("ps", pS)], writes=[("E", ei)])
                if mask_fn is not None:
                    mask_fn(E, ei)
                for r in range(4):
                    waits = sc._collect("pe", [("E", ei)] + vreads, [("ps", po)])
                    sc._emit_waits("pe", waits)
                    inst = nc.tensor.matmul(ps[po][:, r * 65:(r + 1) * 65], lhsT=E[:, r * 128:(r + 1) * 128], rhs=vrhs,
                                            start=(first and r == 0), stop=(last_ and r == 3), skip_group_check=True)
                sc.cnt["pe"] += 1
                inst.then_inc(sc.semh["pe"], 1)
                sc._record(("pe", sc.cnt["pe"], "pe"), [("E", ei)] + vreads, [("ps", po)])
                return E, ei

            def bmask(mask_ap_fn, mreads):
                def f(E, ei):
                    sc.op("dve", lambda e: e.tensor_tensor(
                        out=E[:, :].rearrange("p (r t) -> p r t", r=4), in0=E[:, :].rearrange("p (r t) -> p r t", r=4),
                        in1=mask_ap_fn(), op=ALU.mult), reads=[("E", ei)] + mreads, writes=[("E", ei)])
                return f

            for qt in range(NQT):
                qtok = slice(qt * 128, (qt + 1) * 128)
                qn_rhs = Q[0:64, :, qtok]
                qr_rhs = Q[64:128, :, qtok]
                qreads = [("Q", r, qt // 4, h) for r in range(4) for h in range(2)]
                ncn = min(4, (8 * qt + 7 + 127) // 128)
                Es = []
                for cc in range(ncn):
                    m = 8 * qt - 128 * cc
                    mf = None
                    if 0 <= m <= 128:
                        mi = m // 8
                        mf = bmask(lambda mi=mi: cmask[:, mi, :].unsqueeze(1).broadcast_to([128, 4, 128]), ["c3"])
                    E, ei = unit(KCMP[0:64, cc * 128:(cc + 1) * 128], qn_rhs, ["KCMP"] + qreads, mf,
                                 VC1[:, cc, :], [("VC1", c_) for c_ in range(4)] + ["VC1ones"], PS_OC, cc == 0, cc == ncn - 1)
                    Es.append((E, ei, cc))
                nmm = len(Es) * 4
                i_ = 0
                for (E, ei, cc) in Es:
                    for r in range(4):
                        waits = sc._collect("pe", [("E", ei), "c3"], [("ps", PS_IM)])
                        sc._emit_waits("pe", waits)
                        inst = nc.tensor.matmul(ps[PS_IM][:, r * 128:(r + 1) * 128], lhsT=E[:, r * 128:(r + 1) * 128],
                                                rhs=cmap[:, cc, :], start=(i_ == 0), stop=(i_ == nmm - 1), skip_group_check=True)
                        i_ += 1
                        sc._record(("pe", sc.cnt["pe"] + 1, "pe"), [("E", ei), "c3"], [("ps", PS_IM)])
                sc.cnt["pe"] += 1
                inst.then_inc(sc.semh["pe"], 1)
                sc.op("dve", lambda e: e.tensor_scalar_max(
                    out=rz[:, 0, :], in0=ps[PS_OC][:, 0:260].rearrange("p (r c) -> p r c", r=4)[:, :, 64], scalar1=1e-30),
                    reads=[("ps", PS_OC)], writes=[("rz", 0)])
                sc.op("dve", lambda e: e.reciprocal(out=rz[:, 0, :], in_=rz[:, 0, :]), reads=[("rz", 0)], writes=[("rz", 0)])
                for r in range(4):
                    if r == 0:
                        sc.op("dve", lambda e: e.tensor_scalar(out=impb[:, :], in0=ps[PS_IM][:, 0:128], scalar1=rz[:, 0, 0:1],
                                                               scalar2=None, op0=ALU.mult),
                              reads=[("ps", PS_IM), ("rz", 0)], writes=["impb"])
                    else:
                        sc.op("dve", lambda e: e.scalar_tensor_tensor(
                            out=impb[:, :], in0=ps[PS_IM][:, r * 128:(r + 1) * 128], scalar=rz[:, 0, r:r + 1], in1=impb[:, :],
                            op0=ALU.mult, op1=ALU.add), reads=[("ps", PS_IM), ("rz", 0), "impb"], writes=["impb"])
                off = 127 - 2 * qt
                sc.op("dve", lambda e: e.tensor_tensor(out=imp2[:, :], in0=impb[:, :], in1=t1[:, off:off + 128], op=ALU.mult),
                      reads=["impb", "t12"], writes=["imp2"])
                sc.op("dve", lambda e: e.tensor_tensor(out=imp2[:, :], in0=imp2[:, :], in1=t2[:, off:off + 128], op=ALU.add),
                      reads=["imp2", "t12"], writes=["imp2"])
                sc.op("dve", lambda e: e.memset(imp2[:, 0:1], 1e4), reads=["imp2"], writes=["imp2"])
                sc.op("dve", lambda e: e.max(out=m8[:, 0:8], in_=imp2[:, :]), reads=["imp2"], writes=["m8a"])
                sc.op("dve", lambda e: e.match_replace(out=imp3[:, :], in_to_replace=m8[:, 0:8], in_values=imp2[:, :], imm_value=-1e30),
                      reads=["imp2", "m8a"], writes=["imp3"])
                sc.op("dve", lambda e: e.max(out=m8[:, 8:16], in_=imp3[:, :]), reads=["imp3"], writes=["m8b"])
                sc.op("dve", lambda e: e.tensor_scalar(out=selb[:, :], in0=imp2[:, :], scalar1=m8[:, 15:16], scalar2=None, op0=ALU.is_ge),
                      reads=["imp2", "m8b"], writes=["selb"])
                waits = sc._collect("pe", ["selb", "c3"], ["psb"])
                sc._emit_waits("pe", waits)
                inst = nc.tensor.transpose(psb[:, 0:128], selb[:, :], ident[:, :])
                sc.cnt["pe"] += 1
                inst.then_inc(sc.semh["pe"], 1)
                sc._record(("pe", sc.cnt["pe"], "pe"), ["selb", "c3"], ["psb"])
                sc.op("act", lambda e: e.activation(out=selT[:, :], in_=psb[:, 0:128], func=AF.Copy), reads=["psb"], writes=["selT"])
                for kt in range(qt + 1):
                    kb = kt % 4
                    if kb == 0:
                        nk = min(4, qt + 1 - kt)
                        for u in range(nk):
                            sc.mm(ps[PS_MK][:, u * 128:(u + 1) * 128],
                                  [(ext[:, (kt + u) * 128:(kt + u + 1) * 128], selT[:, :])],
                                  reads=["ext", "selT"], writes=[("mk", u)])
                    ktok = slice(kt * 128, (kt + 1) * 128)
                    if kt == qt:
                        sc.op("dve", lambda e: e.tensor_tensor(out=mkd[:, :], in0=ps[PS_MK][:, kb * 128:(kb + 1) * 128], in1=tri[:, :],
                                                               op=ALU.mult), reads=[("mk", kb), "c3"], writes=["mkd"])
                        mf = bmask(lambda: mkd[:, :].unsqueeze(1).broadcast_to([128, 4, 128]), ["mkd"])
                    else:
                        mf = bmask(lambda kb=kb: ps[PS_MK][:, kb * 128:(kb + 1) * 128].unsqueeze(1).broadcast_to([128, 4, 128]),
                                   [("mk", kb)])
                    unit(KA[64:128, ktok], qr_rhs, [("KAB", 4, kt // 4, 1)] + qreads, mf,
                         V1[:, kt, 0, :], [("V1", kt), "V1ones"], PS_OS, kt == 0, kt == qt)
                k0 = max(0, qt - 4)
                for kt in range(k0, qt + 1):
                    ktok = slice(kt * 128, (kt + 1) * 128)
                    if kt == qt:
                        mf = bmask(lambda: tri[:, :].unsqueeze(1).broadcast_to([128, 4, 128]), ["c3"])
                    elif kt == qt - 4:
                        mf = bmask(lambda: tric[:, :].unsqueeze(1).broadcast_to([128, 4, 128]), ["c3"])
                    else:
                        mf = None
                    unit(KB[64:128, ktok], qr_rhs, [("KAB", 5, kt // 4, 1)] + qreads, mf,
                         V1[:, kt, 1, :], [("V1", kt), "V1ones"], PS_OW, kt == k0, kt == qt)
                for bi, po in ((1, PS_OS), (2, PS_OW)):
                    sc.op("dve", lambda e: e.tensor_scalar_max(
                        out=rz[:, bi, :], in0=ps[po][:, 0:260].rearrange("p (r c) -> p r c", r=4)[:, :, 64], scalar1=1e-30),
                        reads=[("ps", po)], writes=[("rz", bi)])
                    sc.op("dve", lambda e: e.reciprocal(out=rz[:, bi, :], in_=rz[:, bi, :]), reads=[("rz", bi)], writes=[("rz", bi)])
                sc.op("dve", lambda e: e.tensor_tensor(out=cf[:, :, :], in0=rz[:, :, :],
                                                       in1=G[:, qt, :].rearrange("p (r b) -> p b r", b=3), op=ALU.mult),
                      reads=[("rz", 0), ("rz", 1), ("rz", 2), ("G", qt)], writes=["cf"])
                ob = osb[qt % 2]
                for bi, po in ((0, PS_OC), (1, PS_OS), (2, PS_OW)):
                    dst = ob if bi == 0 else otm
                    sc.op("dve", lambda e: e.tensor_tensor(
                        out=dst[:, :, :], in0=ps[po][:, 0:260].rearrange("p (r c) -> p r c", r=4)[:, :, 0:64],
                        in1=cf[:, bi, :].unsqueeze(2).broadcast_to([128, 4, 64]), op=ALU.mult),
                        reads=[("ps", po), "cf"], writes=[("osb", qt % 2) if bi == 0 else "otm"])
                    if bi > 0:
                        sc.op("dve", lambda e: e.tensor_tensor(out=ob[:, :, :], in0=ob[:, :, :], in1=otm[:, :, :], op=ALU.add),
                              reads=[("osb", qt % 2), "otm"], writes=[("osb", qt % 2)])
                sc.dma("sp", "oo%d" % (qt % 2), [(oout[qtok, :], ob[:, :, :].rearrange("p r c -> p (r c)"))],
                       reads=[("osb", qt % 2)])
            sc.barrier()
        sc.final_wait("sp")
    return nc


def nsa_host_inputs(P, j, li, xfull):
    consts = nsa_consts()
    w_in = np.asarray(P["nsa_w_in"][j], np.float32)
    offs = np.cumsum([0, 1024, 256, 256, 256, 256, 256, 256])
    oq, okc, ovc, oks, ovs, okw, ovw, ogl = [int(v) for v in offs]
    in_maps = []
    for c in range(NCORES):
        b, g = divmod(c, 4)
        cols = []
        for r in range(4):
            qc = w_in[:, oq + (g * 4 + r) * 64: oq + (g * 4 + r + 1) * 64]
            cols += [qc, qc]
        sl = lambda o: w_in[:, o + g * 64:o + (g + 1) * 64]
        cols += [sl(okc), sl(oks), sl(ovc), sl(okw)]
        wfm = np.ascontiguousarray(np.concatenate(cols, axis=1))
        wtm = np.ascontiguousarray(np.concatenate([sl(ovs), sl(ovw), w_in[:, ogl + g * 12: ogl + (g + 1) * 12]], axis=1))
        cvec = np.ones((128, 16), np.float32)
        cvec[:, 0:8] = fm(P["mix_norm_g"][li], 8)
        qg = np.asarray(P["nsa_q_norm"][j], np.float32)
        for r in range(4):
            cvec[0:64, 8 + r] = qg
            cvec[64:128, 8 + r] = qg
        cvec[64:128, 12] = np.asarray(P["nsa_ks_norm"][j], np.float32)
        cvec[64:128, 13] = np.asarray(P["nsa_kw_norm"][j], np.float32)
        cvec[0:64, 14] = np.asarray(P["nsa_kc_norm"][j], np.float32)
        m = {"xT": np.ascontiguousarray(xfull[b].T), "wfm": wfm, "wtm": wtm, "cvec": cvec,
             "pekT": np.ascontiguousarray(np.asarray(P["nsa_pe_k"][j], np.float32).T),
             "pevT": np.ascontiguousarray(np.asarray(P["nsa_pe_v"][j], np.float32).T),
             "ck_w1": np.ascontiguousarray(P["nsa_ck_w1"][j], np.float32), "ck_w2": np.ascontiguousarray(P["nsa_ck_w2"][j], np.float32),
             "cv_w1": np.ascontiguousarray(P["nsa_cv_w1"][j], np.float32), "cv_w2": np.ascontiguousarray(P["nsa_cv_w2"][j], np.float32)}
        m.update(consts)
        in_maps.append(m)
    return in_maps


def run_nsa(P, j, li, xfull):
    if "nsa" not in _NC_CACHE:
        _NC_CACHE["nsa"] = build_nsa()
    nc = _NC_CACHE["nsa"]
    in_maps = nsa_host_inputs(P, j, li, xfull)
    res = run_bass_kernel_spmd(nc, in_maps, core_ids=list(range(NCORES)))
    o = np.zeros((B, S, D), np.float32)
    for c in range(NCORES):
        b, g = divmod(c, 4)
        o[b, :, g * 256:(g + 1) * 256] = res.results[c]["o"]
    return o


def kernel(**inputs):
    P = {k: np.asarray(v) for k, v in inputs.items()}
    x = np.ascontiguousarray(P["x"], np.float32)
    x = run_dense([("conv", (0, 0)), ("ffn", 0)], P, x)
    o = run_nsa(P, 0, 1, x)
    x = run_dense([("oproj", 0), ("ffn", 1), ("conv", (2, 1)), ("ffn", 2)], P, x, ofull=o)
    o = run_nsa(P, 1, 3, x)
    x = run_dense([("oproj", 1), ("ffn", 3)], P, x, ofull=o)
    return x.astype(np.float32)
```

```python
import numpy as np
from contextlib import ExitStack
import concourse.bass as bass
import concourse.mybir as mybir
from concourse.bass_utils import run_bass_kernel_spmd

F32 = mybir.dt.float32
BF16 = mybir.dt.bfloat16
ALU = mybir.AluOpType
AF = mybir.ActivationFunctionType
AX = mybir.AxisListType

D = 1024
B = 2
S = 8192
DEPTH = 4
NCORES = 8
TOK = 2048
HALO = 52
TT = TOK + HALO
TW = 420
NT = TT // TW
DFF = 2816
NFC = DFF // 128
CW = 31
EPS = 1e-6
GSZ = 4
GROUPS = [(j, min(GSZ, NFC - j)) for j in range(0, NFC, GSZ)]

HD = 64
NKV = 4
NH = 16
PROJ = 2608


class Sched:
    def __init__(self, nc, es):
        self.nc = nc
        self.es = es
        self.E = {"pe": nc.tensor, "act": nc.scalar, "dve": nc.vector, "pool": nc.gpsimd, "sp": nc.sync}
        self.semh = {}
        self.cnt = {}
        for e in self.E:
            self.semh[e] = es.enter_context(nc.semaphore("s_" + e))
            self.cnt[e] = 0
        self.waited = {e: {} for e in self.E}
        self.res = {}
        self.pending_noinc = {e: False for e in self.E}
        self.log = {e: [] for e in self.E}

    def dsem(self, name):
        k = "d_" + name
        if k not in self.semh:
            self.semh[k] = self.es.enter_context(self.nc.semaphore(k))
            self.cnt[k] = 0
        return k

    def _collect(self, eng, reads, writes):
        waits = {}

        def need(dep, same_ok):
            semk, val, deng = dep
            if deng == eng and (not same_ok or eng == "pe"):
                return
            if self.waited[eng].get(semk, 0) >= val:
                return
            if waits.get(semk, 0) < val:
                waits[semk] = val

        for r in reads:
            st = self.res.get(r)
            if st is not None and st[0] is not None:
                need(st[0], True)
        for w in writes:
            st = self.res.get(w)
            if st is not None:
                if st[0] is not None:
                    need(st[0], True)
                for rd in st[1].values():
                    need(rd, False)
        return waits

    def _emit_waits(self, eng, waits):
        E = self.E[eng]
        for semk, val in waits.items():
            E.wait_ge(self.semh[semk], val)
            self.waited[eng][semk] = val
            self.log[eng].append(("w", semk, val))

    def _record(self, me, reads, writes):
        for r in reads:
            st = self.res.setdefault(r, [None, {}])
            st[1][me[2]] = me
        for w in writes:
            st = self.res.setdefault(w, [None, {}])
            st[0] = me
            st[1] = {}

    def op(self, eng, fn, reads=(), writes=(), inc=True):
        waits = self._collect(eng, reads, writes)
        self._emit_waits(eng, waits)
        inst = fn(self.E[eng])
        if inc:
            self.cnt[eng] += 1
            inst.then_inc(self.semh[eng], 1)
            self.log[eng].append(("i", eng, 1))
            me = (eng, self.cnt[eng], eng)
            self.pending_noinc[eng] = False
        else:
            me = (eng, self.cnt[eng] + 1, eng)
            self.pending_noinc[eng] = True
        self._record(me, reads, writes)
        return inst

    def pe_inc(self, inst):
        self.cnt["pe"] += 1
        inst.then_inc(self.semh["pe"], 1)
        self.log["pe"].append(("i", "pe", 1))

    def mm(self, out, pairs, reads, writes):
        waits = self._collect("pe", reads, writes)
        self._emit_waits("pe", waits)
        n = len(pairs)
        inst = None
        for i, (lhsT, rhs) in enumerate(pairs):
            inst = self.nc.tensor.matmul(out, lhsT=lhsT, rhs=rhs, start=(i == 0), stop=(i == n - 1))
        self.pe_inc(inst)
        me = ("pe", self.cnt["pe"], "pe")
        self._record(me, reads, writes)

    def mm_part(self, out, lhsT, rhs, start, stop, reads, writes, last):
        waits = self._collect("pe", reads, writes)
        self._emit_waits("pe", waits)
        inst = self.nc.tensor.matmul(out, lhsT=lhsT, rhs=rhs, start=start, stop=stop)
        if last:
            self.pe_inc(inst)
            me = ("pe", self.cnt["pe"], "pe")
        else:
            me = ("pe", self.cnt["pe"] + 1, "pe")
        self._record(me, reads, writes)

    def dma(self, q, dname, items, reads=(), writes=()):
        semk = self.dsem(dname)
        waits = self._collect(q, reads, writes)
        if self.cnt[semk] > 0 and self.waited[q].get(semk, 0) < self.cnt[semk]:
            waits[semk] = self.cnt[semk]
        self._emit_waits(q, waits)
        for (o, i) in items:
            inst = self.E[q].dma_start(out=o, in_=i)
            self.cnt[semk] += 16
            inst.then_inc(self.semh[semk], 16)
            self.log[q].append(("i", semk, 16))
        me = (semk, self.cnt[semk], "dma:" + semk)
        self._record(me, reads, writes)

    def check_deadlock(self):
        pos = {e: 0 for e in self.log}
        val = {}
        progress = True
        while progress:
            progress = False
            for e, lg in self.log.items():
                while pos[e] < len(lg):
                    kind, semk, v = lg[pos[e]]
                    if kind == "w":
                        if val.get(semk, 0) >= v:
                            pos[e] += 1; progress = True
                        else:
                            break
                    else:
                        val[semk] = val.get(semk, 0) + v
                        pos[e] += 1; progress = True
        stuck = {e: (pos[e], len(lg), lg[pos[e]] if pos[e] < len(lg) else None, val.get(lg[pos[e]][1]) if pos[e] < len(lg) else None)
                 for e, lg in self.log.items()}
        return all(pos[e] == len(lg) for e, lg in self.log.items()), stuck

    def barrier(self):
        comp = ["pe", "act", "dve", "pool", "sp"]
        for e in comp:
            waits = {}
            for k, v in self.cnt.items():
                if k == e or v == 0:
                    continue
                if self.waited[e].get(k, 0) < v:
                    waits[k] = v
            self._emit_waits(e, waits)
        self.res = {}

    def final_wait(self, eng="sp"):
        waits = {}
        for k, v in self.cnt.items():
            if k == eng or v == 0:
                continue
            if self.waited[eng].get(k, 0) < v:
                waits[k] = v
        self._emit_waits(eng, waits)


def fm(vec, nch):
    return np.ascontiguousarray(np.asarray(vec, np.float32).reshape(nch, 128).T)


class VecPack:
    def __init__(self):
        self.off = {}
        self.n = 0
        self.arrs = []

    def add(self, name, arr):
        arr = np.asarray(arr, np.float32)
        assert arr.shape[0] == 128
        arr = arr.reshape(128, -1)
        self.off[name] = (self.n, arr.shape[1])
        self.n += arr.shape[1]
        self.arrs.append(arr)

    def array(self):
        return np.ascontiguousarray(np.concatenate(self.arrs, axis=1))


def dense_vec_layout(phases):
    off = {}
    n = 0

    def add(name, w):
        nonlocal n
        off[name] = (n, w)
        n += w

    for pi, (kind, _) in enumerate(phases):
        p = "p%d_" % pi
        if kind == "conv":
            add(p + "g", 8); add(p + "b1", 16); add(p + "wdw", 8 * CW); add(p + "bdw", 8)
            add(p + "lng", 8); add(p + "lnb", 8); add(p + "b2", 8)
        elif kind == "ffn":
            add(p + "g", 8); add(p + "wdw", NFC * 3); add(p + "bdw", NFC)
        elif kind == "oproj":
            pass
    return off, n


def build_dense(phases):
    nc = bass.Bass("TRN2", target_bir_lowering=False)
    voff, nv = dense_vec_layout(phases)
    xin = nc.dram_tensor("xT", [D, TT], F32, kind="ExternalInput").ap()
    hm_in = nc.dram_tensor("hmask", [128, HALO], F32, kind="ExternalInput").ap()
    cv_in = nc.dram_tensor("cvec", [128, max(nv, 1)], F32, kind="ExternalInput").ap()
    yout = nc.dram_tensor("yT", [D, TOK], F32, kind="ExternalOutput").ap()
    wd = {}
    for pi, (kind, _) in enumerate(phases):
        p = "p%d_" % pi
        if kind == "conv":
            wd[p + "w1"] = nc.dram_tensor(p + "w1", [D, 2 * D], F32, kind="ExternalInput").ap()
            wd[p + "w2"] = nc.dram_tensor(p + "w2", [D, D], F32, kind="ExternalInput").ap()
        elif kind == "ffn":
            wd[p + "wup"] = nc.dram_tensor(p + "wup", [D, 2 * DFF], F32, kind="ExternalInput").ap()
            wd[p + "wdn"] = nc.dram_tensor(p + "wdn", [DFF, D], F32, kind="ExternalInput").ap()
        elif kind == "oproj":
            wd[p + "wo"] = nc.dram_tensor(p + "wo", [D, D], F32, kind="ExternalInput").ap()
            wd[p + "oT"] = nc.dram_tensor(p + "oT", [D, TT], F32, kind="ExternalInput").ap()

    with ExitStack() as es:
        sc = Sched(nc, es)
        xT = es.enter_context(nc.sbuf_tensor("xTs", [128, 8, TT], F32))
        cv = es.enter_context(nc.sbuf_tensor("cv", [128, max(nv, 1)], F32))
        hmk = es.enter_context(nc.sbuf_tensor("hmk", [128, HALO], F32))
        onesm = es.enter_context(nc.sbuf_tensor("onesm", [128, 128], BF16))
        epsb = es.enter_context(nc.sbuf_tensor("epsb", [128, 1], F32))
        ps = [es.enter_context(nc.psum_tensor("ps%d" % i, [128, 512], F32)) for i in range(8)]

        def V(name, a=0, b=None):
            o, w = voff[name]
            if b is None:
                b = w
            return cv[:, o + a:o + b]

        xin_v = xin.rearrange("(kc p) t -> p kc t", p=128)
        for t in range(NT):
            sc.dma("sp", "xin%d" % t, [(xT[:, :, t * TW:(t + 1) * TW], xin_v[:, :, t * TW:(t + 1) * TW])],
                   writes=[("x", t)])
        sc.dma("sp", "cvl", [(cv[:, :], cv_in[:, :]), (hmk[:, :], hm_in[:, :])], writes=["cv"])
        sc.op("dve", lambda e: e.memset(onesm[:, :], 1.0 / 1024.0), writes=["ones"])
        sc.op("dve", lambda e: e.memset(epsb[:, :], EPS), writes=["eps"])

        def rms_stats(x_ap, reads_x, sq, rs, k, psb):
            sc.op("act", lambda e: e.activation(out=sq[:, :, :], in_=x_ap, func=AF.Square),
                  reads=reads_x, writes=[("sq", k)])
            sc.mm(ps[psb][:, :TW], [(onesm[:, :], sq[:, kc, :]) for kc in range(8)],
                  reads=[("sq", k), "ones"], writes=[("ps", psb)])
            sc.op("act", lambda e: e.activation(out=rs[:, :], in_=ps[psb][:, :TW], func=AF.Sqrt,
                                                bias=epsb[:, 0:1], scale=1.0),
                  reads=[("ps", psb), "eps"], writes=[("rs", k)])
            sc.op("dve", lambda e: e.reciprocal(out=rs[:, :], in_=rs[:, :]), reads=[("rs", k)], writes=[("rs", k)])

        for pi, (kind, _) in enumerate(phases):
            p = "p%d_" % pi
            with ExitStack() as ph:
                if kind == "conv":
                    w1 = ph.enter_context(nc.sbuf_tensor(p + "w1s", [128, 8, 2 * D], BF16))
                    w2 = ph.enter_context(nc.sbuf_tensor(p + "w2s", [128, 8, D], BF16))
                    sqb = [ph.enter_context(nc.sbuf_tensor(p + "sq%d" % i, [128, 8, TW], BF16)) for i in range(2)]
                    rsb = [ph.enter_context(nc.sbuf_tensor(p + "rs%d" % i, [128, TW], F32)) for i in range(2)]
                    hb = [ph.enter_context(nc.sbuf_tensor(p + "hb%d" % i, [128, 8, TW], BF16)) for i in range(2)]
                    uG = ph.enter_context(nc.sbuf_tensor(p + "uG", [128, 8, TW + CW - 1], F32))
                    yb = ph.enter_context(nc.sbuf_tensor(p + "yb", [128, 8, TW], F32))
                    sgb = [ph.enter_context(nc.sbuf_tensor(p + "sg%d" % i, [128, TW], F32)) for i in range(2)]
                    ybf = [ph.enter_context(nc.sbuf_tensor(p + "ybf%d" % i, [128, TW], BF16)) for i in range(2)]
                    ysq = [ph.enter_context(nc.sbuf_tensor(p + "ysq%d" % i, [128, TW], BF16)) for i in range(2)]
                    zb = ph.enter_context(nc.sbuf_tensor(p + "zb", [128, 8, TW], BF16))
                    mu = ph.enter_context(nc.sbuf_tensor(p + "mu", [128, TW], F32))
                    var = ph.enter_context(nc.sbuf_tensor(p + "var", [128, TW], F32))
                    nb = ph.enter_context(nc.sbuf_tensor(p + "nb", [128, TW], F32))
                    ntm = [ph.enter_context(nc.sbuf_tensor(p + "ntm%d" % i, [128, TW], F32)) for i in range(2)]
                    w1v = wd[p + "w1"].rearrange("(kc p) n -> p kc n", p=128)
                    w2v = wd[p + "w2"].rearrange("(kc p) n -> p kc n", p=128)
                    sc.dma("pool", "w1a", [(w1[:, 0:4, :], w1v[:, 0:4, :])], writes=["w1a"])
                    sc.dma("pool", "w1b", [(w1[:, 4:8, :], w1v[:, 4:8, :])], writes=["w1b"])
                    sc.dma("pool", "w2", [(w2[:, :, :], w2v[:, :, :])], writes=["w2"])
                    sc.op("dve", lambda e: e.memset(uG[:, :, 0:CW - 1], 0.0), writes=[("uG", c) for c in range(8)])
                    PS_ST, PS_A, PS_G, PS_MU, PS_E2 = 0, (1, 2), (3, 4), 5, 6
                    for t in range(NT):
                        cols = slice(t * TW, (t + 1) * TW)
                        k = t % 2
                        rms_stats(xT[:, :, cols], [("x", t)], sqb[k], rsb[k], k, PS_ST)
                        for kc in range(8):
                            sc.op("dve", lambda e: e.scalar_tensor_tensor(
                                out=hb[k][:, kc, :], in0=xT[:, kc, cols], scalar=V(p + "g", kc, kc + 1),
                                in1=rsb[k][:, :], op0=ALU.mult, op1=ALU.mult),
                                reads=[("x", t), ("rs", k), "cv"], writes=[("hb", k)])
                        for c2 in range(0, 8, 2):
                            for c in (c2, c2 + 1):
                                pa, pg = PS_A[c % 2], PS_G[c % 2]
                                sc.mm(ps[pa][:, :TW], [(w1[:, kc, c * 128:(c + 1) * 128], hb[k][:, kc, :]) for kc in range(8)],
                                      reads=[("hb", k), "w1a", "w1b"], writes=[("ps", pa)])
                                sc.mm(ps[pg][:, :TW], [(w1[:, kc, D + c * 128:D + (c + 1) * 128], hb[k][:, kc, :]) for kc in range(8)],
                                      reads=[("hb", k), "w1a", "w1b"], writes=[("ps", pg)])
                                sc.op("act", lambda e: e.activation(out=sgb[c % 2][:, :], in_=ps[pg][:, :TW], func=AF.Sigmoid,
                                                                    bias=V(p + "b1", 8 + c, 9 + c), scale=1.0),
                                      reads=[("ps", pg), "cv"], writes=[("sg", c % 2)])
                                sc.op("dve", lambda e: e.scalar_tensor_tensor(
                                    out=uG[:, c, CW - 1:], in0=ps[pa][:, :TW], scalar=V(p + "b1", c, c + 1),
                                    in1=sgb[c % 2][:, :], op0=ALU.add, op1=ALU.mult),
                                    reads=[("ps", pa), ("sg", c % 2), "cv"], writes=[("uG", c)])
                                if t == 0:
                                    sc.op("dve", lambda e: e.tensor_tensor(
                                        out=uG[:, c, CW - 1:CW - 1 + HALO], in0=uG[:, c, CW - 1:CW - 1 + HALO],
                                        in1=hmk[:, :], op=ALU.mult), reads=[("uG", c), "cv"], writes=[("uG", c)])
                            for kk in range(CW):
                                for c in (c2, c2 + 1):
                                    wk = V(p + "wdw", c * CW + kk, c * CW + kk + 1)
                                    if kk == 0:
                                        sc.op("dve", lambda e: e.tensor_scalar(
                                            out=yb[:, c, :], in0=uG[:, c, 0:TW], scalar1=wk, scalar2=V(p + "bdw", c, c + 1),
                                            op0=ALU.mult, op1=ALU.add), reads=[("uG", c), "cv"], writes=[("y", c)])
                                    else:
                                        sc.op("dve", lambda e: e.scalar_tensor_tensor(
                                            out=yb[:, c, :], in0=uG[:, c, kk:kk + TW], scalar=wk, in1=yb[:, c, :],
                                            op0=ALU.mult, op1=ALU.add), reads=[("uG", c), ("y", c), "cv"], writes=[("y", c)])
                            for c in (c2, c2 + 1):
                                sc.op("act", lambda e: e.activation(out=ybf[c % 2][:, :], in_=yb[:, c, :], func=AF.Copy),
                                      reads=[("y", c)], writes=[("ybf", c % 2)])
                                sc.op("act", lambda e: e.activation(out=ysq[c % 2][:, :], in_=yb[:, c, :], func=AF.Square),
                                      reads=[("y", c)], writes=[("ysq", c % 2)])
                                sc.mm_part(ps[PS_MU][:, :TW], onesm[:, :], ybf[c % 2][:, :], c == 0, c == 7,
                                           reads=[("ybf", c % 2), "ones"], writes=[("ps", PS_MU)], last=True)
                                sc.mm_part(ps[PS_E2][:, :TW], onesm[:, :], ysq[c % 2][:, :], c == 0, c == 7,
                                           reads=[("ysq", c % 2), "ones"], writes=[("ps", PS_E2)], last=True)
                        sc.op("dve", lambda e: e.tensor_copy(out=uG[:, :, 0:CW - 1], in_=uG[:, :, TW:TW + CW - 1]),
                              reads=[("uG", c) for c in range(8)], writes=[("uG", c) for c in range(8)])
                        sc.op("act", lambda e: e.activation(out=mu[:, :], in_=ps[PS_MU][:, :TW], func=AF.Copy),
                              reads=[("ps", PS_MU)], writes=["mu"])
                        sc.op("dve", lambda e: e.tensor_tensor(out=var[:, :], in0=mu[:, :], in1=mu[:, :], op=ALU.mult),
                              reads=["mu"], writes=["var"])
                        sc.op("dve", lambda e: e.tensor_tensor(out=var[:, :], in0=ps[PS_E2][:, :TW], in1=var[:, :], op=ALU.subtract),
                              reads=[("ps", PS_E2), "var"], writes=["var"])
                        sc.op("dve", lambda e: e.tensor_scalar_max(out=var[:, :], in0=var[:, :], scalar1=0.0),
                              reads=["var"], writes=["var"])
                        sc.op("act", lambda e: e.activation(out=var[:, :], in_=var[:, :], func=AF.Sqrt, bias=epsb[:, 0:1], scale=1.0),
                              reads=["var", "eps"], writes=["var"])
                        sc.op("dve", lambda e: e.reciprocal(out=var[:, :], in_=var[:, :]), reads=["var"], writes=["var"])
                        sc.op("dve", lambda e: e.scalar_tensor_tensor(out=nb[:, :], in0=mu[:, :], scalar=-1.0, in1=var[:, :],
                                                                      op0=ALU.mult, op1=ALU.mult),
                              reads=["mu", "var"], writes=["nb"])
                        for c in range(8):
                            nt_ = ntm[c % 2]
                            sc.op("dve", lambda e: e.tensor_tensor(out=nt_[:, :], in0=yb[:, c, :], in1=var[:, :], op=ALU.mult),
                                  reads=[("y", c), "var"], writes=[("ntm", c % 2)])
                            sc.op("pool", lambda e: e.tensor_tensor(out=nt_[:, :], in0=nt_[:, :], in1=nb[:, :], op=ALU.add),
                                  reads=[("ntm", c % 2), "nb"], writes=[("ntm", c % 2)])
                            sc.op("act", lambda e: e.activation(out=zb[:, c, :], in_=nt_[:, :], func=AF.Silu,
                                                                bias=V(p + "lnb", c, c + 1), scale=V(p + "lng", c, c + 1)),
                                  reads=[("ntm", c % 2), "cv"], writes=[("z", c)])
                        for m in range(8):
                            po = PS_A[m % 2]
                            sc.mm(ps[po][:, :TW], [(w2[:, c, m * 128:(m + 1) * 128], zb[:, c, :]) for c in range(8)],
                                  reads=[("z", c) for c in range(8)] + ["w2"], writes=[("ps", po)])
                            sc.op("dve", lambda e: e.scalar_tensor_tensor(
                                out=xT[:, m, cols], in0=ps[po][:, :TW], scalar=V(p + "b2", m, m + 1), in1=xT[:, m, cols],
                                op0=ALU.add, op1=ALU.add), reads=[("ps", po), ("x", t), "cv"], writes=[("x", t)])

                elif kind == "oproj":
                    wo = ph.enter_context(nc.sbuf_tensor(p + "wos", [128, 8, D], BF16))
                    oT = ph.enter_context(nc.sbuf_tensor(p + "oTs", [128, 8, TT], BF16))
                    wov = wd[p + "wo"].rearrange("(kc p) n -> p kc n", p=128)
                    oTv = wd[p + "oT"].rearrange("(kc p) t -> p kc t", p=128)
                    sc.dma("pool", "wo", [(wo[:, :, :], wov[:, :, :])], writes=["wo"])
                    for t in range(NT):
                        cols = slice(t * TW, (t + 1) * TW)
                        sc.dma("pool", "oT%d" % t, [(oT[:, :, cols], oTv[:, :, cols])], writes=[("oT", t)])
                    for t in range(NT):
                        cols = slice(t * TW, (t + 1) * TW)
                        for m in range(8):
                            po = 1 + (m % 2)
                            sc.mm(ps[po][:, :TW], [(wo[:, kc, m * 128:(m + 1) * 128], oT[:, kc, cols]) for kc in range(8)],
                                  reads=[("oT", t), "wo"], writes=[("ps", po)])
                            sc.op("dve", lambda e: e.tensor_tensor(out=xT[:, m, cols], in0=ps[po][:, :TW], in1=xT[:, m, cols],
                                                                   op=ALU.add),
                                  reads=[("ps", po), ("x", t)], writes=[("x", t)])

                elif kind == "ffn":
                    hall = ph.enter_context(nc.sbuf_tensor(p + "hall", [128, 8, TT + 2], BF16))
                    wa = [ph.enter_context(nc.sbuf_tensor(p + "wa%d" % i, [128, 8, GSZ * 128], BF16)) for i in range(2)]
                    wv = [ph.enter_context(nc.sbuf_tensor(p + "wv%d" % i, [128, 8, GSZ * 128], BF16)) for i in range(2)]
                    wdn = [ph.enter_context(nc.sbuf_tensor(p + "wdn%d" % i, [128, GSZ, D], BF16)) for i in range(2)]
                    sqb = [ph.enter_context(nc.sbuf_tensor(p + "sq%d" % i, [128, 8, TW], BF16)) for i in range(2)]
                    rsb = [ph.enter_context(nc.sbuf_tensor(p + "rs%d" % i, [128, TW], F32)) for i in range(2)]
                    acc = [ph.enter_context(nc.sbuf_tensor(p + "acc%d" % i, [128, TW], F32)) for i in range(2)]
                    sil = [ph.enter_context(nc.sbuf_tensor(p + "sil%d" % i, [128, TW], F32)) for i in range(2)]
                    gb = [ph.enter_context(nc.sbuf_tensor(p + "gb%d" % i, [128, GSZ, TW], BF16)) for i in range(2)]
                    wupv = wd[p + "wup"].rearrange("(kc p) n -> p kc n", p=128)
                    wdnv = wd[p + "wdn"].rearrange("(jc p) n -> p jc n", p=128)

                    def load_group(gi):
                        j0, G = GROUPS[gi]
                        s = gi % 2
                        sc.dma("pool", "wg%d" % s, [
                            (wa[s][:, :, 0:G * 128], wupv[:, :, j0 * 128:(j0 + G) * 128]),
                            (wv[s][:, :, 0:G * 128], wupv[:, :, DFF + j0 * 128:DFF + (j0 + G) * 128]),
                            (wdn[s][:, 0:G, :], wdnv[:, j0:j0 + G, :]),
                        ], writes=[("wg", s)])

                    load_group(0)
                    sc.op("dve", lambda e: e.memset(hall[:, :, 0:2], 0.0), writes=[("h", -1)])
                    for t in range(NT):
                        cols = slice(t * TW, (t + 1) * TW)
                        k = t % 2
                        rms_stats(xT[:, :, cols], [("x", t)], sqb[k], rsb[k], k, 0)
                        for kc in range(8):
                            sc.op("dve", lambda e: e.scalar_tensor_tensor(
                                out=hall[:, kc, 2 + t * TW:2 + (t + 1) * TW], in0=xT[:, kc, cols], scalar=V(p + "g", kc, kc + 1),
                                in1=rsb[k][:, :], op0=ALU.mult, op1=ALU.mult),
                                reads=[("x", t), ("rs", k), "cv"], writes=[("h", t)])
                        if t == 0:
                            sc.op("dve", lambda e: e.tensor_tensor(
                                out=hall[:, :, 2:2 + HALO], in0=hall[:, :, 2:2 + HALO],
                                in1=hmk[:, :].unsqueeze(1).broadcast_to([128, 8, HALO]), op=ALU.mult),
                                reads=[("h", 0), "cv"], writes=[("h", 0)])
                    PA, PV, PD = (1, 2), (3, 4), (5, 6)
                    it = 0
                    pend = None

                    def emit_down(gi, t, gk):
                        j0, G = GROUPS[gi]
                        s = gi % 2
                        cols = slice(t * TW, (t + 1) * TW)
                        for m in range(8):
                            pd = PD[m % 2]
                            sc.mm(ps[pd][:, :TW], [(wdn[s][:, jj, m * 128:(m + 1) * 128], gb[gk][:, jj, :]) for jj in range(G)],
                                  reads=[("g", gk), ("wg", s)], writes=[("ps", pd)])
                            sc.op("dve", lambda e: e.tensor_tensor(out=xT[:, m, cols], in0=ps[pd][:, :TW], in1=xT[:, m, cols],
                                                                   op=ALU.add),
                                  reads=[("ps", pd), ("x", t)], writes=[("x", t)])

                    for gi, (j0, G) in enumerate(GROUPS):
                        s = gi % 2
                        for t in range(NT):
                            gk = it % 2
                            hreads = [("h", t), ("h", t - 1)]
                            for jj in range(G):
                                j = j0 + jj
                                pa, pv = PA[jj % 2], PV[jj % 2]
                                sc.mm(ps[pa][:, :TW + 2], [(wa[s][:, kc, jj * 128:(jj + 1) * 128], hall[:, kc, t * TW:t * TW + TW + 2])
                                                            for kc in range(8)],
                                      reads=hreads + [("wg", s)], writes=[("ps", pa)])
                                sc.mm(ps[pv][:, :TW], [(wv[s][:, kc, jj * 128:(jj + 1) * 128], hall[:, kc, 2 + t * TW:2 + (t + 1) * TW])
                                                       for kc in range(8)],
                                      reads=hreads + [("wg", s)], writes=[("ps", pv)])
                                a_ = acc[jj % 2]
                                sc.op("dve", lambda e: e.tensor_scalar(
                                    out=a_[:, :], in0=ps[pa][:, 2:TW + 2], scalar1=V(p + "wdw", j * 3 + 2, j * 3 + 3),
                                    scalar2=V(p + "bdw", j, j + 1), op0=ALU.mult, op1=ALU.add),
                                    reads=[("ps", pa), "cv"], writes=[("acc", jj % 2)])
                                sc.op("dve", lambda e: e.scalar_tensor_tensor(
                                    out=a_[:, :], in0=ps[pa][:, 1:TW + 1], scalar=V(p + "wdw", j * 3 + 1, j * 3 + 2), in1=a_[:, :],
                                    op0=ALU.mult, op1=ALU.add), reads=[("ps", pa), ("acc", jj % 2), "cv"], writes=[("acc", jj % 2)])
                                sc.op("dve", lambda e: e.scalar_tensor_tensor(
                                    out=a_[:, :], in0=ps[pa][:, 0:TW], scalar=V(p + "wdw", j * 3, j * 3 + 1), in1=a_[:, :],
                                    op0=ALU.mult, op1=ALU.add), reads=[("ps", pa), ("acc", jj % 2), "cv"], writes=[("acc", jj % 2)])
                                sc.op("act", lambda e: e.activation(out=sil[jj % 2][:, :], in_=a_[:, :], func=AF.Silu),
                                      reads=[("acc", jj % 2)], writes=[("sil", jj % 2)])
                                sc.op("dve", lambda e: e.tensor_tensor(out=gb[gk][:, jj, :], in0=ps[pv][:, :TW], in1=sil[jj % 2][:, :],
                                                                       op=ALU.mult),
                                      reads=[("ps", pv), ("sil", jj % 2)], writes=[("g", gk)])
                            if pend is not None:
                                emit_down(*pend)
                            pend = (gi, t, gk)
                            it += 1
                            if t == 0 and gi + 1 < len(GROUPS):
                                if pend is not None and pend[0] != gi:
                                    pass
                                load_group(gi + 1)
                    emit_down(*pend)
                sc.barrier()

        yv = yout.rearrange("(kc p) t -> p kc t", p=128)
        for t in range(NT):
            lo = max(t * TW, HALO)
            hi = (t + 1) * TW
            sc.dma("sp", "yout%d" % t, [(yv[:, :, lo - HALO:hi - HALO], xT[:, :, lo:hi])], reads=[("x", t)])
        sc.final_wait("sp")
    return nc


def shard_xT(xfull):
    outs = []
    for c in range(NCORES):
        b, q = divmod(c, 4)
        t0 = q * TOK
        buf = np.zeros((TT, xfull.shape[2]), np.float32)
        lo = t0 - HALO
        if lo < 0:
            buf[-lo:] = xfull[b, 0:t0 + TOK]
        else:
            buf[:] = xfull[b, lo:t0 + TOK]
        outs.append(np.ascontiguousarray(buf.T))
    return outs


def unshard_yT(ys):
    out = np.zeros((B, S, D), np.float32)
    for c in range(NCORES):
        b, q = divmod(c, 4)
        out[b, q * TOK:(q + 1) * TOK] = ys[c].T
    return out


def dense_host_inputs(phases, P, xfull, ofull=None):
    vp = VecPack()
    wmaps = {}
    for pi, (kind, idx) in enumerate(phases):
        p = "p%d_" % pi
        if kind == "conv":
            li, j = idx
            vp.add(p + "g", fm(P["mix_norm_g"][li], 8))
            vp.add(p + "b1", fm(P["conv_b_pw1"][j], 16))
            wdw = np.asarray(P["conv_w_dw"][j], np.float32)
            vp.add(p + "wdw", np.ascontiguousarray(wdw.reshape(CW, 8, 128).transpose(2, 1, 0)).reshape(128, 8 * CW))
            vp.add(p + "bdw", fm(P["conv_b_dw"][j], 8))
            vp.add(p + "lng", fm(P["conv_ln_g"][j], 8))
            vp.add(p + "lnb", fm(P["conv_ln_b"][j], 8))
            vp.add(p + "b2", fm(P["conv_b_pw2"][j], 8))
            wmaps[p + "w1"] = np.ascontiguousarray(P["conv_w_pw1"][j], np.float32)
            wmaps[p + "w2"] = np.ascontiguousarray(P["conv_w_pw2"][j], np.float32)
        elif kind == "ffn":
            li = idx
            vp.add(p + "g", fm(P["ffn_norm_g"][li], 8))
            wdw = np.asarray(P["ffn_w_dw"][li], np.float32)
            vp.add(p + "wdw", np.ascontiguousarray(wdw.reshape(3, NFC, 128).transpose(2, 1, 0)).reshape(128, NFC * 3))
            vp.add(p + "bdw", fm(P["ffn_b_dw"][li], NFC))
            wmaps[p + "wup"] = np.ascontiguousarray(P["ffn_w_up"][li], np.float32)
            wmaps[p + "wdn"] = np.ascontiguousarray(P["ffn_w_down"][li], np.float32)
        elif kind == "oproj":
            j = idx
            wmaps[p + "wo"] = np.ascontiguousarray(P["nsa_w_out"][j], np.float32)
    voff, nv = dense_vec_layout(phases)
    assert voff == vp.off, (voff, vp.off)
    cvec = vp.array() if vp.n > 0 else np.zeros((128, 1), np.float32)
    xs = shard_xT(xfull)
    oTs = shard_xT(ofull) if ofull is not None else None
    in_maps = []
    for c in range(NCORES):
        q = c % 4
        m = {"xT": xs[c], "cvec": cvec,
             "hmask": np.full((128, HALO), 0.0 if q == 0 else 1.0, np.float32)}
        m.update(wmaps)
        for pi, (kind, idx) in enumerate(phases):
            if kind == "oproj":
                m["p%d_oT" % pi] = oTs[c]
        in_maps.append(m)
    return in_maps


_NC_CACHE = {}


def run_dense(phases, P, xfull, ofull=None):
    key = ("dense", tuple(k for k, _ in phases))
    if key not in _NC_CACHE:
        _NC_CACHE[key] = build_dense(phases)
    nc = _NC_CACHE[key]
    in_maps = dense_host_inputs(phases, P, xfull, ofull)
    res = run_bass_kernel_spmd(nc, in_maps, core_ids=list(range(NCORES)))
    return unshard_yT([r["yT"] for r in res.results])


NTK = 512
NTT = S // NTK
NQT = S // 128
NCMP = 511
ROT = 16


def nsa_consts():
    c = {}
    half = ROT // 2
    inv_freq = (np.float32(500000.0) ** (-np.arange(half, dtype=np.float32) * np.float32(2.0 / ROT))).astype(np.float32)
    ang = (np.arange(S, dtype=np.float32)[None, :] * inv_freq[:, None]).astype(np.float32)
    cos = np.cos(ang).astype(np.float32)
    sin = np.sin(ang).astype(np.float32)
    c["ropeC"] = np.ascontiguousarray(np.concatenate([cos, cos], 0))
    c["ropeS"] = np.ascontiguousarray(np.concatenate([sin, sin], 0))
    prot = np.zeros((128, 128), np.float32)
    for i in range(half):
        prot[64 + i + half, 64 + i] = -1.0
        prot[64 + i, 64 + i + half] = 1.0
    c["prot"] = prot
    c["ident"] = np.eye(128, dtype=np.float32)
    p = np.arange(128)[:, None]
    t = np.arange(128)[None, :]
    c["tri"] = (p <= t).astype(np.float32)
    c["tric"] = (p > t).astype(np.float32)
    cm = np.zeros((17, 128, 128), np.float32)
    for mi in range(17):
        m = 8 * mi
        cm[mi] = (16 * (p - m) + 31 <= t).astype(np.float32)
    c["cmask"] = np.ascontiguousarray(cm.transpose(1, 0, 2))
    n = np.arange(512)[:, None] * 16
    s_ = np.arange(128)[None, :] * 64
    ov = np.minimum(n + 32, s_ + 64) - np.maximum(n, s_)
    cmap = np.maximum(ov, 0).astype(np.float32) / 32.0
    cmap[511] = 0.0
    c["cmap"] = np.ascontiguousarray(cmap.reshape(4, 128, 128).transpose(1, 0, 2))
    ext = np.zeros((128, S), np.float32)
    ext[np.arange(S) // 64, np.arange(S)] = 1.0
    c["ext"] = ext
    d = np.arange(-127, 129)[None, :]
    tt = np.arange(128)[:, None]
    lo = (tt < 64)
    t1 = np.where(lo, (d <= -2), (d <= -1)).astype(np.float32)
    t2 = np.where(lo,
                  np.where((d == -1) | (d == 0), 1e4, np.where(d > 0, -1.0, 0.0)),
                  np.where((d == 0) | (d == 1), 1e4, np.where(d > 1, -1.0, 0.0))).astype(np.float32)
    c["t1"] = np.ascontiguousarray(t1 * np.ones((128, 1), np.float32))
    c["t2"] = np.ascontiguousarray(t2)
    return c


def build_nsa():
    nc = bass.Bass("TRN2", target_bir_lowering=False)

    def din(name, shape):
        return nc.dram_tensor(name, shape, F32, kind="ExternalInput").ap()

    xin = din("xT", [D, S])
    wfm_in = din("wfm", [D, 768])
    wtm_in = din("wtm", [D, 140])
    cv_in = din("cvec", [128, 8 + 6 + 2])
    pek_in = din("pekT", [64, 32]); pev_in = din("pevT", [64, 32])
    ckw1_in = din("ck_w1", [2048, 256]); ckw2_in = din("ck_w2", [256, 64])
    cvw1_in = din("cv_w1", [2048, 256]); cvw2_in = din("cv_w2", [256, 64])
    ropeC_in = din("ropeC", [16, S]); ropeS_in = din("ropeS", [16, S])
    prot_in = din("prot", [128, 128]); ident_in = din("ident", [128, 128])
    tri_in = din("tri", [128, 128]); tric_in = din("tric", [128, 128])
    cmask_in = din("cmask", [128, 17, 128]); cmap_in = din("cmap", [128, 4, 128])
    ext_in = din("ext", [128, S]); t1_in = din("t1", [128, 256]); t2_in = din("t2", [128, 256])
    oout = nc.dram_tensor("o", [S, 256], F32, kind="ExternalOutput").ap()

    with ExitStack() as es:
        sc = Sched(nc, es)
        SB = lambda name, shape, dt: es.enter_context(nc.sbuf_tensor("sb_" + name, shape, dt))
        Q = SB("Q", [128, 4, S], BF16)
        KA = SB("KA", [128, S], BF16)
        KB = SB("KB", [128, S], BF16)
        V1 = SB("V1", [128, NQT, 2, 65], BF16)
        G = SB("G", [128, NQT, 12], F32)
        KCMP = SB("KCMP", [128, 512], BF16)
        VC1 = SB("VC1", [128, 4, 65], BF16)
        cv = SB("cv", [128, 16], F32)
        onesm = SB("onesm", [128, 128], BF16)
        BD = SB("BD", [128, 128], BF16)
        epsb = SB("epsb", [128, 1], F32)
        prot = SB("prot", [128, 128], F32)
        ps = [es.enter_context(nc.psum_tensor("ps%d" % i, [128, 512], F32)) for i in range(7)]
        psb = es.enter_context(nc.psum_tensor("psb", [128, 1024], BF16))

        sc.dma("sp", "c0", [(cv[:, :], cv_in[:, :]), (prot[:, :], prot_in[:, :])], writes=["cv", "prot"])
        sc.op("dve", lambda e: e.memset(onesm[:, :], 1.0 / 1024.0), writes=["ones"])
        sc.op("dve", lambda e: e.memset(epsb[:, :], EPS), writes=["eps"])
        sc.op("dve", lambda e: e.memset(BD[:, :], 0.0), writes=["BD"])
        sc.op("dve", lambda e: e.memset(BD[0:64, 0:64], 1.0 / 64.0), reads=["BD"], writes=["BD"])
        sc.op("dve", lambda e: e.memset(BD[64:128, 64:128], 1.0 / 64.0), reads=["BD"], writes=["BD"])
        sc.op("dve", lambda e: e.memset(V1[:, :, :, 64:65], 1.0), writes=["V1ones"])
        sc.op("dve", lambda e: e.memset(VC1[:, :, 64:65], 1.0), writes=["VC1ones"])

        with ExitStack() as ph:
            PB = lambda name, shape, dt: ph.enter_context(nc.sbuf_tensor("sb_" + name, shape, dt))
            wfm = PB("wfm", [128, 8, 768], BF16)
            wtm = PB("wtm", [128, 8, 140], BF16)
            xt_ = PB("xt_", [128, 8, NTK], F32)
            xt = [xt_, xt_]
            sq1_ = PB("sq_", [128, 8, NTK], BF16)
            sq = [sq1_, sq1_]
            rs = [PB("rs%d" % i, [128, NTK], F32) for i in range(2)]
            hb = [PB("hb%d" % i, [128, 8, NTK], BF16) for i in range(2)]
            csC = [PB("csC%d" % i, [128, NTK], F32) for i in range(2)]
            csS = [PB("csS%d" % i, [128, NTK], F32) for i in range(2)]
            sq2 = [PB("sq2%d" % i, [128, NTK], BF16) for i in range(2)]
            rt = [PB("rt%d" % i, [128, NTK], F32) for i in range(2)]
            qn = [PB("qn%d" % i, [128, NTK], F32) for i in range(2)]
            r1_ = PB("r1_", [128, NTK], F32)
            r2_ = PB("r2_", [128, NTK], F32)
            r1 = [r1_, r1_]
            r2 = [r2_, r2_]
            xv = xin.rearrange("(kc p) t -> p kc t", p=128)
            sc.dma("pool", "wfm", [(wfm[:, :, :], wfm_in.rearrange("(kc p) n -> p kc n", p=128)),
                                   (wtm[:, :, :], wtm_in.rearrange("(kc p) n -> p kc n", p=128))], writes=["wfm"])

            def load_x(tt):
                k = tt % 2
                tok = slice(tt * NTK, (tt + 1) * NTK)
                sc.dma("sp", "xt0", [(xt[k][:, :, :], xv[:, :, tok])], writes=[("xt", 0)])
                sc.dma("sp", "cs%d" % k, [(csC[k][64:80, :], ropeC_in[:, tok]), (csS[k][64:80, :], ropeS_in[:, tok])],
                       writes=[("cs", k)])

            load_x(0)
            cbi = 0
            for tt in range(NTT):
                k = tt % 2
                tok = slice(tt * NTK, (tt + 1) * NTK)
                sc.op("act", lambda e: e.activation(out=sq[k][:, :, :], in_=xt[k][:, :, :], func=AF.Square),
                      reads=[("xt", 0)], writes=[("sq", 0)])
                sc.mm(ps[0][:, :], [(onesm[:, :], sq[k][:, kc, :]) for kc in range(8)],
                      reads=[("sq", 0), "ones"], writes=[("ps", 0)])
                sc.op("act", lambda e: e.activation(out=rs[k][:, :], in_=ps[0][:, :], func=AF.Sqrt, bias=epsb[:, 0:1], scale=1.0),
                      reads=[("ps", 0), "eps"], writes=[("rs", k)])
                sc.op("dve", lambda e: e.reciprocal(out=rs[k][:, :], in_=rs[k][:, :]), reads=[("rs", k)], writes=[("rs", k)])
                for kc in range(8):
                    sc.op("dve", lambda e: e.scalar_tensor_tensor(
                        out=hb[k][:, kc, :], in0=xt[k][:, kc, :], scalar=cv[:, kc:kc + 1], in1=rs[k][:, :],
                        op0=ALU.mult, op1=ALU.mult), reads=[("xt", 0), ("rs", k), "cv"], writes=[("hb", k)])
                if tt + 1 < NTT:
                    load_x(tt + 1)
                for sub in range(4):
                    kt = tt * 4 + sub
                    pt = 1 + sub % 2
                    sc.mm(ps[pt][:, 0:140], [(hb[k][:, kc, sub * 128:(sub + 1) * 128], wtm[:, kc, :]) for kc in range(8)],
                          reads=[("hb", k), "wfm"], writes=[("ps", pt)])
                    sc.op("act", lambda e: e.activation(out=V1[:, kt, :, 0:64],
                                                        in_=ps[pt][:, 0:128].rearrange("p (a b) -> p a b", a=2), func=AF.Copy),
                          reads=[("ps", pt)], writes=[("V1", kt)])
                    sc.op("act", lambda e: e.activation(out=G[:, kt, :], in_=ps[pt][:, 128:140], func=AF.Sigmoid),
                          reads=[("ps", pt)], writes=[("G", kt)])
                for cb in range(6):
                    j = cbi % 2
                    cbi += 1
                    P1, P2 = 3 + j, 5 + j
                    sc.mm(ps[P1][:, :], [(wfm[:, kc, cb * 128:(cb + 1) * 128], hb[k][:, kc, :]) for kc in range(8)],
                          reads=[("hb", k), "wfm"], writes=[("ps", P1)])
                    sc.op("act", lambda e: e.activation(out=sq2[j][:, :], in_=ps[P1][:, :], func=AF.Square),
                          reads=[("ps", P1)], writes=[("sq2", j)])
                    sc.mm(ps[P2][:, :], [(BD[:, :], sq2[j][:, :])], reads=[("sq2", j), "BD"], writes=[("ps", P2)])
                    sc.op("act", lambda e: e.activation(out=rt[j][:, :], in_=ps[P2][:, :], func=AF.Sqrt, bias=epsb[:, 0:1], scale=1.0),
                          reads=[("ps", P2), "eps"], writes=[("rt", j)])
                    sc.op("dve", lambda e: e.reciprocal(out=rt[j][:, :], in_=rt[j][:, :]), reads=[("rt", j)], writes=[("rt", j)])
                    gcol = cv[:, 8 + cb:9 + cb]
                    if cb < 4:
                        sc.op("dve", lambda e: e.scalar_tensor_tensor(
                            out=qn[j][:, :], in0=ps[P1][:, :], scalar=gcol, in1=rt[j][:, :], op0=ALU.mult, op1=ALU.mult),
                            reads=[("ps", P1), ("rt", j), "cv"], writes=[("qn", j)])
                    else:
                        dst = KA if cb == 4 else KB
                        sc.op("act", lambda e: e.activation(out=dst[0:64, tok], in_=ps[P1][0:64, :], func=AF.Copy),
                              reads=[("ps", P1)], writes=[("KAB", cb, tt, 0)])
                        sc.op("dve", lambda e: e.memset(qn[j][0:64, :], 0.0), writes=[("qn", j)], reads=[("qn", j)])
                        sc.op("dve", lambda e: e.scalar_tensor_tensor(
                            out=qn[j][64:128, :], in0=ps[P1][64:128, :], scalar=gcol[64:128, :], in1=rt[j][64:128, :],
                            op0=ALU.mult, op1=ALU.mult),
                            reads=[("ps", P1), ("rt", j), "cv", ("qn", j)], writes=[("qn", j)])
                    PR = 0
                    sc.mm(ps[PR][:, :], [(prot[:, :], qn[j][:, :])], reads=[("qn", j), "prot"], writes=[("ps", PR)])
                    sc.op("dve", lambda e: e.tensor_tensor(out=r1[j][64:80, :], in0=qn[j][64:80, :], in1=csC[k][64:80, :], op=ALU.mult),
                          reads=[("qn", j), ("cs", k)], writes=[("r1", 0)])
                    sc.op("dve", lambda e: e.tensor_tensor(out=r2[j][64:80, :], in0=ps[PR][64:80, :], in1=csS[k][64:80, :], op=ALU.mult),
                          reads=[("ps", PR), ("cs", k)], writes=[("r2", 0)])
                    sc.op("dve", lambda e: e.tensor_tensor(out=qn[j][64:80, :], in0=r1[j][64:80, :], in1=r2[j][64:80, :], op=ALU.add),
                          reads=[("r1", 0), ("r2", 0), ("qn", j), ("ps", PR)], writes=[("qn", j)])
                    if cb < 4:
                        sc.op("act", lambda e: e.activation(out=Q[0:64, cb, tok], in_=qn[j][0:64, :], func=AF.Copy),
                              reads=[("qn", j)], writes=[("Q", cb, tt, 0)])
                        sc.op("pool", lambda e: e.tensor_copy(out=Q[64:128, cb, tok], in_=qn[j][64:128, :]),
                              reads=[("qn", j)], writes=[("Q", cb, tt, 1)])
                    else:
                        dst = KA if cb == 4 else KB
                        sc.op("pool", lambda e: e.tensor_copy(out=dst[64:128, tok], in_=qn[j][64:128, :]),
                              reads=[("qn", j)], writes=[("KAB", cb, tt, 1)])
            sc.barrier()

        with ExitStack() as ph:
            PB = lambda name, shape, dt: ph.enter_context(nc.sbuf_tensor("sb_" + name, shape, dt))
            w1 = [PB("cw1%d" % i, [64, 32, 256], BF16) for i in range(2)]
            w2 = [PB("cw2%d" % i, [128, 2, 64], BF16) for i in range(2)]
            peT = [PB("peT%d" % i, [64, 32], BF16) for i in range(2)]
            hid = [PB("hid%d" % i, [128, 2, 512], BF16) for i in range(2)]
            bia = [PB("bia%d" % i, [128, 2], F32) for i in range(2)]
            ksq = PB("ksq", [64, 512], BF16)
            krt = PB("krt", [64, 512], F32)
            for i, (a1, a2, ap_) in enumerate([(ckw1_in, ckw2_in, pek_in), (cvw1_in, cvw2_in, pev_in)]):
                sc.dma("pool", "cw%d" % i, [
                    (w1[i][:, :, :], a1.rearrange("(l d) h -> d l h", d=64)),
                    (w2[i][:, :, :], a2.rearrange("(hc p) d -> p hc d", p=128)),
                    (peT[i][:, :], ap_[:, :])], writes=[("cw", i)])
                sc.op("dve", lambda e: e.memset(hid[i][:, :, 511:512], 0.0), writes=[("hid", i)])
            for i in range(2):
                src = KA if i == 0 else KB
                for hc in range(2):
                    sc.mm(ps[0][:, hc:hc + 1], [(w1[i][:, l, hc * 128:(hc + 1) * 128], peT[i][:, l:l + 1]) for l in range(32)],
                          reads=[("cw", i)], writes=[("ps", 0, hc)])
                    sc.op("act", lambda e: e.activation(out=bia[i][:, hc:hc + 1], in_=ps[0][:, hc:hc + 1], func=AF.Copy),
                          reads=[("ps", 0, hc)], writes=[("bia", i, hc)])
                    pb_ = 1 + hc
                    sc.mm(ps[pb_][:, 0:NCMP],
                          [(w1[i][:, l, hc * 128:(hc + 1) * 128], src[0:64, l:l + 16 * (NCMP - 1) + 1:16]) for l in range(32)],
                          reads=[("cw", i)], writes=[("ps", pb_)])
                    sc.op("act", lambda e: e.activation(out=hid[i][:, hc, 0:NCMP], in_=ps[pb_][:, 0:NCMP], func=AF.Silu,
                                                        bias=bia[i][:, hc:hc + 1], scale=1.0),
                          reads=[("ps", pb_), ("bia", i, hc), ("hid", i)], writes=[("hid", i)])
            sc.mm(ps[3][0:64, :], [(w2[0][:, hc, :], hid[0][:, hc, :]) for hc in range(2)],
                  reads=[("hid", 0), ("cw", 0)], writes=[("ps", 3)])
            sc.op("act", lambda e: e.activation(out=ksq[:, :], in_=ps[3][0:64, :], func=AF.Square), reads=[("ps", 3)], writes=["ksq"])
            sc.mm(ps[4][0:64, :], [(BD[0:64, 0:64], ksq[:, :])], reads=["ksq", "BD"], writes=[("ps", 4)])
            sc.op("act", lambda e: e.activation(out=krt[:, :], in_=ps[4][0:64, :], func=AF.Sqrt, bias=epsb[0:64, 0:1], scale=1.0),
                  reads=[("ps", 4), "eps"], writes=["krt"])
            sc.op("dve", lambda e: e.reciprocal(out=krt[:, :], in_=krt[:, :]), reads=["krt"], writes=["krt"])
            sc.op("dve", lambda e: e.scalar_tensor_tensor(out=KCMP[0:64, :], in0=ps[3][0:64, :], scalar=cv[0:64, 14:15], in1=krt[:, :],
                                                          op0=ALU.mult, op1=ALU.mult),
                  reads=[("ps", 3), "krt", "cv"], writes=["KCMP"])
            for cc in range(4):
                pb_ = 5 + cc % 2
                sc.mm(ps[pb_][:, 0:64], [(hid[1][:, hc, cc * 128:(cc + 1) * 128], w2[1][:, hc, :]) for hc in range(2)],
                      reads=[("hid", 1), ("cw", 1)], writes=[("ps", pb_)])
                sc.op("act", lambda e: e.activation(out=VC1[:, cc, 0:64], in_=ps[pb_][:, 0:64], func=AF.Copy),
                      reads=[("ps", pb_)], writes=[("VC1", cc)])
            sc.barrier()

        with ExitStack() as ph:
            PB = lambda name, shape, dt: ph.enter_context(nc.sbuf_tensor("sb_" + name, shape, dt))
            ident = PB("ident", [128, 128], BF16)
            tri = PB("tri", [128, 128], BF16)
            tric = PB("tric", [128, 128], BF16)
            cmask = PB("cmask", [128, 17, 128], BF16)
            cmap = PB("cmap", [128, 4, 128], BF16)
            ext = PB("ext", [128, S], BF16)
            t1 = PB("t1", [128, 256], F32)
            t2 = PB("t2", [128, 256], F32)
            NE = 8
            Eb = [PB("E%d" % i, [128, 512], BF16) for i in range(NE)]
            impb = PB("impb", [128, 128], F32)
            imp2 = PB("imp2", [128, 128], F32)
            imp3 = PB("imp3", [128, 128], F32)
            m8 = PB("m8", [128, 16], F32)
            selb = PB("selb", [128, 128], BF16)
            selT = PB("selT", [128, 128], BF16)
            mkd = PB("mkd", [128, 128], BF16)
            rz = PB("rz", [128, 3, 4], F32)
            cf = PB("cf", [128, 3, 4], F32)
            osb = [PB("osb%d" % i, [128, 4, 64], F32) for i in range(2)]
            otm = PB("otm", [128, 4, 64], F32)
            sc.dma("pool", "c3", [(ident[:, :], ident_in[:, :]), (tri[:, :], tri_in[:, :]), (tric[:, :], tric_in[:, :]),
                                  (cmask[:, :, :], cmask_in[:, :, :]), (cmap[:, :, :], cmap_in[:, :, :])], writes=["c3"])
            sc.dma("pool", "c3b", [(ext[:, 0:4096], ext_in[:, 0:4096]), (ext[:, 4096:S], ext_in[:, 4096:S])], writes=["ext"])
            sc.dma("sp", "c3c", [(t1[:, :], t1_in[:, :]), (t2[:, :], t2_in[:, :])], writes=["t12"])
            PS_S = (0, 1)
            PS_OC, PS_OS, PS_OW, PS_IM, PS_MK = 2, 3, 4, 5, 6
            ecnt = [0]
            scnt = [0]
            mkc = [0]
            MKB = (6, 5)

            SBATCH = 1
            pend = []

            def emitSE(u):
                lhsT, q_rhs, kreads, mask_fn, vrhs, vreads, po, first, last_ = u["a"]
                pS = PS_S[scnt[0] % 2]; scnt[0] += 1
                ei = ecnt[0] % NE; ecnt[0] += 1
                E = Eb[ei]
                u["E"] = E; u["ei"] = ei
                sc.mm(ps[pS][:, :], [(lhsT, q_rhs)], reads=kreads, writes=[("ps", pS)])
                sc.op("act", lambda e: e.activation(out=E[:, :], in_=ps[pS][:, :], func=AF.Exp, scale=0.125),
                      reads=[("ps", pS)], writes=[("E", ei)])
                if mask_fn is not None:
                    mask_fn(E, ei)

            def emitPV(u):
                lhsT, q_rhs, kreads, mask_fn, vrhs, vreads, po, first, last_ = u["a"]
                E, ei = u["E"], u["ei"]
                waits = sc._collect("pe", [("E", ei)] + vreads, [("ps", po)])
                sc._emit_waits("pe", waits)
                for r in range(4):
                    inst = nc.tensor.matmul(ps[po][:, r * 65:(r + 1) * 65], lhsT=E[:, r * 128:(r + 1) * 128], rhs=vrhs,
                                            start=(first and r == 0), stop=(last_ and r == 3), skip_group_check=True)
                sc.pe_inc(inst)
                sc._record(("pe", sc.cnt["pe"], "pe"), [("E", ei)] + vreads, [("ps", po)])

            mode = ["full"]
            sb = [0]

            def pe_mode(m):
                if mode[0] != m:
                    if sc.waited["pe"].get("pe", 0) < sc.cnt["pe"]:
                        sc._emit_waits("pe", {"pe": sc.cnt["pe"]})
                    mode[0] = m

            def push(*a):
                u = {"a": a}
                pe_mode("tile")
                emitSE(u)
                pend.append(u)
                sb[0] += 1
                if sb[0] >= SBATCH:
                    sb[0] = 0
                    if len(pend) > SBATCH:
                        pe_mode("full")
                        while len(pend) > SBATCH:
                            emitPV(pend.pop(0))
                return u

            def flush():
                sb[0] = 0
                if pend:
                    pe_mode("full")
                while pend:
                    emitPV(pend.pop(0))

            def bmask(mask_ap_fn, mreads):
                def f(E, ei):
                    sc.op("dve", lambda e: e.tensor_tensor(
                        out=E[:, :].rearrange("p (r t) -> p r t", r=4), in0=E[:, :].rearrange("p (r t) -> p r t", r=4),
                        in1=mask_ap_fn(), op=ALU.mult), reads=[("E", ei)] + mreads, writes=[("E", ei)])
                return f

            for qt in range(NQT):
                qtok = slice(qt * 128, (qt + 1) * 128)
                qn_rhs = Q[0:64, :, qtok]
                qr_rhs = Q[64:128, :, qtok]
                qreads = []
                ncn = min(4, (8 * qt + 7 + 127) // 128)
                Es = []
                for cc in range(ncn):
                    m = 8 * qt - 128 * cc
                    mf = None
                    if 0 <= m <= 128:
                        mi = m // 8
                        mf = bmask(lambda mi=mi: cmask[:, mi, :].unsqueeze(1).broadcast_to([128, 4, 128]), ["c3"])
                    u = push(KCMP[0:64, cc * 128:(cc + 1) * 128], qn_rhs, ["KCMP"] + qreads, mf,
                             VC1[:, cc, :], [("VC1", c_) for c_ in range(4)] + ["VC1ones"], PS_OC, cc == 0, cc == ncn - 1)
                    Es.append((u, cc))
                flush()
                pe_mode("full")
                nmm = len(Es) * 4
                i_ = 0
                for (u, cc) in Es:
                    E, ei = u["E"], u["ei"]
                    for r in range(4):
                        waits = sc._collect("pe", [("E", ei), "c3"], [("mkb", 1)])
                        sc._emit_waits("pe", waits)
                        inst = nc.tensor.matmul(ps[PS_IM][:, r * 128:(r + 1) * 128], lhsT=E[:, r * 128:(r + 1) * 128],
                                                rhs=cmap[:, cc, :], start=(i_ == 0), stop=(i_ == nmm - 1), skip_group_check=True)
                        i_ += 1
                        sc._record(("pe", sc.cnt["pe"] + 1, "pe"), [("E", ei), "c3"], [("mkb", 1)])
                sc.pe_inc(inst)
                sc.op("dve", lambda e: e.tensor_scalar_max(
                    out=rz[:, 0, :], in0=ps[PS_OC][:, 0:260].rearrange("p (r c) -> p r c", r=4)[:, :, 64], scalar1=1e-30),
                    reads=[("ps", PS_OC)], writes=[("rz", 0)])
                sc.op("dve", lambda e: e.reciprocal(out=rz[:, 0, :], in_=rz[:, 0, :]), reads=[("rz", 0)], writes=[("rz", 0)])
                for r in range(4):
                    if r == 0:
                        sc.op("dve", lambda e: e.tensor_scalar(out=impb[:, :], in0=ps[PS_IM][:, 0:128], scalar1=rz[:, 0, 0:1],
                                                               scalar2=None, op0=ALU.mult),
                              reads=[("mkb", 1), ("rz", 0)], writes=["impb"])
                    else:
                        sc.op("dve", lambda e: e.scalar_tensor_tensor(
                            out=impb[:, :], in0=ps[PS_IM][:, r * 128:(r + 1) * 128], scalar=rz[:, 0, r:r + 1], in1=impb[:, :],
                            op0=ALU.mult, op1=ALU.add), reads=[("mkb", 1), ("rz", 0), "impb"], writes=["impb"])
                off = 127 - 2 * qt
                sc.op("dve", lambda e: e.tensor_tensor(out=imp2[:, :], in0=impb[:, :], in1=t1[:, off:off + 128], op=ALU.mult),
                      reads=["impb", "t12"], writes=["imp2"])
                sc.op("dve", lambda e: e.tensor_tensor(out=imp2[:, :], in0=imp2[:, :], in1=t2[:, off:off + 128], op=ALU.add),
                      reads=["imp2", "t12"], writes=["imp2"])
                sc.op("dve", lambda e: e.memset(imp2[:, 0:1], 1e4), reads=["imp2"], writes=["imp2"])
                sc.op("dve", lambda e: e.max(out=m8[:, 0:8], in_=imp2[:, :]), reads=["imp2"], writes=["m8a"])
                sc.op("dve", lambda e: e.match_replace(out=imp3[:, :], in_to_replace=m8[:, 0:8], in_values=imp2[:, :], imm_value=-1e30),
                      reads=["imp2", "m8a"], writes=["imp3"])
                sc.op("dve", lambda e: e.max(out=m8[:, 8:16], in_=imp3[:, :]), reads=["imp3"], writes=["m8b"])
                sc.op("dve", lambda e: e.tensor_scalar(out=selb[:, :], in0=imp2[:, :], scalar1=m8[:, 15:16], scalar2=None, op0=ALU.is_ge),
                      reads=["imp2", "m8b"], writes=["selb"])
                k0 = max(0, qt - 4)
                for kt in range(k0, qt + 1):
                    ktok = slice(kt * 128, (kt + 1) * 128)
                    if kt == qt:
                        mf = bmask(lambda: tri[:, :].unsqueeze(1).broadcast_to([128, 4, 128]), ["c3"])
                    elif kt == qt - 4:
                        mf = bmask(lambda: tric[:, :].unsqueeze(1).broadcast_to([128, 4, 128]), ["c3"])
                    else:
                        mf = None
                    push(KB[64:128, ktok], qr_rhs, qreads, mf,
                         V1[:, kt, 1, :], [("V1", kt), "V1ones"], PS_OW, kt == k0, kt == qt)
                mode[0] = "x"
                pe_mode("full")
                waits = sc._collect("pe", ["selb", "c3"], ["psb"])
                sc._emit_waits("pe", waits)
                inst = nc.tensor.transpose(psb[:, 0:128], selb[:, :], ident[:, :])
                sc.pe_inc(inst)
                sc._record(("pe", sc.cnt["pe"], "pe"), ["selb", "c3"], ["psb"])
                sc.op("act", lambda e: e.activation(out=selT[:, :], in_=psb[:, 0:128], func=AF.Copy), reads=["psb"], writes=["selT"])
                for kt in range(qt + 1):
                    kb = kt % 4
                    if kb == 0:
                        nk = min(4, qt + 1 - kt)
                        pe_mode("full")
                        mkc[0] += 1
                        mb = MKB[mkc[0] % 2]
                        mkey = ("mkb", mkc[0] % 2)
                        for u_ in range(nk):
                            sc.mm(ps[mb][:, u_ * 128:(u_ + 1) * 128],
                                  [(ext[:, (kt + u_) * 128:(kt + u_ + 1) * 128], selT[:, :])],
                                  reads=["ext", "selT"], writes=[mkey])
                    ktok = slice(kt * 128, (kt + 1) * 128)
                    if kt == qt:
                        sc.op("dve", lambda e: e.tensor_tensor(out=mkd[:, :], in0=ps[mb][:, kb * 128:(kb + 1) * 128], in1=tri[:, :],
                                                               op=ALU.mult), reads=[mkey, "c3"], writes=["mkd"])
                        mf = bmask(lambda: mkd[:, :].unsqueeze(1).broadcast_to([128, 4, 128]), ["mkd"])
                    else:
                        mf = bmask(lambda kb=kb, mb=mb: ps[mb][:, kb * 128:(kb + 1) * 128].unsqueeze(1).broadcast_to([128, 4, 128]),
                                   [mkey])
                    push(KA[64:128, ktok], qr_rhs, qreads, mf,
                         V1[:, kt, 0, :], [("V1", kt), "V1ones"], PS_OS, kt == 0, kt == qt)
                flush()
                for bi, po in ((1, PS_OS), (2, PS_OW)):
                    sc.op("dve", lambda e: e.tensor_scalar_max(
                        out=rz[:, bi, :], in0=ps[po][:, 0:260].rearrange("p (r c) -> p r c", r=4)[:, :, 64], scalar1=1e-30),
                        reads=[("ps", po)], writes=[("rz", bi)])
                    sc.op("dve", lambda e: e.reciprocal(out=rz[:, bi, :], in_=rz[:, bi, :]), reads=[("rz", bi)], writes=[("rz", bi)])
                sc.op("dve", lambda e: e.tensor_tensor(out=cf[:, :, :], in0=rz[:, :, :],
                                                       in1=G[:, qt, :].rearrange("p (r b) -> p b r", b=3), op=ALU.mult),
                      reads=[("rz", 0), ("rz", 1), ("rz", 2), ("G", qt)], writes=["cf"])
                ob = osb[qt % 2]
                for bi, po in ((0, PS_OC), (1, PS_OS), (2, PS_OW)):
                    dst = ob if bi == 0 else otm
                    sc.op("dve", lambda e: e.tensor_tensor(
                        out=dst[:, :, :], in0=ps[po][:, 0:260].rearrange("p (r c) -> p r c", r=4)[:, :, 0:64],
                        in1=cf[:, bi, :].unsqueeze(2).broadcast_to([128, 4, 64]), op=ALU.mult),
                        reads=[("ps", po), "cf"], writes=[("osb", qt % 2) if bi == 0 else "otm"])
                    if bi > 0:
                        sc.op("dve", lambda e: e.tensor_tensor(out=ob[:, :, :], in0=ob[:, :, :], in1=otm[:, :, :], op=ALU.add),
                              reads=[("osb", qt % 2), "otm"], writes=[("osb", qt % 2)])
                sc.dma("sp", "oo%d" % (qt % 2), [(oout[qtok, :], ob[:, :, :].rearrange("p r c -> p (r c)"))],
                       reads=[("osb", qt % 2)])
            sc.barrier()
        sc.final_wait("sp")
        ok, stuck = sc.check_deadlock()
        if not ok:
            raise RuntimeError("deadlock in semaphore program: %r" % (stuck,))
    return nc


def nsa_host_inputs(P, j, li, xfull):
    consts = nsa_consts()
    w_in = np.asarray(P["nsa_w_in"][j], np.float32)
    offs = np.cumsum([0, 1024, 256, 256, 256, 256, 256, 256])
    oq, okc, ovc, oks, ovs, okw, ovw, ogl = [int(v) for v in offs]
    in_maps = []
    for c in range(NCORES):
        b, g = divmod(c, 4)
        cols = []
        for r in range(4):
            qc = w_in[:, oq + (g * 4 + r) * 64: oq + (g * 4 + r + 1) * 64]
            cols += [qc, qc]
        sl = lambda o: w_in[:, o + g * 64:o + (g + 1) * 64]
        cols += [sl(okc), sl(oks), sl(ovc), sl(okw)]
        wfm = np.ascontiguousarray(np.concatenate(cols, axis=1))
        wtm = np.ascontiguousarray(np.concatenate([sl(ovs), sl(ovw), w_in[:, ogl + g * 12: ogl + (g + 1) * 12]], axis=1))
        cvec = np.ones((128, 16), np.float32)
        cvec[:, 0:8] = fm(P["mix_norm_g"][li], 8)
        qg = np.asarray(P["nsa_q_norm"][j], np.float32)
        for r in range(4):
            cvec[0:64, 8 + r] = qg
            cvec[64:128, 8 + r] = qg
        cvec[64:128, 12] = np.asarray(P["nsa_ks_norm"][j], np.float32)
        cvec[64:128, 13] = np.asarray(P["nsa_kw_norm"][j], np.float32)
        cvec[0:64, 14] = np.asarray(P["nsa_kc_norm"][j], np.float32)
        m = {"xT": np.ascontiguousarray(xfull[b].T), "wfm": wfm, "wtm": wtm, "cvec": cvec,
             "pekT": np.ascontiguousarray(np.asarray(P["nsa_pe_k"][j], np.float32).T),
             "pevT": np.ascontiguousarray(np.asarray(P["nsa_pe_v"][j], np.float32).T),
             "ck_w1": np.ascontiguousarray(P["nsa_ck_w1"][j], np.float32), "ck_w2": np.ascontiguousarray(P["nsa_ck_w2"][j], np.float32),
             "cv_w1": np.ascontiguousarray(P["nsa_cv_w1"][j], np.float32), "cv_w2": np.ascontiguousarray(P["nsa_cv_w2"][j], np.float32)}
        m.update(consts)
        in_maps.append(m)
    return in_maps


def run_nsa(P, j, li, xfull):
    if "nsa" not in _NC_CACHE:
        _NC_CACHE["nsa"] = build_nsa()
    nc = _NC_CACHE["nsa"]
    in_maps = nsa_host_inputs(P, j, li, xfull)
    res = run_bass_kernel_spmd(nc, in_maps, core_ids=list(range(NCORES)))
    o = np.zeros((B, S, D), np.float32)
    for c in range(NCORES):
        b, g = divmod(c, 4)
        o[b, :, g * 256:(g + 1) * 256] = res.results[c]["o"]
    return o


def kernel(**inputs):
    P = {k: np.asarray(v) for k, v in inputs.items()}
    x = np.ascontiguousarray(P["x"], np.float32)
    x = run_dense([("conv", (0, 0)), ("ffn", 0)], P, x)
    o = run_nsa(P, 0, 1, x)
    x = run_dense([("oproj", 0), ("ffn", 1), ("conv", (2, 1)), ("ffn", 2)], P, x, ofull=o)
    o = run_nsa(P, 1, 3, x)
    x = run_dense([("oproj", 1), ("ffn", 3)], P, x, ofull=o)
    return x.astype(np.float32)
```

```python
import numpy as np
from contextlib import ExitStack
import concourse.bass as bass
import concourse.mybir as mybir
from concourse.bass_utils import run_bass_kernel_spmd

F32 = mybir.dt.float32
BF16 = mybir.dt.bfloat16
ALU = mybir.AluOpType
AF = mybir.ActivationFunctionType
AX = mybir.AxisListType

D = 1024
B = 2
S = 8192
DEPTH = 4
NCORES = 8
TOK = 2048
HALO = 52
TT = TOK + HALO
TW = 420
NT = TT // TW
DFF = 2816
NFC = DFF // 128
CW = 31
EPS = 1e-6
GSZ = 4
GROUPS = [(j, min(GSZ, NFC - j)) for j in range(0, NFC, GSZ)]

HD = 64
NKV = 4
NH = 16
PROJ = 2608


class Sched:
    def __init__(self, nc, es):
        self.nc = nc
        self.es = es
        self.E = {"pe": nc.tensor, "act": nc.scalar, "dve": nc.vector, "pool": nc.gpsimd, "sp": nc.sync}
        self.semh = {}
        self.cnt = {}
        for e in self.E:
            self.semh[e] = es.enter_context(nc.semaphore("s_" + e))
            self.cnt[e] = 0
        self.waited = {e: {} for e in self.E}
        self.res = {}
        self.pending_noinc = {e: False for e in self.E}
        self.log = {e: [] for e in self.E}

    def dsem(self, name):
        k = "d_" + name
        if k not in self.semh:
            self.semh[k] = self.es.enter_context(self.nc.semaphore(k))
            self.cnt[k] = 0
        return k

    def _collect(self, eng, reads, writes):
        waits = {}

        def need(dep, same_ok):
            semk, val, deng = dep
            if deng == eng and (not same_ok or eng == "pe"):
                return
            if self.waited[eng].get(semk, 0) >= val:
                return
            if waits.get(semk, 0) < val:
                waits[semk] = val

        for r in reads:
            st = self.res.get(r)
            if st is not None and st[0] is not None:
                need(st[0], True)
        for w in writes:
            st = self.res.get(w)
            if st is not None:
                if st[0] is not None:
                    need(st[0], True)
                for rd in st[1].values():
                    need(rd, False)
        return waits

    def _emit_waits(self, eng, waits):
        E = self.E[eng]
        for semk, val in waits.items():
            E.wait_ge(self.semh[semk], val)
            self.waited[eng][semk] = val
            self.log[eng].append(("w", semk, val))

    def _record(self, me, reads, writes):
        for r in reads:
            st = self.res.setdefault(r, [None, {}])
            st[1][me[2]] = me
        for w in writes:
            st = self.res.setdefault(w, [None, {}])
            st[0] = me
            st[1] = {}

    def op(self, eng, fn, reads=(), writes=(), inc=True):
        waits = self._collect(eng, reads, writes)
        self._emit_waits(eng, waits)
        inst = fn(self.E[eng])
        if inc:
            self.cnt[eng] += 1
            inst.then_inc(self.semh[eng], 1)
            self.log[eng].append(("i", eng, 1))
            me = (eng, self.cnt[eng], eng)
            self.pending_noinc[eng] = False
        else:
            me = (eng, self.cnt[eng] + 1, eng)
            self.pending_noinc[eng] = True
        self._record(me, reads, writes)
        return inst

    def pe_inc(self, inst):
        self.cnt["pe"] += 1
        inst.then_inc(self.semh["pe"], 1)
        self.log["pe"].append(("i", "pe", 1))

    def mm(self, out, pairs, reads, writes):
        waits = self._collect("pe", reads, writes)
        self._emit_waits("pe", waits)
        n = len(pairs)
        inst = None
        for i, (lhsT, rhs) in enumerate(pairs):
            inst = self.nc.tensor.matmul(out, lhsT=lhsT, rhs=rhs, start=(i == 0), stop=(i == n - 1))
        self.pe_inc(inst)
        me = ("pe", self.cnt["pe"], "pe")
        self._record(me, reads, writes)

    def mm_part(self, out, lhsT, rhs, start, stop, reads, writes, last):
        waits = self._collect("pe", reads, writes)
        self._emit_waits("pe", waits)
        inst = self.nc.tensor.matmul(out, lhsT=lhsT, rhs=rhs, start=start, stop=stop)
        if last:
            self.pe_inc(inst)
            me = ("pe", self.cnt["pe"], "pe")
        else:
            me = ("pe", self.cnt["pe"] + 1, "pe")
        self._record(me, reads, writes)

    def dma(self, q, dname, items, reads=(), writes=()):
        semk = self.dsem(dname)
        waits = self._collect(q, reads, writes)
        if self.cnt[semk] > 0 and self.waited[q].get(semk, 0) < self.cnt[semk]:
            waits[semk] = self.cnt[semk]
        self._emit_waits(q, waits)
        for (o, i) in items:
            inst = self.E[q].dma_start(out=o, in_=i)
            self.cnt[semk] += 16
            inst.then_inc(self.semh[semk], 16)
            self.log[q].append(("i", semk, 16))
        me = (semk, self.cnt[semk], "dma:" + semk)
        self._record(me, reads, writes)

    def check_deadlock(self):
        pos = {e: 0 for e in self.log}
        val = {}
        progress = True
        while progress:
            progress = False
            for e, lg in self.log.items():
                while pos[e] < len(lg):
                    kind, semk, v = lg[pos[e]]
                    if kind == "w":
                        if val.get(semk, 0) >= v:
                            pos[e] += 1; progress = True
                        else:
                            break
                    else:
                        val[semk] = val.get(semk, 0) + v
                        pos[e] += 1; progress = True
        stuck = {e: (pos[e], len(lg), lg[pos[e]] if pos[e] < len(lg) else None, val.get(lg[pos[e]][1]) if pos[e] < len(lg) else None)
                 for e, lg in self.log.items()}
        return all(pos[e] == len(lg) for e, lg in self.log.items()), stuck

    def barrier(self):
        comp = ["pe", "act", "dve", "pool", "sp"]
        for e in comp:
            waits = {}
            for k, v in self.cnt.items():
                if k == e or v == 0:
                    continue
                if self.waited[e].get(k, 0) < v:
                    waits[k] = v
            self._emit_waits(e, waits)
        self.res = {}

    def final_wait(self, eng="sp"):
        waits = {}
        for k, v in self.cnt.items():
            if k == eng or v == 0:
                continue
            if self.waited[eng].get(k, 0) < v:
                waits[k] = v
        self._emit_waits(eng, waits)


def fm(vec, nch):
    return np.ascontiguousarray(np.asarray(vec, np.float32).reshape(nch, 128).T)


class VecPack:
    def __init__(self):
        self.off = {}
        self.n = 0
        self.arrs = []

    def add(self, name, arr):
        arr = np.asarray(arr, np.float32)
        assert arr.shape[0] == 128
        arr = arr.reshape(128, -1)
        self.off[name] = (self.n, arr.shape[1])
        self.n += arr.shape[1]
        self.arrs.append(arr)

    def array(self):
        return np.ascontiguousarray(np.concatenate(self.arrs, axis=1))


def dense_vec_layout(phases):
    off = {}
    n = 0

    def add(name, w):
        nonlocal n
        off[name] = (n, w)
        n += w

    for pi, (kind, _) in enumerate(phases):
        p = "p%d_" % pi
        if kind == "conv":
            add(p + "g", 8); add(p + "b1", 16); add(p + "wdw", 8 * CW); add(p + "bdw", 8)
            add(p + "lng", 8); add(p + "lnb", 8); add(p + "b2", 8)
        elif kind == "ffn":
            add(p + "g", 8); add(p + "wdw", NFC * 3); add(p + "bdw", NFC)
        elif kind == "oproj":
            pass
    return off, n


def build_dense(phases):
    nc = bass.Bass("TRN2", target_bir_lowering=False)
    voff, nv = dense_vec_layout(phases)
    xin = nc.dram_tensor("xT", [D, TT], F32, kind="ExternalInput").ap()
    hm_in = nc.dram_tensor("hmask", [128, HALO], F32, kind="ExternalInput").ap()
    cv_in = nc.dram_tensor("cvec", [128, max(nv, 1)], F32, kind="ExternalInput").ap()
    yout = nc.dram_tensor("yT", [D, TOK], F32, kind="ExternalOutput").ap()
    wd = {}
    for pi, (kind, _) in enumerate(phases):
        p = "p%d_" % pi
        if kind == "conv":
            wd[p + "w1"] = nc.dram_tensor(p + "w1", [D, 2 * D], F32, kind="ExternalInput").ap()
            wd[p + "w2"] = nc.dram_tensor(p + "w2", [D, D], F32, kind="ExternalInput").ap()
        elif kind == "ffn":
            wd[p + "wup"] = nc.dram_tensor(p + "wup", [D, 2 * DFF], F32, kind="ExternalInput").ap()
            wd[p + "wdn"] = nc.dram_tensor(p + "wdn", [DFF, D], F32, kind="ExternalInput").ap()
        elif kind == "oproj":
            wd[p + "wo"] = nc.dram_tensor(p + "wo", [D, D], F32, kind="ExternalInput").ap()
            wd[p + "oT"] = nc.dram_tensor(p + "oT", [D, TT], F32, kind="ExternalInput").ap()

    with ExitStack() as es:
        sc = Sched(nc, es)
        xT = es.enter_context(nc.sbuf_tensor("xTs", [128, 8, TT], F32))
        cv = es.enter_context(nc.sbuf_tensor("cv", [128, max(nv, 1)], F32))
        hmk = es.enter_context(nc.sbuf_tensor("hmk", [128, HALO], F32))
        onesm = es.enter_context(nc.sbuf_tensor("onesm", [128, 128], BF16))
        epsb = es.enter_context(nc.sbuf_tensor("epsb", [128, 1], F32))
        ps = [es.enter_context(nc.psum_tensor("ps%d" % i, [128, 512], F32)) for i in range(8)]

        def V(name, a=0, b=None):
            o, w = voff[name]
            if b is None:
                b = w
            return cv[:, o + a:o + b]

        xin_v = xin.rearrange("(kc p) t -> p kc t", p=128)
        for t in range(NT):
            sc.dma("sp", "xin%d" % t, [(xT[:, :, t * TW:(t + 1) * TW], xin_v[:, :, t * TW:(t + 1) * TW])],
                   writes=[("x", t)])
        sc.dma("sp", "cvl", [(cv[:, :], cv_in[:, :]), (hmk[:, :], hm_in[:, :])], writes=["cv"])
        sc.op("dve", lambda e: e.memset(onesm[:, :], 1.0 / 1024.0), writes=["ones"])
        sc.op("dve", lambda e: e.memset(epsb[:, :], EPS), writes=["eps"])

        def rms_stats(x_ap, reads_x, sq, rs, k, psb):
            sc.op("act", lambda e: e.activation(out=sq[:, :, :], in_=x_ap, func=AF.Square),
                  reads=reads_x, writes=[("sq", k)])
            sc.mm(ps[psb][:, :TW], [(onesm[:, :], sq[:, kc, :]) for kc in range(8)],
                  reads=[("sq", k), "ones"], writes=[("ps", psb)])
            sc.op("act", lambda e: e.activation(out=rs[:, :], in_=ps[psb][:, :TW], func=AF.Sqrt,
                                                bias=epsb[:, 0:1], scale=1.0),
                  reads=[("ps", psb), "eps"], writes=[("rs", k)])
            sc.op("dve", lambda e: e.reciprocal(out=rs[:, :], in_=rs[:, :]), reads=[("rs", k)], writes=[("rs", k)])

        for pi, (kind, _) in enumerate(phases):
            p = "p%d_" % pi
            with ExitStack() as ph:
                if kind == "conv":
                    w1 = ph.enter_context(nc.sbuf_tensor(p + "w1s", [128, 8, 2 * D], BF16))
                    w2 = ph.enter_context(nc.sbuf_tensor(p + "w2s", [128, 8, D], BF16))
                    sqb = [ph.enter_context(nc.sbuf_tensor(p + "sq%d" % i, [128, 8, TW], BF16)) for i in range(2)]
                    rsb = [ph.enter_context(nc.sbuf_tensor(p + "rs%d" % i, [128, TW], F32)) for i in range(2)]
                    hb = [ph.enter_context(nc.sbuf_tensor(p + "hb%d" % i, [128, 8, TW], BF16)) for i in range(2)]
                    uG = ph.enter_context(nc.sbuf_tensor(p + "uG", [128, 8, TW + CW - 1], F32))
                    yb = ph.enter_context(nc.sbuf_tensor(p + "yb", [128, 8, TW], F32))
                    sgb = [ph.enter_context(nc.sbuf_tensor(p + "sg%d" % i, [128, TW], F32)) for i in range(2)]
                    ybf = [ph.enter_context(nc.sbuf_tensor(p + "ybf%d" % i, [128, TW], BF16)) for i in range(2)]
                    ysq = [ph.enter_context(nc.sbuf_tensor(p + "ysq%d" % i, [128, TW], BF16)) for i in range(2)]
                    zb = ph.enter_context(nc.sbuf_tensor(p + "zb", [128, 8, TW], BF16))
                    mu = ph.enter_context(nc.sbuf_tensor(p + "mu", [128, TW], F32))
                    var = ph.enter_context(nc.sbuf_tensor(p + "var", [128, TW], F32))
                    nb = ph.enter_context(nc.sbuf_tensor(p + "nb", [128, TW], F32))
                    ntm = [ph.enter_context(nc.sbuf_tensor(p + "ntm%d" % i, [128, TW], F32)) for i in range(2)]
                    w1v = wd[p + "w1"].rearrange("(kc p) n -> p kc n", p=128)
                    w2v = wd[p + "w2"].rearrange("(kc p) n -> p kc n", p=128)
                    sc.dma("pool", "w1a", [(w1[:, 0:4, :], w1v[:, 0:4, :])], writes=["w1a"])
                    sc.dma("pool", "w1b", [(w1[:, 4:8, :], w1v[:, 4:8, :])], writes=["w1b"])
                    sc.dma("pool", "w2", [(w2[:, :, :], w2v[:, :, :])], writes=["w2"])
                    sc.op("dve", lambda e: e.memset(uG[:, :, 0:CW - 1], 0.0), writes=[("uG", c) for c in range(8)])
                    PS_ST, PS_A, PS_G, PS_MU, PS_E2 = 0, (1, 2), (3, 4), 5, 6
                    for t in range(NT):
                        cols = slice(t * TW, (t + 1) * TW)
                        k = t % 2
                        rms_stats(xT[:, :, cols], [("x", t)], sqb[k], rsb[k], k, PS_ST)
                        for kc in range(8):
                            sc.op("dve", lambda e: e.scalar_tensor_tensor(
                                out=hb[k][:, kc, :], in0=xT[:, kc, cols], scalar=V(p + "g", kc, kc + 1),
                                in1=rsb[k][:, :], op0=ALU.mult, op1=ALU.mult),
                                reads=[("x", t), ("rs", k), "cv"], writes=[("hb", k)])
                        for c2 in range(0, 8, 2):
                            for c in (c2, c2 + 1):
                                pa, pg = PS_A[c % 2], PS_G[c % 2]
                                sc.mm(ps[pa][:, :TW], [(w1[:, kc, c * 128:(c + 1) * 128], hb[k][:, kc, :]) for kc in range(8)],
                                      reads=[("hb", k), "w1a", "w1b"], writes=[("ps", pa)])
                                sc.mm(ps[pg][:, :TW], [(w1[:, kc, D + c * 128:D + (c + 1) * 128], hb[k][:, kc, :]) for kc in range(8)],
                                      reads=[("hb", k), "w1a", "w1b"], writes=[("ps", pg)])
                                sc.op("act", lambda e: e.activation(out=sgb[c % 2][:, :], in_=ps[pg][:, :TW], func=AF.Sigmoid,
                                                                    bias=V(p + "b1", 8 + c, 9 + c), scale=1.0),
                                      reads=[("ps", pg), "cv"], writes=[("sg", c % 2)])
                                sc.op("dve", lambda e: e.scalar_tensor_tensor(
                                    out=uG[:, c, CW - 1:], in0=ps[pa][:, :TW], scalar=V(p + "b1", c, c + 1),
                                    in1=sgb[c % 2][:, :], op0=ALU.add, op1=ALU.mult),
                                    reads=[("ps", pa), ("sg", c % 2), "cv"], writes=[("uG", c)])
                                if t == 0:
                                    sc.op("dve", lambda e: e.tensor_tensor(
                                        out=uG[:, c, CW - 1:CW - 1 + HALO], in0=uG[:, c, CW - 1:CW - 1 + HALO],
                                        in1=hmk[:, :], op=ALU.mult), reads=[("uG", c), "cv"], writes=[("uG", c)])
                            for kk in range(CW):
                                for c in (c2, c2 + 1):
                                    wk = V(p + "wdw", c * CW + kk, c * CW + kk + 1)
                                    if kk == 0:
                                        sc.op("dve", lambda e: e.tensor_scalar(
                                            out=yb[:, c, :], in0=uG[:, c, 0:TW], scalar1=wk, scalar2=V(p + "bdw", c, c + 1),
                                            op0=ALU.mult, op1=ALU.add), reads=[("uG", c), "cv"], writes=[("y", c)])
                                    else:
                                        sc.op("dve", lambda e: e.scalar_tensor_tensor(
                                            out=yb[:, c, :], in0=uG[:, c, kk:kk + TW], scalar=wk, in1=yb[:, c, :],
                                            op0=ALU.mult, op1=ALU.add), reads=[("uG", c), ("y", c), "cv"], writes=[("y", c)])
                            for c in (c2, c2 + 1):
                                sc.op("act", lambda e: e.activation(out=ybf[c % 2][:, :], in_=yb[:, c, :], func=AF.Copy),
                                      reads=[("y", c)], writes=[("ybf", c % 2)])
                                sc.op("act", lambda e: e.activation(out=ysq[c % 2][:, :], in_=yb[:, c, :], func=AF.Square),
                                      reads=[("y", c)], writes=[("ysq", c % 2)])
                                sc.mm_part(ps[PS_MU][:, :TW], onesm[:, :], ybf[c % 2][:, :], c == 0, c == 7,
                                           reads=[("ybf", c % 2), "ones"], writes=[("ps", PS_MU)], last=True)
                                sc.mm_part(ps[PS_E2][:, :TW], onesm[:, :], ysq[c % 2][:, :], c == 0, c == 7,
                                           reads=[("ysq", c % 2), "ones"], writes=[("ps", PS_E2)], last=True)
                        sc.op("dve", lambda e: e.tensor_copy(out=uG[:, :, 0:CW - 1], in_=uG[:, :, TW:TW + CW - 1]),
                              reads=[("uG", c) for c in range(8)], writes=[("uG", c) for c in range(8)])
                        sc.op("act", lambda e: e.activation(out=mu[:, :], in_=ps[PS_MU][:, :TW], func=AF.Copy),
                              reads=[("ps", PS_MU)], writes=["mu"])
                        sc.op("dve", lambda e: e.tensor_tensor(out=var[:, :], in0=mu[:, :], in1=mu[:, :], op=ALU.mult),
                              reads=["mu"], writes=["var"])
                        sc.op("dve", lambda e: e.tensor_tensor(out=var[:, :], in0=ps[PS_E2][:, :TW], in1=var[:, :], op=ALU.subtract),
                              reads=[("ps", PS_E2), "var"], writes=["var"])
                        sc.op("dve", lambda e: e.tensor_scalar_max(out=var[:, :], in0=var[:, :], scalar1=0.0),
                              reads=["var"], writes=["var"])
                        sc.op("act", lambda e: e.activation(out=var[:, :], in_=var[:, :], func=AF.Sqrt, bias=epsb[:, 0:1], scale=1.0),
                              reads=["var", "eps"], writes=["var"])
                        sc.op("dve", lambda e: e.reciprocal(out=var[:, :], in_=var[:, :]), reads=["var"], writes=["var"])
                        sc.op("dve", lambda e: e.scalar_tensor_tensor(out=nb[:, :], in0=mu[:, :], scalar=-1.0, in1=var[:, :],
                                                                      op0=ALU.mult, op1=ALU.mult),
                              reads=["mu", "var"], writes=["nb"])
                        for c in range(8):
                            nt_ = ntm[c % 2]
                            sc.op("dve", lambda e: e.tensor_tensor(out=nt_[:, :], in0=yb[:, c, :], in1=var[:, :], op=ALU.mult),
                                  reads=[("y", c), "var"], writes=[("ntm", c % 2)])
                            sc.op("pool", lambda e: e.tensor_tensor(out=nt_[:, :], in0=nt_[:, :], in1=nb[:, :], op=ALU.add),
                                  reads=[("ntm", c % 2), "nb"], writes=[("ntm", c % 2)])
                            sc.op("act", lambda e: e.activation(out=zb[:, c, :], in_=nt_[:, :], func=AF.Silu,
                                                                bias=V(p + "lnb", c, c + 1), scale=V(p + "lng", c, c + 1)),
                                  reads=[("ntm", c % 2), "cv"], writes=[("z", c)])
                        for m in range(8):
                            po = PS_A[m % 2]
                            sc.mm(ps[po][:, :TW], [(w2[:, c, m * 128:(m + 1) * 128], zb[:, c, :]) for c in range(8)],
                                  reads=[("z", c) for c in range(8)] + ["w2"], writes=[("ps", po)])
                            sc.op("dve", lambda e: e.scalar_tensor_tensor(
                                out=xT[:, m, cols], in0=ps[po][:, :TW], scalar=V(p + "b2", m, m + 1), in1=xT[:, m, cols],
                                op0=ALU.add, op1=ALU.add), reads=[("ps", po), ("x", t), "cv"], writes=[("x", t)])

                elif kind == "oproj":
                    wo = ph.enter_context(nc.sbuf_tensor(p + "wos", [128, 8, D], BF16))
                    oT = ph.enter_context(nc.sbuf_tensor(p + "oTs", [128, 8, TT], BF16))
                    wov = wd[p + "wo"].rearrange("(kc p) n -> p kc n", p=128)
                    oTv = wd[p + "oT"].rearrange("(kc p) t -> p kc t", p=128)
                    sc.dma("pool", "wo", [(wo[:, :, :], wov[:, :, :])], writes=["wo"])
                    for t in range(NT):
                        cols = slice(t * TW, (t + 1) * TW)
                        sc.dma("pool", "oT%d" % t, [(oT[:, :, cols], oTv[:, :, cols])], writes=[("oT", t)])
                    for t in range(NT):
                        cols = slice(t * TW, (t + 1) * TW)
                        for m in range(8):
                            po = 1 + (m % 2)
                            sc.mm(ps[po][:, :TW], [(wo[:, kc, m * 128:(m + 1) * 128], oT[:, kc, cols]) for kc in range(8)],
                                  reads=[("oT", t), "wo"], writes=[("ps", po)])
                            sc.op("dve", lambda e: e.tensor_tensor(out=xT[:, m, cols], in0=ps[po][:, :TW], in1=xT[:, m, cols],
                                                                   op=ALU.add),
                                  reads=[("ps", po), ("x", t)], writes=[("x", t)])

                elif kind == "ffn":
                    hall = ph.enter_context(nc.sbuf_tensor(p + "hall", [128, 8, TT + 2], BF16))
                    wa = [ph.enter_context(nc.sbuf_tensor(p + "wa%d" % i, [128, 8, GSZ * 128], BF16)) for i in range(2)]
                    wv = [ph.enter_context(nc.sbuf_tensor(p + "wv%d" % i, [128, 8, GSZ * 128], BF16)) for i in range(2)]
                    wdn = [ph.enter_context(nc.sbuf_tensor(p + "wdn%d" % i, [128, GSZ, D], BF16)) for i in range(2)]
                    sqb = [ph.enter_context(nc.sbuf_tensor(p + "sq%d" % i, [128, 8, TW], BF16)) for i in range(2)]
                    rsb = [ph.enter_context(nc.sbuf_tensor(p + "rs%d" % i, [128, TW], F32)) for i in range(2)]
                    acc = [ph.enter_context(nc.sbuf_tensor(p + "acc%d" % i, [128, TW], F32)) for i in range(2)]
                    sil = [ph.enter_context(nc.sbuf_tensor(p + "sil%d" % i, [128, TW], F32)) for i in range(2)]
                    gb = [ph.enter_context(nc.sbuf_tensor(p + "gb%d" % i, [128, GSZ, TW], BF16)) for i in range(2)]
                    wupv = wd[p + "wup"].rearrange("(kc p) n -> p kc n", p=128)
                    wdnv = wd[p + "wdn"].rearrange("(jc p) n -> p jc n", p=128)

                    def load_group(gi):
                        j0, G = GROUPS[gi]
                        s = gi % 2
                        sc.dma("pool", "wg%d" % s, [
                            (wa[s][:, :, 0:G * 128], wupv[:, :, j0 * 128:(j0 + G) * 128]),
                            (wv[s][:, :, 0:G * 128], wupv[:, :, DFF + j0 * 128:DFF + (j0 + G) * 128]),
                            (wdn[s][:, 0:G, :], wdnv[:, j0:j0 + G, :]),
                        ], writes=[("wg", s)])

                    load_group(0)
                    sc.op("dve", lambda e: e.memset(hall[:, :, 0:2], 0.0), writes=[("h", -1)])
                    for t in range(NT):
                        cols = slice(t * TW, (t + 1) * TW)
                        k = t % 2
                        rms_stats(xT[:, :, cols], [("x", t)], sqb[k], rsb[k], k, 0)
                        for kc in range(8):
                            sc.op("dve", lambda e: e.scalar_tensor_tensor(
                                out=hall[:, kc, 2 + t * TW:2 + (t + 1) * TW], in0=xT[:, kc, cols], scalar=V(p + "g", kc, kc + 1),
                                in1=rsb[k][:, :], op0=ALU.mult, op1=ALU.mult),
                                reads=[("x", t), ("rs", k), "cv"], writes=[("h", t)])
                        if t == 0:
                            sc.op("dve", lambda e: e.tensor_tensor(
                                out=hall[:, :, 2:2 + HALO], in0=hall[:, :, 2:2 + HALO],
                                in1=hmk[:, :].unsqueeze(1).broadcast_to([128, 8, HALO]), op=ALU.mult),
                                reads=[("h", 0), "cv"], writes=[("h", 0)])
                    PA, PV, PD = (1, 2), (3, 4), (5, 6)
                    it = 0
                    pend = None

                    def emit_down(gi, t, gk):
                        j0, G = GROUPS[gi]
                        s = gi % 2
                        cols = slice(t * TW, (t + 1) * TW)
                        for m in range(8):
                            pd = PD[m % 2]
                            sc.mm(ps[pd][:, :TW], [(wdn[s][:, jj, m * 128:(m + 1) * 128], gb[gk][:, jj, :]) for jj in range(G)],
                                  reads=[("g", gk), ("wg", s)], writes=[("ps", pd)])
                            sc.op("dve", lambda e: e.tensor_tensor(out=xT[:, m, cols], in0=ps[pd][:, :TW], in1=xT[:, m, cols],
                                                                   op=ALU.add),
                                  reads=[("ps", pd), ("x", t)], writes=[("x", t)])

                    for gi, (j0, G) in enumerate(GROUPS):
                        s = gi % 2
                        for t in range(NT):
                            gk = it % 2
                            hreads = [("h", t), ("h", t - 1)]
                            for jj in range(G):
                                j = j0 + jj
                                pa, pv = PA[jj % 2], PV[jj % 2]
                                sc.mm(ps[pa][:, :TW + 2], [(wa[s][:, kc, jj * 128:(jj + 1) * 128], hall[:, kc, t * TW:t * TW + TW + 2])
                                                            for kc in range(8)],
                                      reads=hreads + [("wg", s)], writes=[("ps", pa)])
                                sc.mm(ps[pv][:, :TW], [(wv[s][:, kc, jj * 128:(jj + 1) * 128], hall[:, kc, 2 + t * TW:2 + (t + 1) * TW])
                                                       for kc in range(8)],
                                      reads=hreads + [("wg", s)], writes=[("ps", pv)])
                                a_ = acc[jj % 2]
                                sc.op("dve", lambda e: e.tensor_scalar(
                                    out=a_[:, :], in0=ps[pa][:, 2:TW + 2], scalar1=V(p + "wdw", j * 3 + 2, j * 3 + 3),
                                    scalar2=V(p + "bdw", j, j + 1), op0=ALU.mult, op1=ALU.add),
                                    reads=[("ps", pa), "cv"], writes=[("acc", jj % 2)])
                                sc.op("dve", lambda e: e.scalar_tensor_tensor(
                                    out=a_[:, :], in0=ps[pa][:, 1:TW + 1], scalar=V(p + "wdw", j * 3 + 1, j * 3 + 2), in1=a_[:, :],
                                    op0=ALU.mult, op1=ALU.add), reads=[("ps", pa), ("acc", jj % 2), "cv"], writes=[("acc", jj % 2)])
                                sc.op("dve", lambda e: e.scalar_tensor_tensor(
                                    out=a_[:, :], in0=ps[pa][:, 0:TW], scalar=V(p + "wdw", j * 3, j * 3 + 1), in1=a_[:, :],
                                    op0=ALU.mult, op1=ALU.add), reads=[("ps", pa), ("acc", jj % 2), "cv"], writes=[("acc", jj % 2)])
                                sc.op("act", lambda e: e.activation(out=sil[jj % 2][:, :], in_=a_[:, :], func=AF.Silu),
                                      reads=[("acc", jj % 2)], writes=[("sil", jj % 2)])
                                sc.op("dve", lambda e: e.tensor_tensor(out=gb[gk][:, jj, :], in0=ps[pv][:, :TW], in1=sil[jj % 2][:, :],
                                                                       op=ALU.mult),
                                      reads=[("ps", pv), ("sil", jj % 2)], writes=[("g", gk)])
                            if pend is not None:
                                emit_down(*pend)
                            pend = (gi, t, gk)
                            it += 1
                            if t == 0 and gi + 1 < len(GROUPS):
                                if pend is not None and pend[0] != gi:
                                    pass
                                load_group(gi + 1)
                    emit_down(*pend)
                sc.barrier()

        yv = yout.rearrange("(kc p) t -> p kc t", p=128)
        for t in range(NT):
            lo = max(t * TW, HALO)
            hi = (t + 1) * TW
            sc.dma("sp", "yout%d" % t, [(yv[:, :, lo - HALO:hi - HALO], xT[:, :, lo:hi])], reads=[("x", t)])
        sc.final_wait("sp")
    return nc


def shard_xT(xfull):
    outs = []
    for c in range(NCORES):
        b, q = divmod(c, 4)
        t0 = q * TOK
        buf = np.zeros((TT, xfull.shape[2]), np.float32)
        lo = t0 - HALO
        if lo < 0:
            buf[-lo:] = xfull[b, 0:t0 + TOK]
        else:
            buf[:] = xfull[b, lo:t0 + TOK]
        outs.append(np.ascontiguousarray(buf.T))
    return outs


def unshard_yT(ys):
    out = np.zeros((B, S, D), np.float32)
    for c in range(NCORES):
        b, q = divmod(c, 4)
        out[b, q * TOK:(q + 1) * TOK] = ys[c].T
    return out


def dense_host_inputs(phases, P, xfull, ofull=None):
    vp = VecPack()
    wmaps = {}
    for pi, (kind, idx) in enumerate(phases):
        p = "p%d_" % pi
        if kind == "conv":
            li, j = idx
            vp.add(p + "g", fm(P["mix_norm_g"][li], 8))
            vp.add(p + "b1", fm(P["conv_b_pw1"][j], 16))
            wdw = np.asarray(P["conv_w_dw"][j], np.float32)
            vp.add(p + "wdw", np.ascontiguousarray(wdw.reshape(CW, 8, 128).transpose(2, 1, 0)).reshape(128, 8 * CW))
            vp.add(p + "bdw", fm(P["conv_b_dw"][j], 8))
            vp.add(p + "lng", fm(P["conv_ln_g"][j], 8))
            vp.add(p + "lnb", fm(P["conv_ln_b"][j], 8))
            vp.add(p + "b2", fm(P["conv_b_pw2"][j], 8))
            wmaps[p + "w1"] = np.ascontiguousarray(P["conv_w_pw1"][j], np.float32)
            wmaps[p + "w2"] = np.ascontiguousarray(P["conv_w_pw2"][j], np.float32)
        elif kind == "ffn":
            li = idx
            vp.add(p + "g", fm(P["ffn_norm_g"][li], 8))
            wdw = np.asarray(P["ffn_w_dw"][li], np.float32)
            vp.add(p + "wdw", np.ascontiguousarray(wdw.reshape(3, NFC, 128).transpose(2, 1, 0)).reshape(128, NFC * 3))
            vp.add(p + "bdw", fm(P["ffn_b_dw"][li], NFC))
            wmaps[p + "wup"] = np.ascontiguousarray(P["ffn_w_up"][li], np.float32)
            wmaps[p + "wdn"] = np.ascontiguousarray(P["ffn_w_down"][li], np.float32)
        elif kind == "oproj":
            j = idx
            wmaps[p + "wo"] = np.ascontiguousarray(P["nsa_w_out"][j], np.float32)
    voff, nv = dense_vec_layout(phases)
    assert voff == vp.off, (voff, vp.off)
    cvec = vp.array() if vp.n > 0 else np.zeros((128, 1), np.float32)
    xs = shard_xT(xfull)
    oTs = shard_xT(ofull) if ofull is not None else None
    in_maps = []
    for c in range(NCORES):
        q = c % 4
        m = {"xT": xs[c], "cvec": cvec,
             "hmask": np.full((128, HALO), 0.0 if q == 0 else 1.0, np.float32)}
        m.update(wmaps)
        for pi, (kind, idx) in enumerate(phases):
            if kind == "oproj":
                m["p%d_oT" % pi] = oTs[c]
        in_maps.append(m)
    return in_maps


_NC_CACHE = {}


def run_dense(phases, P, xfull, ofull=None):
    key = ("dense", tuple(k for k, _ in phases))
    if key not in _NC_CACHE:
        _NC_CACHE[key] = build_dense(phases)
    nc = _NC_CACHE[key]
    in_maps = dense_host_inputs(phases, P, xfull, ofull)
    res = run_bass_kernel_spmd(nc, in_maps, core_ids=list(range(NCORES)))
    return unshard_yT([r["yT"] for r in res.results])


NTK = 512
NTT = S // NTK
NQT = S // 128
NCMP = 511
ROT = 16


def nsa_consts():
    c = {}
    half = ROT // 2
    inv_freq = (np.float32(500000.0) ** (-np.arange(half, dtype=np.float32) * np.float32(2.0 / ROT))).astype(np.float32)
    ang = (np.arange(S, dtype=np.float32)[None, :] * inv_freq[:, None]).astype(np.float32)
    cos = np.cos(ang).astype(np.float32)
    sin = np.sin(ang).astype(np.float32)
    c["ropeC"] = np.ascontiguousarray(np.concatenate([cos, cos], 0))
    c["ropeS"] = np.ascontiguousarray(np.concatenate([sin, sin], 0))
    prot = np.zeros((128, 128), np.float32)
    for i in range(half):
        prot[64 + i + half, 64 + i] = -1.0
        prot[64 + i, 64 + i + half] = 1.0
    c["prot"] = prot
    c["ident"] = np.eye(128, dtype=np.float32)
    p = np.arange(128)[:, None]
    t = np.arange(128)[None, :]
    c["tri"] = (p <= t).astype(np.float32)
    c["tric"] = (p > t).astype(np.float32)
    cm = np.zeros((17, 128, 128), np.float32)
    for mi in range(17):
        m = 8 * mi
        cm[mi] = (16 * (p - m) + 31 <= t).astype(np.float32)
    c["cmask"] = np.ascontiguousarray(cm.transpose(1, 0, 2))
    n = np.arange(512)[:, None] * 16
    s_ = np.arange(128)[None, :] * 64
    ov = np.minimum(n + 32, s_ + 64) - np.maximum(n, s_)
    cmap = np.maximum(ov, 0).astype(np.float32) / 32.0
    cmap[511] = 0.0
    c["cmap"] = np.ascontiguousarray(cmap.reshape(4, 128, 128).transpose(1, 0, 2))
    ext = np.zeros((128, S), np.float32)
    ext[np.arange(S) // 64, np.arange(S)] = 1.0
    c["ext"] = ext
    d = np.arange(-127, 129)[None, :]
    tt = np.arange(128)[:, None]
    lo = (tt < 64)
    t1 = np.where(lo, (d <= -2), (d <= -1)).astype(np.float32)
    t2 = np.where(lo,
                  np.where((d == -1) | (d == 0), 1e4, np.where(d > 0, -1.0, 0.0)),
                  np.where((d == 0) | (d == 1), 1e4, np.where(d > 1, -1.0, 0.0))).astype(np.float32)
    c["t1"] = np.ascontiguousarray(t1 * np.ones((128, 1), np.float32))
    c["t2"] = np.ascontiguousarray(t2)
    return c


def build_nsa():
    nc = bass.Bass("TRN2", target_bir_lowering=False)

    def din(name, shape):
        return nc.dram_tensor(name, shape, F32, kind="ExternalInput").ap()

    xin = din("xT", [D, S])
    wfm_in = din("wfm", [D, 768])
    wtm_in = din("wtm", [D, 140])
    cv_in = din("cvec", [128, 8 + 6 + 2])
    pek_in = din("pekT", [64, 32]); pev_in = din("pevT", [64, 32])
    ckw1_in = din("ck_w1", [2048, 256]); ckw2_in = din("ck_w2", [256, 64])
    cvw1_in = din("cv_w1", [2048, 256]); cvw2_in = din("cv_w2", [256, 64])
    ropeC_in = din("ropeC", [16, S]); ropeS_in = din("ropeS", [16, S])
    prot_in = din("prot", [128, 128]); ident_in = din("ident", [128, 128])
    tri_in = din("tri", [128, 128]); tric_in = din("tric", [128, 128])
    cmask_in = din("cmask", [128, 17, 128]); cmap_in = din("cmap", [128, 4, 128])
    ext_in = din("ext", [128, S]); t1_in = din("t1", [128, 256]); t2_in = din("t2", [128, 256])
    oout = nc.dram_tensor("o", [S, 256], F32, kind="ExternalOutput").ap()

    with ExitStack() as es:
        sc = Sched(nc, es)
        SB = lambda name, shape, dt: es.enter_context(nc.sbuf_tensor("sb_" + name, shape, dt))
        Q = SB("Q", [128, 4, S], BF16)
        KA = SB("KA", [128, S], BF16)
        KB = SB("KB", [128, S], BF16)
        V1 = SB("V1", [128, NQT, 2, 65], BF16)
        G = SB("G", [128, NQT, 12], F32)
        KCMP = SB("KCMP", [128, 512], BF16)
        VC1 = SB("VC1", [128, 4, 65], BF16)
        cv = SB("cv", [128, 16], F32)
        onesm = SB("onesm", [128, 128], BF16)
        BD = SB("BD", [128, 128], BF16)
        epsb = SB("epsb", [128, 1], F32)
        prot = SB("prot", [128, 128], F32)
        ps = [es.enter_context(nc.psum_tensor("ps%d" % i, [128, 512], F32)) for i in range(7)]
        psb = es.enter_context(nc.psum_tensor("psb", [128, 1024], BF16))

        sc.dma("sp", "c0", [(cv[:, :], cv_in[:, :]), (prot[:, :], prot_in[:, :])], writes=["cv", "prot"])
        sc.op("dve", lambda e: e.memset(onesm[:, :], 1.0 / 1024.0), writes=["ones"])
        sc.op("dve", lambda e: e.memset(epsb[:, :], EPS), writes=["eps"])
        sc.op("dve", lambda e: e.memset(BD[:, :], 0.0), writes=["BD"])
        sc.op("dve", lambda e: e.memset(BD[0:64, 0:64], 1.0 / 64.0), reads=["BD"], writes=["BD"])
        sc.op("dve", lambda e: e.memset(BD[64:128, 64:128], 1.0 / 64.0), reads=["BD"], writes=["BD"])
        sc.op("dve", lambda e: e.memset(V1[:, :, :, 64:65], 1.0), writes=["V1ones"])
        sc.op("dve", lambda e: e.memset(VC1[:, :, 64:65], 1.0), writes=["VC1ones"])

        with ExitStack() as ph:
            PB = lambda name, shape, dt: ph.enter_context(nc.sbuf_tensor("sb_" + name, shape, dt))
            wfm = PB("wfm", [128, 8, 768], BF16)
            wtm = PB("wtm", [128, 8, 140], BF16)
            xt_ = PB("xt_", [128, 8, NTK], F32)
            xt = [xt_, xt_]
            sq1_ = PB("sq_", [128, 8, NTK], BF16)
            sq = [sq1_, sq1_]
            rs = [PB("rs%d" % i, [128, NTK], F32) for i in range(2)]
            hb = [PB("hb%d" % i, [128, 8, NTK], BF16) for i in range(2)]
            csC = [PB("csC%d" % i, [128, NTK], F32) for i in range(2)]
            csS = [PB("csS%d" % i, [128, NTK], F32) for i in range(2)]
            sq2 = [PB("sq2%d" % i, [128, NTK], BF16) for i in range(2)]
            rt = [PB("rt%d" % i, [128, NTK], F32) for i in range(2)]
            qn = [PB("qn%d" % i, [128, NTK], F32) for i in range(2)]
            r1_ = PB("r1_", [128, NTK], F32)
            r2_ = PB("r2_", [128, NTK], F32)
            r1 = [r1_, r1_]
            r2 = [r2_, r2_]
            xv = xin.rearrange("(kc p) t -> p kc t", p=128)
            sc.dma("pool", "wfm", [(wfm[:, :, :], wfm_in.rearrange("(kc p) n -> p kc n", p=128)),
                                   (wtm[:, :, :], wtm_in.rearrange("(kc p) n -> p kc n", p=128))], writes=["wfm"])

            def load_x(tt):
                k = tt % 2
                tok = slice(tt * NTK, (tt + 1) * NTK)
                sc.dma("sp", "xt0", [(xt[k][:, :, :], xv[:, :, tok])], writes=[("xt", 0)])
                sc.dma("sp", "cs%d" % k, [(csC[k][64:80, :], ropeC_in[:, tok]), (csS[k][64:80, :], ropeS_in[:, tok])],
                       writes=[("cs", k)])

            load_x(0)
            cbi = 0
            for tt in range(NTT):
                k = tt % 2
                tok = slice(tt * NTK, (tt + 1) * NTK)
                sc.op("act", lambda e: e.activation(out=sq[k][:, :, :], in_=xt[k][:, :, :], func=AF.Square),
                      reads=[("xt", 0)], writes=[("sq", 0)])
                sc.mm(ps[0][:, :], [(onesm[:, :], sq[k][:, kc, :]) for kc in range(8)],
                      reads=[("sq", 0), "ones"], writes=[("ps", 0)])
                sc.op("act", lambda e: e.activation(out=rs[k][:, :], in_=ps[0][:, :], func=AF.Sqrt, bias=epsb[:, 0:1], scale=1.0),
                      reads=[("ps", 0), "eps"], writes=[("rs", k)])
                sc.op("dve", lambda e: e.reciprocal(out=rs[k][:, :], in_=rs[k][:, :]), reads=[("rs", k)], writes=[("rs", k)])
                for kc in range(8):
                    sc.op("dve", lambda e: e.scalar_tensor_tensor(
                        out=hb[k][:, kc, :], in0=xt[k][:, kc, :], scalar=cv[:, kc:kc + 1], in1=rs[k][:, :],
                        op0=ALU.mult, op1=ALU.mult), reads=[("xt", 0), ("rs", k), "cv"], writes=[("hb", k)])
                if tt + 1 < NTT:
                    load_x(tt + 1)
                for sub in range(4):
                    kt = tt * 4 + sub
                    pt = 1 + sub % 2
                    sc.mm(ps[pt][:, 0:140], [(hb[k][:, kc, sub * 128:(sub + 1) * 128], wtm[:, kc, :]) for kc in range(8)],
                          reads=[("hb", k), "wfm"], writes=[("ps", pt)])
                    sc.op("act", lambda e: e.activation(out=V1[:, kt, :, 0:64],
                                                        in_=ps[pt][:, 0:128].rearrange("p (a b) -> p a b", a=2), func=AF.Copy),
                          reads=[("ps", pt)], writes=[("V1", kt)])
                    sc.op("act", lambda e: e.activation(out=G[:, kt, :], in_=ps[pt][:, 128:140], func=AF.Sigmoid),
                          reads=[("ps", pt)], writes=[("G", kt)])
                for cb in range(6):
                    j = cbi % 2
                    cbi += 1
                    P1, P2 = 3 + j, 5 + j
                    sc.mm(ps[P1][:, :], [(wfm[:, kc, cb * 128:(cb + 1) * 128], hb[k][:, kc, :]) for kc in range(8)],
                          reads=[("hb", k), "wfm"], writes=[("ps", P1)])
                    sc.op("act", lambda e: e.activation(out=sq2[j][:, :], in_=ps[P1][:, :], func=AF.Square),
                          reads=[("ps", P1)], writes=[("sq2", j)])
                    sc.mm(ps[P2][:, :], [(BD[:, :], sq2[j][:, :])], reads=[("sq2", j), "BD"], writes=[("ps", P2)])
                    sc.op("act", lambda e: e.activation(out=rt[j][:, :], in_=ps[P2][:, :], func=AF.Sqrt, bias=epsb[:, 0:1], scale=1.0),
                          reads=[("ps", P2), "eps"], writes=[("rt", j)])
                    sc.op("dve", lambda e: e.reciprocal(out=rt[j][:, :], in_=rt[j][:, :]), reads=[("rt", j)], writes=[("rt", j)])
                    gcol = cv[:, 8 + cb:9 + cb]
                    if cb < 4:
                        sc.op("dve", lambda e: e.scalar_tensor_tensor(
                            out=qn[j][:, :], in0=ps[P1][:, :], scalar=gcol, in1=rt[j][:, :], op0=ALU.mult, op1=ALU.mult),
                            reads=[("ps", P1), ("rt", j), "cv"], writes=[("qn", j)])
                    else:
                        dst = KA if cb == 4 else KB
                        sc.op("act", lambda e: e.activation(out=dst[0:64, tok], in_=ps[P1][0:64, :], func=AF.Copy),
                              reads=[("ps", P1)], writes=[("KAB", cb, tt, 0)])
                        sc.op("dve", lambda e: e.memset(qn[j][0:64, :], 0.0), writes=[("qn", j)], reads=[("qn", j)])
                        sc.op("dve", lambda e: e.scalar_tensor_tensor(
                            out=qn[j][64:128, :], in0=ps[P1][64:128, :], scalar=gcol[64:128, :], in1=rt[j][64:128, :],
                            op0=ALU.mult, op1=ALU.mult),
                            reads=[("ps", P1), ("rt", j), "cv", ("qn", j)], writes=[("qn", j)])
                    PR = 0
                    sc.mm(ps[PR][:, :], [(prot[:, :], qn[j][:, :])], reads=[("qn", j), "prot"], writes=[("ps", PR)])
                    sc.op("dve", lambda e: e.tensor_tensor(out=r1[j][64:80, :], in0=qn[j][64:80, :], in1=csC[k][64:80, :], op=ALU.mult),
                          reads=[("qn", j), ("cs", k)], writes=[("r1", 0)])
                    sc.op("dve", lambda e: e.tensor_tensor(out=r2[j][64:80, :], in0=ps[PR][64:80, :], in1=csS[k][64:80, :], op=ALU.mult),
                          reads=[("ps", PR), ("cs", k)], writes=[("r2", 0)])
                    sc.op("dve", lambda e: e.tensor_tensor(out=qn[j][64:80, :], in0=r1[j][64:80, :], in1=r2[j][64:80, :], op=ALU.add),
                          reads=[("r1", 0), ("r2", 0), ("qn", j), ("ps", PR)], writes=[("qn", j)])
                    if cb < 4:
                        sc.op("act", lambda e: e.activation(out=Q[0:64, cb, tok], in_=qn[j][0:64, :], func=AF.Copy),
                              reads=[("qn", j)], writes=[("Q", cb, tt, 0)])
                        sc.op("pool", lambda e: e.tensor_copy(out=Q[64:128, cb, tok], in_=qn[j][64:128, :]),
                              reads=[("qn", j)], writes=[("Q", cb, tt, 1)])
                    else:
                        dst = KA if cb == 4 else KB
                        sc.op("pool", lambda e: e.tensor_copy(out=dst[64:128, tok], in_=qn[j][64:128, :]),
                              reads=[("qn", j)], writes=[("KAB", cb, tt, 1)])
            sc.barrier()

        with ExitStack() as ph:
            PB = lambda name, shape, dt: ph.enter_context(nc.sbuf_tensor("sb_" + name, shape, dt))
            w1 = [PB("cw1%d" % i, [64, 32, 256], BF16) for i in range(2)]
            w2 = [PB("cw2%d" % i, [128, 2, 64], BF16) for i in range(2)]
            peT = [PB("peT%d" % i, [64, 32], BF16) for i in range(2)]
            hid = [PB("hid%d" % i, [128, 2, 512], BF16) for i in range(2)]
            bia = [PB("bia%d" % i, [128, 2], F32) for i in range(2)]
            ksq = PB("ksq", [64, 512], BF16)
            krt = PB("krt", [64, 512], F32)
            for i, (a1, a2, ap_) in enumerate([(ckw1_in, ckw2_in, pek_in), (cvw1_in, cvw2_in, pev_in)]):
                sc.dma("pool", "cw%d" % i, [
                    (w1[i][:, :, :], a1.rearrange("(l d) h -> d l h", d=64)),
                    (w2[i][:, :, :], a2.rearrange("(hc p) d -> p hc d", p=128)),
                    (peT[i][:, :], ap_[:, :])], writes=[("cw", i)])
                sc.op("dve", lambda e: e.memset(hid[i][:, :, 511:512], 0.0), writes=[("hid", i)])
            for i in range(2):
                src = KA if i == 0 else KB
                for hc in range(2):
                    sc.mm(ps[0][:, hc:hc + 1], [(w1[i][:, l, hc * 128:(hc + 1) * 128], peT[i][:, l:l + 1]) for l in range(32)],
                          reads=[("cw", i)], writes=[("ps", 0, hc)])
                    sc.op("act", lambda e: e.activation(out=bia[i][:, hc:hc + 1], in_=ps[0][:, hc:hc + 1], func=AF.Copy),
                          reads=[("ps", 0, hc)], writes=[("bia", i, hc)])
                    pb_ = 1 + hc
                    sc.mm(ps[pb_][:, 0:NCMP],
                          [(w1[i][:, l, hc * 128:(hc + 1) * 128], src[0:64, l:l + 16 * (NCMP - 1) + 1:16]) for l in range(32)],
                          reads=[("cw", i)], writes=[("ps", pb_)])
                    sc.op("act", lambda e: e.activation(out=hid[i][:, hc, 0:NCMP], in_=ps[pb_][:, 0:NCMP], func=AF.Silu,
                                                        bias=bia[i][:, hc:hc + 1], scale=1.0),
                          reads=[("ps", pb_), ("bia", i, hc), ("hid", i)], writes=[("hid", i)])
            sc.mm(ps[3][0:64, :], [(w2[0][:, hc, :], hid[0][:, hc, :]) for hc in range(2)],
                  reads=[("hid", 0), ("cw", 0)], writes=[("ps", 3)])
            sc.op("act", lambda e: e.activation(out=ksq[:, :], in_=ps[3][0:64, :], func=AF.Square), reads=[("ps", 3)], writes=["ksq"])
            sc.mm(ps[4][0:64, :], [(BD[0:64, 0:64], ksq[:, :])], reads=["ksq", "BD"], writes=[("ps", 4)])
            sc.op("act", lambda e: e.activation(out=krt[:, :], in_=ps[4][0:64, :], func=AF.Sqrt, bias=epsb[0:64, 0:1], scale=1.0),
                  reads=[("ps", 4), "eps"], writes=["krt"])
            sc.op("dve", lambda e: e.reciprocal(out=krt[:, :], in_=krt[:, :]), reads=["krt"], writes=["krt"])
            sc.op("dve", lambda e: e.scalar_tensor_tensor(out=KCMP[0:64, :], in0=ps[3][0:64, :], scalar=cv[0:64, 14:15], in1=krt[:, :],
                                                          op0=ALU.mult, op1=ALU.mult),
                  reads=[("ps", 3), "krt", "cv"], writes=["KCMP"])
            for cc in range(4):
                pb_ = 5 + cc % 2
                sc.mm(ps[pb_][:, 0:64], [(hid[1][:, hc, cc * 128:(cc + 1) * 128], w2[1][:, hc, :]) for hc in range(2)],
                      reads=[("hid", 1), ("cw", 1)], writes=[("ps", pb_)])
                sc.op("act", lambda e: e.activation(out=VC1[:, cc, 0:64], in_=ps[pb_][:, 0:64], func=AF.Copy),
                      reads=[("ps", pb_)], writes=[("VC1", cc)])
            sc.barrier()

        with ExitStack() as ph:
            PB = lambda name, shape, dt: ph.enter_context(nc.sbuf_tensor("sb_" + name, shape, dt))
            ident = PB("ident", [128, 128], BF16)
            tri = PB("tri", [128, 128], BF16)
            tric = PB("tric", [128, 128], BF16)
            cmask = PB("cmask", [128, 17, 128], BF16)
            cmap = PB("cmap", [128, 4, 128], BF16)
            ext = PB("ext", [128, S], BF16)
            t1 = PB("t1", [128, 256], F32)
            t2 = PB("t2", [128, 256], F32)
            NE = 8
            Eb = [PB("E%d" % i, [128, 512], BF16) for i in range(NE)]
            impb = PB("impb", [128, 128], F32)
            imp2 = PB("imp2", [128, 128], F32)
            imp3 = PB("imp3", [128, 128], F32)
            m8 = PB("m8", [128, 16], F32)
            selb = PB("selb", [128, 128], BF16)
            selT = PB("selT", [128, 128], BF16)
            mkd = PB("mkd", [128, 128], BF16)
            rz = PB("rz", [128, 3, 4], F32)
            cf = PB("cf", [128, 3, 4], F32)
            osb = [PB("osb%d" % i, [128, 4, 64], F32) for i in range(2)]
            otm = PB("otm", [128, 4, 64], F32)
            sc.dma("pool", "c3", [(ident[:, :], ident_in[:, :]), (tri[:, :], tri_in[:, :]), (tric[:, :], tric_in[:, :]),
                                  (cmask[:, :, :], cmask_in[:, :, :]), (cmap[:, :, :], cmap_in[:, :, :])], writes=["c3"])
            sc.dma("pool", "c3b", [(ext[:, 0:4096], ext_in[:, 0:4096]), (ext[:, 4096:S], ext_in[:, 4096:S])], writes=["ext"])
            sc.dma("sp", "c3c", [(t1[:, :], t1_in[:, :]), (t2[:, :], t2_in[:, :])], writes=["t12"])
            PS_S = (0, 1)
            PS_OC, PS_OS, PS_OW, PS_IM, PS_MK = 2, 3, 4, 5, 6
            ecnt = [0]
            scnt = [0]
            mkc = [0]
            MKB = (6, 5)

            SBATCH = 2
            pend = []

            def emitSE(u):
                lhsT, q_rhs, kreads, mask_fn, vrhs, vreads, po, first, last_ = u["a"]
                pS = PS_S[scnt[0] % 2]; scnt[0] += 1
                ei = ecnt[0] % NE; ecnt[0] += 1
                E = Eb[ei]
                u["E"] = E; u["ei"] = ei
                sc.mm(ps[pS][:, :], [(lhsT, q_rhs)], reads=kreads, writes=[("ps", pS)])
                sc.op("act", lambda e: e.activation(out=E[:, :], in_=ps[pS][:, :], func=AF.Exp, scale=0.125),
                      reads=[("ps", pS)], writes=[("E", ei)])
                if mask_fn is not None:
                    mask_fn(E, ei)

            def emitPV(u):
                lhsT, q_rhs, kreads, mask_fn, vrhs, vreads, po, first, last_ = u["a"]
                E, ei = u["E"], u["ei"]
                waits = sc._collect("pe", [("E", ei)] + vreads, [("ps", po)])
                sc._emit_waits("pe", waits)
                for r in range(4):
                    inst = nc.tensor.matmul(ps[po][:, r * 65:(r + 1) * 65], lhsT=E[:, r * 128:(r + 1) * 128], rhs=vrhs,
                                            start=(first and r == 0), stop=(last_ and r == 3), skip_group_check=True)
                sc.pe_inc(inst)
                sc._record(("pe", sc.cnt["pe"], "pe"), [("E", ei)] + vreads, [("ps", po)])

            mode = ["full"]
            sb = [0]

            def pe_mode(m):
                if mode[0] != m:
                    if sc.waited["pe"].get("pe", 0) < sc.cnt["pe"]:
                        sc._emit_waits("pe", {"pe": sc.cnt["pe"]})
                    mode[0] = m

            def push(*a):
                u = {"a": a}
                pe_mode("tile")
                emitSE(u)
                pend.append(u)
                sb[0] += 1
                if sb[0] >= SBATCH:
                    sb[0] = 0
                    if len(pend) > SBATCH:
                        pe_mode("full")
                        while len(pend) > SBATCH:
                            emitPV(pend.pop(0))
                return u

            def flush():
                sb[0] = 0
                if pend:
                    pe_mode("full")
                while pend:
                    emitPV(pend.pop(0))

            def bmask(mask_ap_fn, mreads):
                def f(E, ei):
                    sc.op("dve", lambda e: e.tensor_tensor(
                        out=E[:, :].rearrange("p (r t) -> p r t", r=4), in0=E[:, :].rearrange("p (r t) -> p r t", r=4),
                        in1=mask_ap_fn(), op=ALU.mult), reads=[("E", ei)] + mreads, writes=[("E", ei)])
                return f

            for qt in range(NQT):
                qtok = slice(qt * 128, (qt + 1) * 128)
                qn_rhs = Q[0:64, :, qtok]
                qr_rhs = Q[64:128, :, qtok]
                qreads = []
                ncn = min(4, (8 * qt + 7 + 127) // 128)
                Es = []
                for cc in range(ncn):
                    m = 8 * qt - 128 * cc
                    mf = None
                    if 0 <= m <= 128:
                        mi = m // 8
                        mf = bmask(lambda mi=mi: cmask[:, mi, :].unsqueeze(1).broadcast_to([128, 4, 128]), ["c3"])
                    u = push(KCMP[0:64, cc * 128:(cc + 1) * 128], qn_rhs, ["KCMP"] + qreads, mf,
                             VC1[:, cc, :], [("VC1", c_) for c_ in range(4)] + ["VC1ones"], PS_OC, cc == 0, cc == ncn - 1)
                    Es.append((u, cc))
                flush()
                pe_mode("full")
                nmm = len(Es) * 4
                i_ = 0
                for (u, cc) in Es:
                    E, ei = u["E"], u["ei"]
                    for r in range(4):
                        waits = sc._collect("pe", [("E", ei), "c3"], [("mkb", 1)])
                        sc._emit_waits("pe", waits)
                        inst = nc.tensor.matmul(ps[PS_IM][:, r * 128:(r + 1) * 128], lhsT=E[:, r * 128:(r + 1) * 128],
                                                rhs=cmap[:, cc, :], start=(i_ == 0), stop=(i_ == nmm - 1), skip_group_check=True)
                        i_ += 1
                        sc._record(("pe", sc.cnt["pe"] + 1, "pe"), [("E", ei), "c3"], [("mkb", 1)])
                sc.pe_inc(inst)
                sc.op("dve", lambda e: e.tensor_scalar_max(
                    out=rz[:, 0, :], in0=ps[PS_OC][:, 0:260].rearrange("p (r c) -> p r c", r=4)[:, :, 64], scalar1=1e-30),
                    reads=[("ps", PS_OC)], writes=[("rz", 0)])
                sc.op("dve", lambda e: e.reciprocal(out=rz[:, 0, :], in_=rz[:, 0, :]), reads=[("rz", 0)], writes=[("rz", 0)])
                for r in range(4):
                    if r == 0:
                        sc.op("dve", lambda e: e.tensor_scalar(out=impb[:, :], in0=ps[PS_IM][:, 0:128], scalar1=rz[:, 0, 0:1],
                                                               scalar2=None, op0=ALU.mult),
                              reads=[("mkb", 1), ("rz", 0)], writes=["impb"])
                    else:
                        sc.op("dve", lambda e: e.scalar_tensor_tensor(
                            out=impb[:, :], in0=ps[PS_IM][:, r * 128:(r + 1) * 128], scalar=rz[:, 0, r:r + 1], in1=impb[:, :],
                            op0=ALU.mult, op1=ALU.add), reads=[("mkb", 1), ("rz", 0), "impb"], writes=["impb"])
                off = 127 - 2 * qt
                sc.op("dve", lambda e: e.tensor_tensor(out=imp2[:, :], in0=impb[:, :], in1=t1[:, off:off + 128], op=ALU.mult),
                      reads=["impb", "t12"], writes=["imp2"])
                sc.op("dve", lambda e: e.tensor_tensor(out=imp2[:, :], in0=imp2[:, :], in1=t2[:, off:off + 128], op=ALU.add),
                      reads=["imp2", "t12"], writes=["imp2"])
                sc.op("dve", lambda e: e.memset(imp2[:, 0:1], 1e4), reads=["imp2"], writes=["imp2"])
                sc.op("dve", lambda e: e.max(out=m8[:, 0:8], in_=imp2[:, :]), reads=["imp2"], writes=["m8a"])
                sc.op("dve", lambda e: e.match_replace(out=imp3[:, :], in_to_replace=m8[:, 0:8], in_values=imp2[:, :], imm_value=-1e30),
                      reads=["imp2", "m8a"], writes=["imp3"])
                sc.op("dve", lambda e: e.max(out=m8[:, 8:16], in_=imp3[:, :]), reads=["imp3"], writes=["m8b"])
                sc.op("dve", lambda e: e.tensor_scalar(out=selb[:, :], in0=imp2[:, :], scalar1=m8[:, 15:16], scalar2=None, op0=ALU.is_ge),
                      reads=["imp2", "m8b"], writes=["selb"])
                k0 = max(0, qt - 4)
                for kt in range(k0, qt + 1):
                    ktok = slice(kt * 128, (kt + 1) * 128)
                    if kt == qt:
                        mf = bmask(lambda: tri[:, :].unsqueeze(1).broadcast_to([128, 4, 128]), ["c3"])
                    elif kt == qt - 4:
                        mf = bmask(lambda: tric[:, :].unsqueeze(1).broadcast_to([128, 4, 128]), ["c3"])
                    else:
                        mf = None
                    push(KB[64:128, ktok], qr_rhs, qreads, mf,
                         V1[:, kt, 1, :], [("V1", kt), "V1ones"], PS_OW, kt == k0, kt == qt)
                mode[0] = "x"
                pe_mode("full")
                waits = sc._collect("pe", ["selb", "c3"], ["psb"])
                sc._emit_waits("pe", waits)
                inst = nc.tensor.transpose(psb[:, 0:128], selb[:, :], ident[:, :])
                sc.pe_inc(inst)
                sc._record(("pe", sc.cnt["pe"], "pe"), ["selb", "c3"], ["psb"])
                sc.op("act", lambda e: e.activation(out=selT[:, :], in_=psb[:, 0:128], func=AF.Copy), reads=["psb"], writes=["selT"])
                for kt in range(qt + 1):
                    kb = kt % 4
                    if kb == 0:
                        nk = min(4, qt + 1 - kt)
                        pe_mode("full")
                        mkc[0] += 1
                        mb = MKB[mkc[0] % 2]
                        mkey = ("mkb", mkc[0] % 2)
                        for u_ in range(nk):
                            sc.mm(ps[mb][:, u_ * 128:(u_ + 1) * 128],
                                  [(ext[:, (kt + u_) * 128:(kt + u_ + 1) * 128], selT[:, :])],
                                  reads=["ext", "selT"], writes=[mkey])
                    ktok = slice(kt * 128, (kt + 1) * 128)
                    if kt == qt:
                        sc.op("dve", lambda e: e.tensor_tensor(out=mkd[:, :], in0=ps[mb][:, kb * 128:(kb + 1) * 128], in1=tri[:, :],
                                                               op=ALU.mult), reads=[mkey, "c3"], writes=["mkd"])
                        mf = bmask(lambda: mkd[:, :].unsqueeze(1).broadcast_to([128, 4, 128]), ["mkd"])
                    else:
                        mf = bmask(lambda kb=kb, mb=mb: ps[mb][:, kb * 128:(kb + 1) * 128].unsqueeze(1).broadcast_to([128, 4, 128]),
                                   [mkey])
                    push(KA[64:128, ktok], qr_rhs, qreads, mf,
                         V1[:, kt, 0, :], [("V1", kt), "V1ones"], PS_OS, kt == 0, kt == qt)
                flush()
                for bi, po in ((1, PS_OS), (2, PS_OW)):
                    sc.op("dve", lambda e: e.tensor_scalar_max(
                        out=rz[:, bi, :], in0=ps[po][:, 0:260].rearrange("p (r c) -> p r c", r=4)[:, :, 64], scalar1=1e-30),
                        reads=[("ps", po)], writes=[("rz", bi)])
                    sc.op("dve", lambda e: e.reciprocal(out=rz[:, bi, :], in_=rz[:, bi, :]), reads=[("rz", bi)], writes=[("rz", bi)])
                sc.op("dve", lambda e: e.tensor_tensor(out=cf[:, :, :], in0=rz[:, :, :],
                                                       in1=G[:, qt, :].rearrange("p (r b) -> p b r", b=3), op=ALU.mult),
                      reads=[("rz", 0), ("rz", 1), ("rz", 2), ("G", qt)], writes=["cf"])
                ob = osb[qt % 2]
                for bi, po in ((0, PS_OC), (1, PS_OS), (2, PS_OW)):
                    dst = ob if bi == 0 else otm
                    sc.op("dve", lambda e: e.tensor_tensor(
                        out=dst[:, :, :], in0=ps[po][:, 0:260].rearrange("p (r c) -> p r c", r=4)[:, :, 0:64],
                        in1=cf[:, bi, :].unsqueeze(2).broadcast_to([128, 4, 64]), op=ALU.mult),
                        reads=[("ps", po), "cf"], writes=[("osb", qt % 2) if bi == 0 else "otm"])
                    if bi > 0:
                        sc.op("dve", lambda e: e.tensor_tensor(out=ob[:, :, :], in0=ob[:, :, :], in1=otm[:, :, :], op=ALU.add),
                              reads=[("osb", qt % 2), "otm"], writes=[("osb", qt % 2)])
                sc.dma("sp", "oo%d" % (qt % 2), [(oout[qtok, :], ob[:, :, :].rearrange("p r c -> p (r c)"))],
                       reads=[("osb", qt % 2)])
            sc.barrier()
        sc.final_wait("sp")
        ok, stuck = sc.check_deadlock()
        if not ok:
            raise RuntimeError("deadlock in semaphore program: %r" % (stuck,))
    return nc


def nsa_host_inputs(P, j, li, xfull):
    consts = nsa_consts()
    w_in = np.asarray(P["nsa_w_in"][j], np.float32)
    offs = np.cumsum([0, 1024, 256, 256, 256, 256, 256, 256])
    oq, okc, ovc, oks, ovs, okw, ovw, ogl = [int(v) for v in offs]
    in_maps = []
    for c in range(NCORES):
        b, g = divmod(c, 4)
        cols = []
        for r in range(4):
            qc = w_in[:, oq + (g * 4 + r) * 64: oq + (g * 4 + r + 1) * 64]
            cols += [qc, qc]
        sl = lambda o: w_in[:, o + g * 64:o + (g + 1) * 64]
        cols += [sl(okc), sl(oks), sl(ovc), sl(okw)]
        wfm = np.ascontiguousarray(np.concatenate(cols, axis=1))
        wtm = np.ascontiguousarray(np.concatenate([sl(ovs), sl(ovw), w_in[:, ogl + g * 12: ogl + (g + 1) * 12]], axis=1))
        cvec = np.ones((128, 16), np.float32)
        cvec[:, 0:8] = fm(P["mix_norm_g"][li], 8)
        qg = np.asarray(P["nsa_q_norm"][j], np.float32)
        for r in range(4):
            cvec[0:64, 8 + r] = qg
            cvec[64:128, 8 + r] = qg
        cvec[64:128, 12] = np.asarray(P["nsa_ks_norm"][j], np.float32)
        cvec[64:128, 13] = np.asarray(P["nsa_kw_norm"][j], np.float32)
        cvec[0:64, 14] = np.asarray(P["nsa_kc_norm"][j], np.float32)
        m = {"xT": np.ascontiguousarray(xfull[b].T), "wfm": wfm, "wtm": wtm, "cvec": cvec,
             "pekT": np.ascontiguousarray(np.asarray(P["nsa_pe_k"][j], np.float32).T),
             "pevT": np.ascontiguousarray(np.asarray(P["nsa_pe_v"][j], np.float32).T),
             "ck_w1": np.ascontiguousarray(P["nsa_ck_w1"][j], np.float32), "ck_w2": np.ascontiguousarray(P["nsa_ck_w2"][j], np.float32),
             "cv_w1": np.ascontiguousarray(P["nsa_cv_w1"][j], np.float32), "cv_w2": np.ascontiguousarray(P["nsa_cv_w2"][j], np.float32)}
        m.update(consts)
        in_maps.append(m)
    return in_maps


def run_nsa(P, j, li, xfull):
    if "nsa" not in _NC_CACHE:
        _NC_CACHE["nsa"] = build_nsa()
    nc = _NC_CACHE["nsa"]
    in_maps = nsa_host_inputs(P, j, li, xfull)
    res = run_bass_kernel_spmd(nc, in_maps, core_ids=list(range(NCORES)))
    o = np.zeros((B, S, D), np.float32)
    for c in range(NCORES):
        b, g = divmod(c, 4)
        o[b, :, g * 256:(g + 1) * 256] = res.results[c]["o"]
    return o


def kernel(**inputs):
    P = {k: np.asarray(v) for k, v in inputs.items()}
    x = np.ascontiguousarray(P["x"], np.float32)
    x = run_dense([("conv", (0, 0)), ("ffn", 0)], P, x)
    o = run_nsa(P, 0, 1, x)
    x = run_dense([("oproj", 0), ("ffn", 1), ("conv", (2, 1)), ("ffn", 2)], P, x, ofull=o)
    o = run_nsa(P, 1, 3, x)
    x = run_dense([("oproj", 1), ("ffn", 3)], P, x, ofull=o)
    return x.astype(np.float32)
```

```python
import numpy as np
from contextlib import ExitStack
import concourse.bass as bass
import concourse.mybir as mybir
from concourse.bass_utils import run_bass_kernel_spmd

F32 = mybir.dt.float32
BF16 = mybir.dt.bfloat16
ALU = mybir.AluOpType
AF = mybir.ActivationFunctionType
AX = mybir.AxisListType

D = 1024
B = 2
S = 8192
DEPTH = 4
NCORES = 8
TOK = 2048
HALO = 52
TT = TOK + HALO
TW = 420
NT = TT // TW
DFF = 2816
NFC = DFF // 128
CW = 31
EPS = 1e-6
GSZ = 4
GROUPS = [(j, min(GSZ, NFC - j)) for j in range(0, NFC, GSZ)]

HD = 64
NKV = 4
NH = 16
PROJ = 2608


class Sched:
    def __init__(self, nc, es):
        self.nc = nc
        self.es = es
        self.E = {"pe": nc.tensor, "act": nc.scalar, "dve": nc.vector, "pool": nc.gpsimd, "sp": nc.sync}
        self.semh = {}
        self.cnt = {}
        for e in self.E:
            self.semh[e] = es.enter_context(nc.semaphore("s_" + e))
            self.cnt[e] = 0
        self.waited = {e: {} for e in self.E}
        self.res = {}
        self.pending_noinc = {e: False for e in self.E}
        self.log = {e: [] for e in self.E}

    def dsem(self, name):
        k = "d_" + name
        if k not in self.semh:
            self.semh[k] = self.es.enter_context(self.nc.semaphore(k))
            self.cnt[k] = 0
        return k

    def _collect(self, eng, reads, writes):
        waits = {}

        def need(dep, same_ok):
            semk, val, deng = dep
            if deng == eng and (not same_ok or eng == "pe"):
                return
            if self.waited[eng].get(semk, 0) >= val:
                return
            if waits.get(semk, 0) < val:
                waits[semk] = val

        for r in reads:
            st = self.res.get(r)
            if st is not None and st[0] is not None:
                need(st[0], True)
        for w in writes:
            st = self.res.get(w)
            if st is not None:
                if st[0] is not None:
                    need(st[0], True)
                for rd in st[1].values():
                    need(rd, False)
        return waits

    def _emit_waits(self, eng, waits):
        E = self.E[eng]
        for semk, val in waits.items():
            E.wait_ge(self.semh[semk], val)
            self.waited[eng][semk] = val
            self.log[eng].append(("w", semk, val))

    def _record(self, me, reads, writes):
        for r in reads:
            st = self.res.setdefault(r, [None, {}])
            st[1][me[2]] = me
        for w in writes:
            st = self.res.setdefault(w, [None, {}])
            st[0] = me
            st[1] = {}

    def op(self, eng, fn, reads=(), writes=(), inc=True):
        waits = self._collect(eng, reads, writes)
        self._emit_waits(eng, waits)
        inst = fn(self.E[eng])
        if inc:
            self.cnt[eng] += 1
            inst.then_inc(self.semh[eng], 1)
            self.log[eng].append(("i", eng, 1))
            me = (eng, self.cnt[eng], eng)
            self.pending_noinc[eng] = False
        else:
            me = (eng, self.cnt[eng] + 1, eng)
            self.pending_noinc[eng] = True
        self._record(me, reads, writes)
        return inst

    def pe_inc(self, inst):
        self.cnt["pe"] += 1
        inst.then_inc(self.semh["pe"], 1)
        self.log["pe"].append(("i", "pe", 1))

    def mm(self, out, pairs, reads, writes):
        waits = self._collect("pe", reads, writes)
        self._emit_waits("pe", waits)
        n = len(pairs)
        inst = None
        for i, (lhsT, rhs) in enumerate(pairs):
            inst = self.nc.tensor.matmul(out, lhsT=lhsT, rhs=rhs, start=(i == 0), stop=(i == n - 1))
        self.pe_inc(inst)
        me = ("pe", self.cnt["pe"], "pe")
        self._record(me, reads, writes)

    def mm_part(self, out, lhsT, rhs, start, stop, reads, writes, last):
        waits = self._collect("pe", reads, writes)
        self._emit_waits("pe", waits)
        inst = self.nc.tensor.matmul(out, lhsT=lhsT, rhs=rhs, start=start, stop=stop)
        if last:
            self.pe_inc(inst)
            me = ("pe", self.cnt["pe"], "pe")
        else:
            me = ("pe", self.cnt["pe"] + 1, "pe")
        self._record(me, reads, writes)

    def dma(self, q, dname, items, reads=(), writes=()):
        semk = self.dsem(dname)
        waits = self._collect(q, reads, writes)
        if self.cnt[semk] > 0 and self.waited[q].get(semk, 0) < self.cnt[semk]:
            waits[semk] = self.cnt[semk]
        self._emit_waits(q, waits)
        for (o, i) in items:
            inst = self.E[q].dma_start(out=o, in_=i)
            self.cnt[semk] += 16
            inst.then_inc(self.semh[semk], 16)
            self.log[q].append(("i", semk, 16))
        me = (semk, self.cnt[semk], "dma:" + semk)
        self._record(me, reads, writes)

    def check_deadlock(self):
        pos = {e: 0 for e in self.log}
        val = {}
        progress = True
        while progress:
            progress = False
            for e, lg in self.log.items():
                while pos[e] < len(lg):
                    kind, semk, v = lg[pos[e]]
                    if kind == "w":
                        if val.get(semk, 0) >= v:
                            pos[e] += 1; progress = True
                        else:
                            break
                    else:
                        val[semk] = val.get(semk, 0) + v
                        pos[e] += 1; progress = True
        stuck = {e: (pos[e], len(lg), lg[pos[e]] if pos[e] < len(lg) else None, val.get(lg[pos[e]][1]) if pos[e] < len(lg) else None)
                 for e, lg in self.log.items()}
        return all(pos[e] == len(lg) for e, lg in self.log.items()), stuck

    def barrier(self):
        comp = ["pe", "act", "dve", "pool", "sp"]
        for e in comp:
            waits = {}
            for k, v in self.cnt.items():
                if k == e or v == 0:
                    continue
                if self.waited[e].get(k, 0) < v:
                    waits[k] = v
            self._emit_waits(e, waits)
        self.res = {}

    def final_wait(self, eng="sp"):
        waits = {}
        for k, v in self.cnt.items():
            if k == eng or v == 0:
                continue
            if self.waited[eng].get(k, 0) < v:
                waits[k] = v
        self._emit_waits(eng, waits)


def fm(vec, nch):
    return np.ascontiguousarray(np.asarray(vec, np.float32).reshape(nch, 128).T)


class VecPack:
    def __init__(self):
        self.off = {}
        self.n = 0
        self.arrs = []

    def add(self, name, arr):
        arr = np.asarray(arr, np.float32)
        assert arr.shape[0] == 128
        arr = arr.reshape(128, -1)
        self.off[name] = (self.n, arr.shape[1])
        self.n += arr.shape[1]
        self.arrs.append(arr)

    def array(self):
        return np.ascontiguousarray(np.concatenate(self.arrs, axis=1))


def dense_vec_layout(phases):
    off = {}
    n = 0

    def add(name, w):
        nonlocal n
        off[name] = (n, w)
        n += w

    for pi, (kind, _) in enumerate(phases):
        p = "p%d_" % pi
        if kind == "conv":
            add(p + "g", 8); add(p + "b1", 16); add(p + "wdw", 8 * CW); add(p + "bdw", 8)
            add(p + "lng", 8); add(p + "lnb", 8); add(p + "b2", 8)
        elif kind == "ffn":
            add(p + "g", 8); add(p + "wdw", NFC * 3); add(p + "bdw", NFC)
        elif kind == "oproj":
            pass
    return off, n


def build_dense(phases):
    nc = bass.Bass("TRN2", target_bir_lowering=False)
    voff, nv = dense_vec_layout(phases)
    xin = nc.dram_tensor("xT", [D, TT], F32, kind="ExternalInput").ap()
    hm_in = nc.dram_tensor("hmask", [128, HALO], F32, kind="ExternalInput").ap()
    cv_in = nc.dram_tensor("cvec", [128, max(nv, 1)], F32, kind="ExternalInput").ap()
    yout = nc.dram_tensor("yT", [D, TOK], F32, kind="ExternalOutput").ap()
    wd = {}
    for pi, (kind, _) in enumerate(phases):
        p = "p%d_" % pi
        if kind == "conv":
            wd[p + "w1"] = nc.dram_tensor(p + "w1", [D, 2 * D], F32, kind="ExternalInput").ap()
            wd[p + "w2"] = nc.dram_tensor(p + "w2", [D, D], F32, kind="ExternalInput").ap()
        elif kind == "ffn":
            wd[p + "wup"] = nc.dram_tensor(p + "wup", [D, 2 * DFF], F32, kind="ExternalInput").ap()
            wd[p + "wdn"] = nc.dram_tensor(p + "wdn", [DFF, D], F32, kind="ExternalInput").ap()
        elif kind == "oproj":
            wd[p + "wo"] = nc.dram_tensor(p + "wo", [D, D], F32, kind="ExternalInput").ap()
            wd[p + "oT"] = nc.dram_tensor(p + "oT", [D, TT], F32, kind="ExternalInput").ap()

    with ExitStack() as es:
        sc = Sched(nc, es)
        xT = es.enter_context(nc.sbuf_tensor("xTs", [128, 8, TT], F32))
        cv = es.enter_context(nc.sbuf_tensor("cv", [128, max(nv, 1)], F32))
        hmk = es.enter_context(nc.sbuf_tensor("hmk", [128, HALO], F32))
        onesm = es.enter_context(nc.sbuf_tensor("onesm", [128, 128], BF16))
        epsb = es.enter_context(nc.sbuf_tensor("epsb", [128, 1], F32))
        ps = [es.enter_context(nc.psum_tensor("ps%d" % i, [128, 512], F32)) for i in range(8)]

        def V(name, a=0, b=None):
            o, w = voff[name]
            if b is None:
                b = w
            return cv[:, o + a:o + b]

        xin_v = xin.rearrange("(kc p) t -> p kc t", p=128)
        for t in range(NT):
            sc.dma("sp", "xin%d" % t, [(xT[:, :, t * TW:(t + 1) * TW], xin_v[:, :, t * TW:(t + 1) * TW])],
                   writes=[("x", t)])
        sc.dma("sp", "cvl", [(cv[:, :], cv_in[:, :]), (hmk[:, :], hm_in[:, :])], writes=["cv"])
        sc.op("dve", lambda e: e.memset(onesm[:, :], 1.0 / 1024.0), writes=["ones"])
        sc.op("dve", lambda e: e.memset(epsb[:, :], EPS), writes=["eps"])

        def rms_stats(x_ap, reads_x, sq, rs, k, psb):
            sc.op("act", lambda e: e.activation(out=sq[:, :, :], in_=x_ap, func=AF.Square),
                  reads=reads_x, writes=[("sq", k)])
            sc.mm(ps[psb][:, :TW], [(onesm[:, :], sq[:, kc, :]) for kc in range(8)],
                  reads=[("sq", k), "ones"], writes=[("ps", psb)])
            sc.op("act", lambda e: e.activation(out=rs[:, :], in_=ps[psb][:, :TW], func=AF.Sqrt,
                                                bias=epsb[:, 0:1], scale=1.0),
                  reads=[("ps", psb), "eps"], writes=[("rs", k)])
            sc.op("dve", lambda e: e.reciprocal(out=rs[:, :], in_=rs[:, :]), reads=[("rs", k)], writes=[("rs", k)])

        for pi, (kind, _) in enumerate(phases):
            p = "p%d_" % pi
            with ExitStack() as ph:
                if kind == "conv":
                    w1 = ph.enter_context(nc.sbuf_tensor(p + "w1s", [128, 8, 2 * D], BF16))
                    w2 = ph.enter_context(nc.sbuf_tensor(p + "w2s", [128, 8, D], BF16))
                    sqb = [ph.enter_context(nc.sbuf_tensor(p + "sq%d" % i, [128, 8, TW], BF16)) for i in range(2)]
                    rsb = [ph.enter_context(nc.sbuf_tensor(p + "rs%d" % i, [128, TW], F32)) for i in range(2)]
                    hb = [ph.enter_context(nc.sbuf_tensor(p + "hb%d" % i, [128, 8, TW], BF16)) for i in range(2)]
                    uG = ph.enter_context(nc.sbuf_tensor(p + "uG", [128, 8, TW + CW - 1], F32))
                    yb = ph.enter_context(nc.sbuf_tensor(p + "yb", [128, 8, TW], F32))
                    sgb = [ph.enter_context(nc.sbuf_tensor(p + "sg%d" % i, [128, TW], F32)) for i in range(2)]
                    ybf = [ph.enter_context(nc.sbuf_tensor(p + "ybf%d" % i, [128, TW], BF16)) for i in range(2)]
                    ysq = [ph.enter_context(nc.sbuf_tensor(p + "ysq%d" % i, [128, TW], BF16)) for i in range(2)]
                    zb = ph.enter_context(nc.sbuf_tensor(p + "zb", [128, 8, TW], BF16))
                    mu = ph.enter_context(nc.sbuf_tensor(p + "mu", [128, TW], F32))
                    var = ph.enter_context(nc.sbuf_tensor(p + "var", [128, TW], F32))
                    nb = ph.enter_context(nc.sbuf_tensor(p + "nb", [128, TW], F32))
                    ntm = [ph.enter_context(nc.sbuf_tensor(p + "ntm%d" % i, [128, TW], F32)) for i in range(2)]
                    w1v = wd[p + "w1"].rearrange("(kc p) n -> p kc n", p=128)
                    w2v = wd[p + "w2"].rearrange("(kc p) n -> p kc n", p=128)
                    sc.dma("pool", "w1a", [(w1[:, 0:4, :], w1v[:, 0:4, :])], writes=["w1a"])
                    sc.dma("pool", "w1b", [(w1[:, 4:8, :], w1v[:, 4:8, :])], writes=["w1b"])
                    sc.dma("pool", "w2", [(w2[:, :, :], w2v[:, :, :])], writes=["w2"])
                    sc.op("dve", lambda e: e.memset(uG[:, :, 0:CW - 1], 0.0), writes=[("uG", c) for c in range(8)])
                    PS_ST, PS_A, PS_G, PS_MU, PS_E2 = 0, (1, 2), (3, 4), 5, 6
                    for t in range(NT):
                        cols = slice(t * TW, (t + 1) * TW)
                        k = t % 2
                        rms_stats(xT[:, :, cols], [("x", t)], sqb[k], rsb[k], k, PS_ST)
                        for kc in range(8):
                            sc.op("dve", lambda e: e.scalar_tensor_tensor(
                                out=hb[k][:, kc, :], in0=xT[:, kc, cols], scalar=V(p + "g", kc, kc + 1),
                                in1=rsb[k][:, :], op0=ALU.mult, op1=ALU.mult),
                                reads=[("x", t), ("rs", k), "cv"], writes=[("hb", k)])
                        for c2 in range(0, 8, 2):
                            for c in (c2, c2 + 1):
                                pa, pg = PS_A[c % 2], PS_G[c % 2]
                                sc.mm(ps[pa][:, :TW], [(w1[:, kc, c * 128:(c + 1) * 128], hb[k][:, kc, :]) for kc in range(8)],
                                      reads=[("hb", k), "w1a", "w1b"], writes=[("ps", pa)])
                                sc.mm(ps[pg][:, :TW], [(w1[:, kc, D + c * 128:D + (c + 1) * 128], hb[k][:, kc, :]) for kc in range(8)],
                                      reads=[("hb", k), "w1a", "w1b"], writes=[("ps", pg)])
                                sc.op("act", lambda e: e.activation(out=sgb[c % 2][:, :], in_=ps[pg][:, :TW], func=AF.Sigmoid,
                                                                    bias=V(p + "b1", 8 + c, 9 + c), scale=1.0),
                                      reads=[("ps", pg), "cv"], writes=[("sg", c % 2)])
                                sc.op("dve", lambda e: e.scalar_tensor_tensor(
                                    out=uG[:, c, CW - 1:], in0=ps[pa][:, :TW], scalar=V(p + "b1", c, c + 1),
                                    in1=sgb[c % 2][:, :], op0=ALU.add, op1=ALU.mult),
                                    reads=[("ps", pa), ("sg", c % 2), "cv"], writes=[("uG", c)])
                                if t == 0:
                                    sc.op("dve", lambda e: e.tensor_tensor(
                                        out=uG[:, c, CW - 1:CW - 1 + HALO], in0=uG[:, c, CW - 1:CW - 1 + HALO],
                                        in1=hmk[:, :], op=ALU.mult), reads=[("uG", c), "cv"], writes=[("uG", c)])
                            for kk in range(CW):
                                for c in (c2, c2 + 1):
                                    wk = V(p + "wdw", c * CW + kk, c * CW + kk + 1)
                                    if kk == 0:
                                        sc.op("dve", lambda e: e.tensor_scalar(
                                            out=yb[:, c, :], in0=uG[:, c, 0:TW], scalar1=wk, scalar2=V(p + "bdw", c, c + 1),
                                            op0=ALU.mult, op1=ALU.add), reads=[("uG", c), "cv"], writes=[("y", c)])
                                    else:
                                        sc.op("dve", lambda e: e.scalar_tensor_tensor(
                                            out=yb[:, c, :], in0=uG[:, c, kk:kk + TW], scalar=wk, in1=yb[:, c, :],
                                            op0=ALU.mult, op1=ALU.add), reads=[("uG", c), ("y", c), "cv"], writes=[("y", c)])
                            for c in (c2, c2 + 1):
                                sc.op("act", lambda e: e.activation(out=ybf[c % 2][:, :], in_=yb[:, c, :], func=AF.Copy),
                                      reads=[("y", c)], writes=[("ybf", c % 2)])
                                sc.op("act", lambda e: e.activation(out=ysq[c % 2][:, :], in_=yb[:, c, :], func=AF.Square),
                                      reads=[("y", c)], writes=[("ysq", c % 2)])
                                sc.mm_part(ps[PS_MU][:, :TW], onesm[:, :], ybf[c % 2][:, :], c == 0, c == 7,
                                           reads=[("ybf", c % 2), "ones"], writes=[("ps", PS_MU)], last=True)
                                sc.mm_part(ps[PS_E2][:, :TW], onesm[:, :], ysq[c % 2][:, :], c == 0, c == 7,
                                           reads=[("ysq", c % 2), "ones"], writes=[("ps", PS_E2)], last=True)
                        sc.op("dve", lambda e: e.tensor_copy(out=uG[:, :, 0:CW - 1], in_=uG[:, :, TW:TW + CW - 1]),
                              reads=[("uG", c) for c in range(8)], writes=[("uG", c) for c in range(8)])
                        sc.op("act", lambda e: e.activation(out=mu[:, :], in_=ps[PS_MU][:, :TW], func=AF.Copy),
                              reads=[("ps", PS_MU)], writes=["mu"])
                        sc.op("dve", lambda e: e.tensor_tensor(out=var[:, :], in0=mu[:, :], in1=mu[:, :], op=ALU.mult),
                              reads=["mu"], writes=["var"])
                        sc.op("dve", lambda e: e.tensor_tensor(out=var[:, :], in0=ps[PS_E2][:, :TW], in1=var[:, :], op=ALU.subtract),
                              reads=[("ps", PS_E2), "var"], writes=["var"])
                        sc.op("dve", lambda e: e.tensor_scalar_max(out=var[:, :], in0=var[:, :], scalar1=0.0),
                              reads=["var"], writes=["var"])
                        sc.op("act", lambda e: e.activation(out=var[:, :], in_=var[:, :], func=AF.Sqrt, bias=epsb[:, 0:1], scale=1.0),
                              reads=["var", "eps"], writes=["var"])
                        sc.op("dve", lambda e: e.reciprocal(out=var[:, :], in_=var[:, :]), reads=["var"], writes=["var"])
                        sc.op("dve", lambda e: e.scalar_tensor_tensor(out=nb[:, :], in0=mu[:, :], scalar=-1.0, in1=var[:, :],
                                                                      op0=ALU.mult, op1=ALU.mult),
                              reads=["mu", "var"], writes=["nb"])
                        for c in range(8):
                            nt_ = ntm[c % 2]
                            sc.op("dve", lambda e: e.tensor_tensor(out=nt_[:, :], in0=yb[:, c, :], in1=var[:, :], op=ALU.mult),
                                  reads=[("y", c), "var"], writes=[("ntm", c % 2)])
                            sc.op("pool", lambda e: e.tensor_tensor(out=nt_[:, :], in0=nt_[:, :], in1=nb[:, :], op=ALU.add),
                                  reads=[("ntm", c % 2), "nb"], writes=[("ntm", c % 2)])
                            sc.op("act", lambda e: e.activation(out=zb[:, c, :], in_=nt_[:, :], func=AF.Silu,
                                                                bias=V(p + "lnb", c, c + 1), scale=V(p + "lng", c, c + 1)),
                                  reads=[("ntm", c % 2), "cv"], writes=[("z", c)])
                        for m in range(8):
                            po = PS_A[m % 2]
                            sc.mm(ps[po][:, :TW], [(w2[:, c, m * 128:(m + 1) * 128], zb[:, c, :]) for c in range(8)],
                                  reads=[("z", c) for c in range(8)] + ["w2"], writes=[("ps", po)])
                            sc.op("dve", lambda e: e.scalar_tensor_tensor(
                                out=xT[:, m, cols], in0=ps[po][:, :TW], scalar=V(p + "b2", m, m + 1), in1=xT[:, m, cols],
                                op0=ALU.add, op1=ALU.add), reads=[("ps", po), ("x", t), "cv"], writes=[("x", t)])

                elif kind == "oproj":
                    wo = ph.enter_context(nc.sbuf_tensor(p + "wos", [128, 8, D], BF16))
                    oT = ph.enter_context(nc.sbuf_tensor(p + "oTs", [128, 8, TT], BF16))
                    wov = wd[p + "wo"].rearrange("(kc p) n -> p kc n", p=128)
                    oTv = wd[p + "oT"].rearrange("(kc p) t -> p kc t", p=128)
                    sc.dma("pool", "wo", [(wo[:, :, :], wov[:, :, :])], writes=["wo"])
                    for t in range(NT):
                        cols = slice(t * TW, (t + 1) * TW)
                        sc.dma("pool", "oT%d" % t, [(oT[:, :, cols], oTv[:, :, cols])], writes=[("oT", t)])
                    for t in range(NT):
                        cols = slice(t * TW, (t + 1) * TW)
                        for m in range(8):
                            po = 1 + (m % 2)
                            sc.mm(ps[po][:, :TW], [(wo[:, kc, m * 128:(m + 1) * 128], oT[:, kc, cols]) for kc in range(8)],
                                  reads=[("oT", t), "wo"], writes=[("ps", po)])
                            sc.op("dve", lambda e: e.tensor_tensor(out=xT[:, m, cols], in0=ps[po][:, :TW], in1=xT[:, m, cols],
                                                                   op=ALU.add),
                                  reads=[("ps", po), ("x", t)], writes=[("x", t)])

                elif kind == "ffn":
                    hall = ph.enter_context(nc.sbuf_tensor(p + "hall", [128, 8, TT + 2], BF16))
                    wa = [ph.enter_context(nc.sbuf_tensor(p + "wa%d" % i, [128, 8, GSZ * 128], BF16)) for i in range(2)]
                    wv = [ph.enter_context(nc.sbuf_tensor(p + "wv%d" % i, [128, 8, GSZ * 128], BF16)) for i in range(2)]
                    wdn = [ph.enter_context(nc.sbuf_tensor(p + "wdn%d" % i, [128, GSZ, D], BF16)) for i in range(2)]
                    sqb = [ph.enter_context(nc.sbuf_tensor(p + "sq%d" % i, [128, 8, TW], BF16)) for i in range(2)]
                    rsb = [ph.enter_context(nc.sbuf_tensor(p + "rs%d" % i, [128, TW], F32)) for i in range(2)]
                    acc = [ph.enter_context(nc.sbuf_tensor(p + "acc%d" % i, [128, TW], F32)) for i in range(2)]
                    sil = [ph.enter_context(nc.sbuf_tensor(p + "sil%d" % i, [128, TW], F32)) for i in range(2)]
                    gb = [ph.enter_context(nc.sbuf_tensor(p + "gb%d" % i, [128, GSZ, TW], BF16)) for i in range(2)]
                    wupv = wd[p + "wup"].rearrange("(kc p) n -> p kc n", p=128)
                    wdnv = wd[p + "wdn"].rearrange("(jc p) n -> p jc n", p=128)

                    def load_group(gi):
                        j0, G = GROUPS[gi]
                        s = gi % 2
                        sc.dma("pool", "wg%d" % s, [
                            (wa[s][:, :, 0:G * 128], wupv[:, :, j0 * 128:(j0 + G) * 128]),
                            (wv[s][:, :, 0:G * 128], wupv[:, :, DFF + j0 * 128:DFF + (j0 + G) * 128]),
                            (wdn[s][:, 0:G, :], wdnv[:, j0:j0 + G, :]),
                        ], writes=[("wg", s)])

                    load_group(0)
                    sc.op("dve", lambda e: e.memset(hall[:, :, 0:2], 0.0), writes=[("h", -1)])
                    for t in range(NT):
                        cols = slice(t * TW, (t + 1) * TW)
                        k = t % 2
                        rms_stats(xT[:, :, cols], [("x", t)], sqb[k], rsb[k], k, 0)
                        for kc in range(8):
                            sc.op("dve", lambda e: e.scalar_tensor_tensor(
                                out=hall[:, kc, 2 + t * TW:2 + (t + 1) * TW], in0=xT[:, kc, cols], scalar=V(p + "g", kc, kc + 1),
                                in1=rsb[k][:, :], op0=ALU.mult, op1=ALU.mult),
                                reads=[("x", t), ("rs", k), "cv"], writes=[("h", t)])
                        if t == 0:
                            sc.op("dve", lambda e: e.tensor_tensor(
                                out=hall[:, :, 2:2 + HALO], in0=hall[:, :, 2:2 + HALO],
                                in1=hmk[:, :].unsqueeze(1).broadcast_to([128, 8, HALO]), op=ALU.mult),
                                reads=[("h", 0), "cv"], writes=[("h", 0)])
                    PA, PV, PD = (1, 2), (3, 4), (5, 6)
                    it = 0
                    pend = None

                    def emit_down(gi, t, gk):
                        j0, G = GROUPS[gi]
                        s = gi % 2
                        cols = slice(t * TW, (t + 1) * TW)
                        for m in range(8):
                            pd = PD[m % 2]
                            sc.mm(ps[pd][:, :TW], [(wdn[s][:, jj, m * 128:(m + 1) * 128], gb[gk][:, jj, :]) for jj in range(G)],
                                  reads=[("g", gk), ("wg", s)], writes=[("ps", pd)])
                            sc.op("dve", lambda e: e.tensor_tensor(out=xT[:, m, cols], in0=ps[pd][:, :TW], in1=xT[:, m, cols],
                                                                   op=ALU.add),
                                  reads=[("ps", pd), ("x", t)], writes=[("x", t)])

                    for gi, (j0, G) in enumerate(GROUPS):
                        s = gi % 2
                        for t in range(NT):
                            gk = it % 2
                            hreads = [("h", t), ("h", t - 1)]
                            for jj in range(G):
                                j = j0 + jj
                                pa, pv = PA[jj % 2], PV[jj % 2]
                                sc.mm(ps[pa][:, :TW + 2], [(wa[s][:, kc, jj * 128:(jj + 1) * 128], hall[:, kc, t * TW:t * TW + TW + 2])
                                                            for kc in range(8)],
                                      reads=hreads + [("wg", s)], writes=[("ps", pa)])
                                sc.mm(ps[pv][:, :TW], [(wv[s][:, kc, jj * 128:(jj + 1) * 128], hall[:, kc, 2 + t * TW:2 + (t + 1) * TW])
                                                       for kc in range(8)],
                                      reads=hreads + [("wg", s)], writes=[("ps", pv)])
                                a_ = acc[jj % 2]
                                sc.op("dve", lambda e: e.tensor_scalar(
                                    out=a_[:, :], in0=ps[pa][:, 2:TW + 2], scalar1=V(p + "wdw", j * 3 + 2, j * 3 + 3),
                                    scalar2=V(p + "bdw", j, j + 1), op0=ALU.mult, op1=ALU.add),
                                    reads=[("ps", pa), "cv"], writes=[("acc", jj % 2)])
                                sc.op("dve", lambda e: e.scalar_tensor_tensor(
                                    out=a_[:, :], in0=ps[pa][:, 1:TW + 1], scalar=V(p + "wdw", j * 3 + 1, j * 3 + 2), in1=a_[:, :],
                                    op0=ALU.mult, op1=ALU.add), reads=[("ps", pa), ("acc", jj % 2), "cv"], writes=[("acc", jj % 2)])
                                sc.op("dve", lambda e: e.scalar_tensor_tensor(
                                    out=a_[:, :], in0=ps[pa][:, 0:TW], scalar=V(p + "wdw", j * 3, j * 3 + 1), in1=a_[:, :],
                                    op0=ALU.mult, op1=ALU.add), reads=[("ps", pa), ("acc", jj % 2), "cv"], writes=[("acc", jj % 2)])
                                sc.op("act", lambda e: e.activation(out=sil[jj % 2][:, :], in_=a_[:, :], func=AF.Silu),
                                      reads=[("acc", jj % 2)], writes=[("sil", jj % 2)])
                                sc.op("dve", lambda e: e.tensor_tensor(out=gb[gk][:, jj, :], in0=ps[pv][:, :TW], in1=sil[jj % 2][:, :],
                                                                       op=ALU.mult),
                                      reads=[("ps", pv), ("sil", jj % 2)], writes=[("g", gk)])
                            if pend is not None:
                                emit_down(*pend)
                            pend = (gi, t, gk)
                            it += 1
                            if t == 0 and gi + 1 < len(GROUPS):
                                if pend is not None and pend[0] != gi:
                                    pass
                                load_group(gi + 1)
                    emit_down(*pend)
                sc.barrier()

        yv = yout.rearrange("(kc p) t -> p kc t", p=128)
        for t in range(NT):
            lo = max(t * TW, HALO)
            hi = (t + 1) * TW
            sc.dma("sp", "yout%d" % t, [(yv[:, :, lo - HALO:hi - HALO], xT[:, :, lo:hi])], reads=[("x", t)])
        sc.final_wait("sp")
    return nc


def shard_xT(xfull):
    outs = []
    for c in range(NCORES):
        b, q = divmod(c, 4)
        t0 = q * TOK
        buf = np.zeros((TT, xfull.shape[2]), np.float32)
        lo = t0 - HALO
        if lo < 0:
            buf[-lo:] = xfull[b, 0:t0 + TOK]
        else:
            buf[:] = xfull[b, lo:t0 + TOK]
        outs.append(np.ascontiguousarray(buf.T))
    return outs


def unshard_yT(ys):
    out = np.zeros((B, S, D), np.float32)
    for c in range(NCORES):
        b, q = divmod(c, 4)
        out[b, q * TOK:(q + 1) * TOK] = ys[c].T
    return out


def dense_host_inputs(phases, P, xfull, ofull=None):
    vp = VecPack()
    wmaps = {}
    for pi, (kind, idx) in enumerate(phases):
        p = "p%d_" % pi
        if kind == "conv":
            li, j = idx
            vp.add(p + "g", fm(P["mix_norm_g"][li], 8))
            vp.add(p + "b1", fm(P["conv_b_pw1"][j], 16))
            wdw = np.asarray(P["conv_w_dw"][j], np.float32)
            vp.add(p + "wdw", np.ascontiguousarray(wdw.reshape(CW, 8, 128).transpose(2, 1, 0)).reshape(128, 8 * CW))
            vp.add(p + "bdw", fm(P["conv_b_dw"][j], 8))
            vp.add(p + "lng", fm(P["conv_ln_g"][j], 8))
            vp.add(p + "lnb", fm(P["conv_ln_b"][j], 8))
            vp.add(p + "b2", fm(P["conv_b_pw2"][j], 8))
            wmaps[p + "w1"] = np.ascontiguousarray(P["conv_w_pw1"][j], np.float32)
            wmaps[p + "w2"] = np.ascontiguousarray(P["conv_w_pw2"][j], np.float32)
        elif kind == "ffn":
            li = idx
            vp.add(p + "g", fm(P["ffn_norm_g"][li], 8))
            wdw = np.asarray(P["ffn_w_dw"][li], np.float32)
            vp.add(p + "wdw", np.ascontiguousarray(wdw.reshape(3, NFC, 128).transpose(2, 1, 0)).reshape(128, NFC * 3))
            vp.add(p + "bdw", fm(P["ffn_b_dw"][li], NFC))
            wmaps[p + "wup"] = np.ascontiguousarray(P["ffn_w_up"][li], np.float32)
            wmaps[p + "wdn"] = np.ascontiguousarray(P["ffn_w_down"][li], np.float32)
        elif kind == "oproj":
            j = idx
            wmaps[p + "wo"] = np.ascontiguousarray(P["nsa_w_out"][j], np.float32)
    voff, nv = dense_vec_layout(phases)
    assert voff == vp.off, (voff, vp.off)
    cvec = vp.array() if vp.n > 0 else np.zeros((128, 1), np.float32)
    xs = shard_xT(xfull)
    oTs = shard_xT(ofull) if ofull is not None else None
    in_maps = []
    for c in range(NCORES):
        q = c % 4
        m = {"xT": xs[c], "cvec": cvec,
             "hmask": np.full((128, HALO), 0.0 if q == 0 else 1.0, np.float32)}
        m.update(wmaps)
        for pi, (kind, idx) in enumerate(phases):
            if kind == "oproj":
                m["p%d_oT" % pi] = oTs[c]
        in_maps.append(m)
    return in_maps


_NC_CACHE = {}


def run_dense(phases, P, xfull, ofull=None):
    key = ("dense", tuple(k for k, _ in phases))
    if key not in _NC_CACHE:
        _NC_CACHE[key] = build_dense(phases)
    nc = _NC_CACHE[key]
    in_maps = dense_host_inputs(phases, P, xfull, ofull)
    res = run_bass_kernel_spmd(nc, in_maps, core_ids=list(range(NCORES)))
    return unshard_yT([r["yT"] for r in res.results])


NTK = 512
NTT = S // NTK
NQT = S // 128
NCMP = 511
ROT = 16


def nsa_consts():
    c = {}
    half = ROT // 2
    inv_freq = (np.float32(500000.0) ** (-np.arange(half, dtype=np.float32) * np.float32(2.0 / ROT))).astype(np.float32)
    ang = (np.arange(S, dtype=np.float32)[None, :] * inv_freq[:, None]).astype(np.float32)
    cos = np.cos(ang).astype(np.float32)
    sin = np.sin(ang).astype(np.float32)
    c["ropeC"] = np.ascontiguousarray(np.concatenate([cos, cos], 0))
    c["ropeS"] = np.ascontiguousarray(np.concatenate([sin, sin], 0))
    prot = np.zeros((128, 128), np.float32)
    for i in range(half):
        prot[64 + i + half, 64 + i] = -1.0
        prot[64 + i, 64 + i + half] = 1.0
    c["prot"] = prot
    c["ident"] = np.eye(128, dtype=np.float32)
    p = np.arange(128)[:, None]
    t = np.arange(128)[None, :]
    c["tri"] = (p <= t).astype(np.float32)
    c["tric"] = (p > t).astype(np.float32)
    cm = np.zeros((17, 128, 128), np.float32)
    for mi in range(17):
        m = 8 * mi
        cm[mi] = (16 * (p - m) + 31 <= t).astype(np.float32)
    c["cmask"] = np.ascontiguousarray(cm.transpose(1, 0, 2))
    n = np.arange(512)[:, None] * 16
    s_ = np.arange(128)[None, :] * 64
    ov = np.minimum(n + 32, s_ + 64) - np.maximum(n, s_)
    cmap = np.maximum(ov, 0).astype(np.float32) / 32.0
    cmap[511] = 0.0
    c["cmap"] = np.ascontiguousarray(cmap.reshape(4, 128, 128).transpose(1, 0, 2))
    ext = np.zeros((128, S), np.float32)
    ext[np.arange(S) // 64, np.arange(S)] = 1.0
    c["ext"] = ext
    d = np.arange(-127, 129)[None, :]
    tt = np.arange(128)[:, None]
    lo = (tt < 64)
    t1 = np.where(lo, (d <= -2), (d <= -1)).astype(np.float32)
    t2 = np.where(lo,
                  np.where((d == -1) | (d == 0), 1e4, np.where(d > 0, -1.0, 0.0)),
                  np.where((d == 0) | (d == 1), 1e4, np.where(d > 1, -1.0, 0.0))).astype(np.float32)
    c["t1"] = np.ascontiguousarray(t1 * np.ones((128, 1), np.float32))
    c["t2"] = np.ascontiguousarray(t2)
    return c


def build_nsa():
    nc = bass.Bass("TRN2", target_bir_lowering=False)

    def din(name, shape):
        return nc.dram_tensor(name, shape, F32, kind="ExternalInput").ap()

    xin = din("xT", [D, S])
    wfm_in = din("wfm", [D, 768])
    wtm_in = din("wtm", [D, 140])
    cv_in = din("cvec", [128, 8 + 6 + 2])
    pek_in = din("pekT", [64, 32]); pev_in = din("pevT", [64, 32])
    ckw1_in = din("ck_w1", [2048, 256]); ckw2_in = din("ck_w2", [256, 64])
    cvw1_in = din("cv_w1", [2048, 256]); cvw2_in = din("cv_w2", [256, 64])
    ropeC_in = din("ropeC", [16, S]); ropeS_in = din("ropeS", [16, S])
    prot_in = din("prot", [128, 128]); ident_in = din("ident", [128, 128])
    tri_in = din("tri", [128, 128]); tric_in = din("tric", [128, 128])
    cmask_in = din("cmask", [128, 17, 128]); cmap_in = din("cmap", [128, 4, 128])
    ext_in = din("ext", [128, S]); t1_in = din("t1", [128, 256]); t2_in = din("t2", [128, 256])
    oout = nc.dram_tensor("o", [S, 256], F32, kind="ExternalOutput").ap()

    with ExitStack() as es:
        sc = Sched(nc, es)
        SB = lambda name, shape, dt: es.enter_context(nc.sbuf_tensor("sb_" + name, shape, dt))
        Q = SB("Q", [128, 4, S], BF16)
        KA = SB("KA", [128, S], BF16)
        KB = SB("KB", [128, S], BF16)
        V1 = SB("V1", [128, NQT, 2, 65], BF16)
        G = SB("G", [128, NQT, 12], F32)
        KCMP = SB("KCMP", [128, 512], BF16)
        VC1 = SB("VC1", [128, 4, 65], BF16)
        cv = SB("cv", [128, 16], F32)
        onesm = SB("onesm", [128, 128], BF16)
        BD = SB("BD", [128, 128], BF16)
        epsb = SB("epsb", [128, 1], F32)
        prot = SB("prot", [128, 128], F32)
        ps = [es.enter_context(nc.psum_tensor("ps%d" % i, [128, 512], F32)) for i in range(7)]
        psb = es.enter_context(nc.psum_tensor("psb", [128, 1024], BF16))

        sc.dma("sp", "c0", [(cv[:, :], cv_in[:, :]), (prot[:, :], prot_in[:, :])], writes=["cv", "prot"])
        sc.op("dve", lambda e: e.memset(onesm[:, :], 1.0 / 1024.0), writes=["ones"])
        sc.op("dve", lambda e: e.memset(epsb[:, :], EPS), writes=["eps"])
        sc.op("dve", lambda e: e.memset(BD[:, :], 0.0), writes=["BD"])
        sc.op("dve", lambda e: e.memset(BD[0:64, 0:64], 1.0 / 64.0), reads=["BD"], writes=["BD"])
        sc.op("dve", lambda e: e.memset(BD[64:128, 64:128], 1.0 / 64.0), reads=["BD"], writes=["BD"])
        sc.op("dve", lambda e: e.memset(V1[:, :, :, 64:65], 1.0), writes=["V1ones"])
        sc.op("dve", lambda e: e.memset(VC1[:, :, 64:65], 1.0), writes=["VC1ones"])

        with ExitStack() as ph:
            PB = lambda name, shape, dt: ph.enter_context(nc.sbuf_tensor("sb_" + name, shape, dt))
            wfm = PB("wfm", [128, 8, 768], BF16)
            wtm = PB("wtm", [128, 8, 140], BF16)
            xt_ = PB("xt_", [128, 8, NTK], F32)
            xt = [xt_, xt_]
            sq1_ = PB("sq_", [128, 8, NTK], BF16)
            sq = [sq1_, sq1_]
            rs = [PB("rs%d" % i, [128, NTK], F32) for i in range(2)]
            hb = [PB("hb%d" % i, [128, 8, NTK], BF16) for i in range(2)]
            csC = [PB("csC%d" % i, [128, NTK], F32) for i in range(2)]
            csS = [PB("csS%d" % i, [128, NTK], F32) for i in range(2)]
            sq2 = [PB("sq2%d" % i, [128, NTK], BF16) for i in range(2)]
            rt = [PB("rt%d" % i, [128, NTK], F32) for i in range(2)]
            qn = [PB("qn%d" % i, [128, NTK], F32) for i in range(2)]
            r1_ = PB("r1_", [128, NTK], F32)
            r2_ = PB("r2_", [128, NTK], F32)
            r1 = [r1_, r1_]
            r2 = [r2_, r2_]
            xv = xin.rearrange("(kc p) t -> p kc t", p=128)
            sc.dma("pool", "wfm", [(wfm[:, :, :], wfm_in.rearrange("(kc p) n -> p kc n", p=128)),
                                   (wtm[:, :, :], wtm_in.rearrange("(kc p) n -> p kc n", p=128))], writes=["wfm"])

            def load_x(tt):
                k = tt % 2
                tok = slice(tt * NTK, (tt + 1) * NTK)
                sc.dma("sp", "xt0", [(xt[k][:, :, :], xv[:, :, tok])], writes=[("xt", 0)])
                sc.dma("sp", "cs%d" % k, [(csC[k][64:80, :], ropeC_in[:, tok]), (csS[k][64:80, :], ropeS_in[:, tok])],
                       writes=[("cs", k)])

            load_x(0)
            cbi = 0
            for tt in range(NTT):
                k = tt % 2
                tok = slice(tt * NTK, (tt + 1) * NTK)
                sc.op("act", lambda e: e.activation(out=sq[k][:, :, :], in_=xt[k][:, :, :], func=AF.Square),
                      reads=[("xt", 0)], writes=[("sq", 0)])
                sc.mm(ps[0][:, :], [(onesm[:, :], sq[k][:, kc, :]) for kc in range(8)],
                      reads=[("sq", 0), "ones"], writes=[("ps", 0)])
                sc.op("act", lambda e: e.activation(out=rs[k][:, :], in_=ps[0][:, :], func=AF.Sqrt, bias=epsb[:, 0:1], scale=1.0),
                      reads=[("ps", 0), "eps"], writes=[("rs", k)])
                sc.op("dve", lambda e: e.reciprocal(out=rs[k][:, :], in_=rs[k][:, :]), reads=[("rs", k)], writes=[("rs", k)])
                for kc in range(8):
                    sc.op("dve", lambda e: e.scalar_tensor_tensor(
                        out=hb[k][:, kc, :], in0=xt[k][:, kc, :], scalar=cv[:, kc:kc + 1], in1=rs[k][:, :],
                        op0=ALU.mult, op1=ALU.mult), reads=[("xt", 0), ("rs", k), "cv"], writes=[("hb", k)])
                if tt + 1 < NTT:
                    load_x(tt + 1)
                for sub in range(4):
                    kt = tt * 4 + sub
                    pt = 1 + sub % 2
                    sc.mm(ps[pt][:, 0:140], [(hb[k][:, kc, sub * 128:(sub + 1) * 128], wtm[:, kc, :]) for kc in range(8)],
                          reads=[("hb", k), "wfm"], writes=[("ps", pt)])
                    sc.op("act", lambda e: e.activation(out=V1[:, kt, :, 0:64],
                                                        in_=ps[pt][:, 0:128].rearrange("p (a b) -> p a b", a=2), func=AF.Copy),
                          reads=[("ps", pt)], writes=[("V1", kt)])
                    sc.op("act", lambda e: e.activation(out=G[:, kt, :], in_=ps[pt][:, 128:140], func=AF.Sigmoid),
                          reads=[("ps", pt)], writes=[("G", kt)])
                for cb in range(6):
                    j = cbi % 2
                    cbi += 1
                    P1, P2 = 3 + j, 5 + j
                    sc.mm(ps[P1][:, :], [(wfm[:, kc, cb * 128:(cb + 1) * 128], hb[k][:, kc, :]) for kc in range(8)],
                          reads=[("hb", k), "wfm"], writes=[("ps", P1)])
                    sc.op("act", lambda e: e.activation(out=sq2[j][:, :], in_=ps[P1][:, :], func=AF.Square),
                          reads=[("ps", P1)], writes=[("sq2", j)])
                    sc.mm(ps[P2][:, :], [(BD[:, :], sq2[j][:, :])], reads=[("sq2", j), "BD"], writes=[("ps", P2)])
                    sc.op("act", lambda e: e.activation(out=rt[j][:, :], in_=ps[P2][:, :], func=AF.Sqrt, bias=epsb[:, 0:1], scale=1.0),
                          reads=[("ps", P2), "eps"], writes=[("rt", j)])
                    sc.op("dve", lambda e: e.reciprocal(out=rt[j][:, :], in_=rt[j][:, :]), reads=[("rt", j)], writes=[("rt", j)])
                    gcol = cv[:, 8 + cb:9 + cb]
                    if cb < 4:
                        sc.op("dve", lambda e: e.scalar_tensor_tensor(
                            out=qn[j][:, :], in0=ps[P1][:, :], scalar=gcol, in1=rt[j][:, :], op0=ALU.mult, op1=ALU.mult),
                            reads=[("ps", P1), ("rt", j), "cv"], writes=[("qn", j)])
                    else:
                        dst = KA if cb == 4 else KB
                        sc.op("act", lambda e: e.activation(out=dst[0:64, tok], in_=ps[P1][0:64, :], func=AF.Copy),
                              reads=[("ps", P1)], writes=[("KAB", cb, tt, 0)])
                        sc.op("dve", lambda e: e.memset(qn[j][0:64, :], 0.0), writes=[("qn", j)], reads=[("qn", j)])
                        sc.op("dve", lambda e: e.scalar_tensor_tensor(
                            out=qn[j][64:128, :], in0=ps[P1][64:128, :], scalar=gcol[64:128, :], in1=rt[j][64:128, :],
                            op0=ALU.mult, op1=ALU.mult),
                            reads=[("ps", P1), ("rt", j), "cv", ("qn", j)], writes=[("qn", j)])
                    PR = 0
                    sc.mm(ps[PR][:, :], [(prot[:, :], qn[j][:, :])], reads=[("qn", j), "prot"], writes=[("ps", PR)])
                    sc.op("dve", lambda e: e.tensor_tensor(out=r1[j][64:80, :], in0=qn[j][64:80, :], in1=csC[k][64:80, :], op=ALU.mult),
                          reads=[("qn", j), ("cs", k)], writes=[("r1", 0)])
                    sc.op("dve", lambda e: e.tensor_tensor(out=r2[j][64:80, :], in0=ps[PR][64:80, :], in1=csS[k][64:80, :], op=ALU.mult),
                          reads=[("ps", PR), ("cs", k)], writes=[("r2", 0)])
                    sc.op("dve", lambda e: e.tensor_tensor(out=qn[j][64:80, :], in0=r1[j][64:80, :], in1=r2[j][64:80, :], op=ALU.add),
                          reads=[("r1", 0), ("r2", 0), ("qn", j), ("ps", PR)], writes=[("qn", j)])
                    if cb < 4:
                        sc.op("act", lambda e: e.activation(out=Q[0:64, cb, tok], in_=qn[j][0:64, :], func=AF.Copy),
                              reads=[("qn", j)], writes=[("Q", cb, tt, 0)])
                        sc.op("pool", lambda e: e.tensor_copy(out=Q[64:128, cb, tok], in_=qn[j][64:128, :]),
                              reads=[("qn", j)], writes=[("Q", cb, tt, 1)])
                    else:
                        dst = KA if cb == 4 else KB
                        sc.op("pool", lambda e: e.tensor_copy(out=dst[64:128, tok], in_=qn[j][64:128, :]),
                              reads=[("qn", j)], writes=[("KAB", cb, tt, 1)])
            sc.barrier()

        with ExitStack() as ph:
            PB = lambda name, shape, dt: ph.enter_context(nc.sbuf_tensor("sb_" + name, shape, dt))
            w1 = [PB("cw1%d" % i, [64, 32, 256], BF16) for i in range(2)]
            w2 = [PB("cw2%d" % i, [128, 2, 64], BF16) for i in range(2)]
            peT = [PB("peT%d" % i, [64, 32], BF16) for i in range(2)]
            hid = [PB("hid%d" % i, [128, 2, 512], BF16) for i in range(2)]
            bia = [PB("bia%d" % i, [128, 2], F32) for i in range(2)]
            ksq = PB("ksq", [64, 512], BF16)
            krt = PB("krt", [64, 512], F32)
            for i, (a1, a2, ap_) in enumerate([(ckw1_in, ckw2_in, pek_in), (cvw1_in, cvw2_in, pev_in)]):
                sc.dma("pool", "cw%d" % i, [
                    (w1[i][:, :, :], a1.rearrange("(l d) h -> d l h", d=64)),
                    (w2[i][:, :, :], a2.rearrange("(hc p) d -> p hc d", p=128)),
                    (peT[i][:, :], ap_[:, :])], writes=[("cw", i)])
                sc.op("dve", lambda e: e.memset(hid[i][:, :, 511:512], 0.0), writes=[("hid", i)])
            for i in range(2):
                src = KA if i == 0 else KB
                for hc in range(2):
                    sc.mm(ps[0][:, hc:hc + 1], [(w1[i][:, l, hc * 128:(hc + 1) * 128], peT[i][:, l:l + 1]) for l in range(32)],
                          reads=[("cw", i)], writes=[("ps", 0, hc)])
                    sc.op("act", lambda e: e.activation(out=bia[i][:, hc:hc + 1], in_=ps[0][:, hc:hc + 1], func=AF.Copy),
                          reads=[("ps", 0, hc)], writes=[("bia", i, hc)])
                    pb_ = 1 + hc
                    sc.mm(ps[pb_][:, 0:NCMP],
                          [(w1[i][:, l, hc * 128:(hc + 1) * 128], src[0:64, l:l + 16 * (NCMP - 1) + 1:16]) for l in range(32)],
                          reads=[("cw", i)], writes=[("ps", pb_)])
                    sc.op("act", lambda e: e.activation(out=hid[i][:, hc, 0:NCMP], in_=ps[pb_][:, 0:NCMP], func=AF.Silu,
                                                        bias=bia[i][:, hc:hc + 1], scale=1.0),
                          reads=[("ps", pb_), ("bia", i, hc), ("hid", i)], writes=[("hid", i)])
            sc.mm(ps[3][0:64, :], [(w2[0][:, hc, :], hid[0][:, hc, :]) for hc in range(2)],
                  reads=[("hid", 0), ("cw", 0)], writes=[("ps", 3)])
            sc.op("act", lambda e: e.activation(out=ksq[:, :], in_=ps[3][0:64, :], func=AF.Square), reads=[("ps", 3)], writes=["ksq"])
            sc.mm(ps[4][0:64, :], [(BD[0:64, 0:64], ksq[:, :])], reads=["ksq", "BD"], writes=[("ps", 4)])
            sc.op("act", lambda e: e.activation(out=krt[:, :], in_=ps[4][0:64, :], func=AF.Sqrt, bias=epsb[0:64, 0:1], scale=1.0),
                  reads=[("ps", 4), "eps"], writes=["krt"])
            sc.op("dve", lambda e: e.reciprocal(out=krt[:, :], in_=krt[:, :]), reads=["krt"], writes=["krt"])
            sc.op("dve", lambda e: e.scalar_tensor_tensor(out=KCMP[0:64, :], in0=ps[3][0:64, :], scalar=cv[0:64, 14:15], in1=krt[:, :],
                                                          op0=ALU.mult, op1=ALU.mult),
                  reads=[("ps", 3), "krt", "cv"], writes=["KCMP"])
            for cc in range(4):
                pb_ = 5 + cc % 2
                sc.mm(ps[pb_][:, 0:64], [(hid[1][:, hc, cc * 128:(cc + 1) * 128], w2[1][:, hc, :]) for hc in range(2)],
                      reads=[("hid", 1), ("cw", 1)], writes=[("ps", pb_)])
                sc.op("act", lambda e: e.activation(out=VC1[:, cc, 0:64], in_=ps[pb_][:, 0:64], func=AF.Copy),
                      reads=[("ps", pb_)], writes=[("VC1", cc)])
            sc.barrier()

        with ExitStack() as ph:
            PB = lambda name, shape, dt: ph.enter_context(nc.sbuf_tensor("sb_" + name, shape, dt))
            ident = PB("ident", [128, 128], BF16)
            tri = PB("tri", [128, 128], BF16)
            tric = PB("tric", [128, 128], BF16)
            cmask = PB("cmask", [128, 17, 128], BF16)
            cmap = PB("cmap", [128, 4, 128], BF16)
            ext = PB("ext", [128, S], BF16)
            t1 = PB("t1", [128, 256], F32)
            t2 = PB("t2", [128, 256], F32)
            NE = 8
            Eb = [PB("E%d" % i, [128, 512], BF16) for i in range(NE)]
            impb = PB("impb", [128, 128], F32)
            imp2 = PB("imp2", [128, 128], F32)
            imp3 = PB("imp3", [128, 128], F32)
            m8 = PB("m8", [128, 16], F32)
            selb = PB("selb", [128, 128], BF16)
            selT = PB("selT", [128, 128], BF16)
            mkd = PB("mkd", [128, 128], BF16)
            mks = [PB("mks%d" % i, [128, 512], BF16) for i in range(2)]
            rz = PB("rz", [128, 3, 4], F32)
            cf = PB("cf", [128, 3, 4], F32)
            osb = [PB("osb%d" % i, [128, 4, 64], F32) for i in range(2)]
            otm = PB("otm", [128, 4, 64], F32)
            sc.dma("pool", "c3", [(ident[:, :], ident_in[:, :]), (tri[:, :], tri_in[:, :]), (tric[:, :], tric_in[:, :]),
                                  (cmask[:, :, :], cmask_in[:, :, :]), (cmap[:, :, :], cmap_in[:, :, :])], writes=["c3"])
            sc.dma("pool", "c3b", [(ext[:, 0:4096], ext_in[:, 0:4096]), (ext[:, 4096:S], ext_in[:, 4096:S])], writes=["ext"])
            sc.dma("sp", "c3c", [(t1[:, :], t1_in[:, :]), (t2[:, :], t2_in[:, :])], writes=["t12"])
            PS_S = (0, 1)
            PS_OC, PS_OS, PS_OW, PS_IM, PS_MK = 2, 3, 4, 5, 6
            ecnt = [0]
            scnt = [0]
            mkc = [0]
            MKB = (6, 5)

            SBATCH = 2
            pend = []

            def emitSE(u):
                lhsT, q_rhs, kreads, mask_fn, vrhs, vreads, po, first, last_ = u["a"]
                pS = PS_S[scnt[0] % 2]; scnt[0] += 1
                ei = ecnt[0] % NE; ecnt[0] += 1
                E = Eb[ei]
                u["E"] = E; u["ei"] = ei
                sc.mm(ps[pS][:, :], [(lhsT, q_rhs)], reads=kreads, writes=[("ps", pS)])
                sc.op("act", lambda e: e.activation(out=E[:, :], in_=ps[pS][:, :], func=AF.Exp, scale=0.125),
                      reads=[("ps", pS)], writes=[("E", ei)])
                if mask_fn is not None:
                    mask_fn(E, ei)

            def emitPV(u):
                lhsT, q_rhs, kreads, mask_fn, vrhs, vreads, po, first, last_ = u["a"]
                E, ei = u["E"], u["ei"]
                waits = sc._collect("pe", [("E", ei)] + vreads, [("ps", po)])
                sc._emit_waits("pe", waits)
                for r in range(4):
                    inst = nc.tensor.matmul(ps[po][:, r * 65:(r + 1) * 65], lhsT=E[:, r * 128:(r + 1) * 128], rhs=vrhs,
                                            start=(first and r == 0), stop=(last_ and r == 3), skip_group_check=True)
                sc.pe_inc(inst)
                sc._record(("pe", sc.cnt["pe"], "pe"), [("E", ei)] + vreads, [("ps", po)])

            mode = ["full"]
            sb = [0]

            def pe_mode(m):
                if mode[0] != m:
                    if sc.waited["pe"].get("pe", 0) < sc.cnt["pe"]:
                        sc._emit_waits("pe", {"pe": sc.cnt["pe"]})
                    mode[0] = m

            def push(*a):
                u = {"a": a}
                pe_mode("tile")
                emitSE(u)
                pend.append(u)
                sb[0] += 1
                if sb[0] >= SBATCH:
                    sb[0] = 0
                    if len(pend) > SBATCH:
                        pe_mode("full")
                        while len(pend) > SBATCH:
                            emitPV(pend.pop(0))
                return u

            def flush():
                sb[0] = 0
                if pend:
                    pe_mode("full")
                while pend:
                    emitPV(pend.pop(0))

            def bmask(mask_ap_fn, mreads):
                def f(E, ei):
                    sc.op("dve", lambda e: e.tensor_tensor(
                        out=E[:, :].rearrange("p (r t) -> p r t", r=4), in0=E[:, :].rearrange("p (r t) -> p r t", r=4),
                        in1=mask_ap_fn(), op=ALU.mult), reads=[("E", ei)] + mreads, writes=[("E", ei)])
                return f

            for qt in range(NQT):
                qtok = slice(qt * 128, (qt + 1) * 128)
                qn_rhs = Q[0:64, :, qtok]
                qr_rhs = Q[64:128, :, qtok]
                qreads = []
                ncn = min(4, (8 * qt + 7 + 127) // 128)
                Es = []
                for cc in range(ncn):
                    m = 8 * qt - 128 * cc
                    mf = None
                    if 0 <= m <= 128:
                        mi = m // 8
                        mf = bmask(lambda mi=mi: cmask[:, mi, :].unsqueeze(1).broadcast_to([128, 4, 128]), ["c3"])
                    u = push(KCMP[0:64, cc * 128:(cc + 1) * 128], qn_rhs, ["KCMP"] + qreads, mf,
                             VC1[:, cc, :], [("VC1", c_) for c_ in range(4)] + ["VC1ones"], PS_OC, cc == 0, cc == ncn - 1)
                    Es.append((u, cc))
                flush()
                pe_mode("full")
                nmm = len(Es) * 4
                i_ = 0
                for (u, cc) in Es:
                    E, ei = u["E"], u["ei"]
                    for r in range(4):
                        waits = sc._collect("pe", [("E", ei), "c3"], [("mkb", 1)])
                        sc._emit_waits("pe", waits)
                        inst = nc.tensor.matmul(ps[PS_IM][:, r * 128:(r + 1) * 128], lhsT=E[:, r * 128:(r + 1) * 128],
                                                rhs=cmap[:, cc, :], start=(i_ == 0), stop=(i_ == nmm - 1), skip_group_check=True)
                        i_ += 1
                        sc._record(("pe", sc.cnt["pe"] + 1, "pe"), [("E", ei), "c3"], [("mkb", 1)])
                sc.pe_inc(inst)
                sc.op("dve", lambda e: e.tensor_scalar_max(
                    out=rz[:, 0, :], in0=ps[PS_OC][:, 0:260].rearrange("p (r c) -> p r c", r=4)[:, :, 64], scalar1=1e-30),
                    reads=[("ps", PS_OC)], writes=[("rz", 0)])
                sc.op("dve", lambda e: e.reciprocal(out=rz[:, 0, :], in_=rz[:, 0, :]), reads=[("rz", 0)], writes=[("rz", 0)])
                for r in range(4):
                    if r == 0:
                        sc.op("dve", lambda e: e.tensor_scalar(out=impb[:, :], in0=ps[PS_IM][:, 0:128], scalar1=rz[:, 0, 0:1],
                                                               scalar2=None, op0=ALU.mult),
                              reads=[("mkb", 1), ("rz", 0)], writes=["impb"])
                    else:
                        sc.op("dve", lambda e: e.scalar_tensor_tensor(
                            out=impb[:, :], in0=ps[PS_IM][:, r * 128:(r + 1) * 128], scalar=rz[:, 0, r:r + 1], in1=impb[:, :],
                            op0=ALU.mult, op1=ALU.add), reads=[("mkb", 1), ("rz", 0), "impb"], writes=["impb"])
                off = 127 - 2 * qt
                sc.op("dve", lambda e: e.tensor_tensor(out=imp2[:, :], in0=impb[:, :], in1=t1[:, off:off + 128], op=ALU.mult),
                      reads=["impb", "t12"], writes=["imp2"])
                sc.op("dve", lambda e: e.tensor_tensor(out=imp2[:, :], in0=imp2[:, :], in1=t2[:, off:off + 128], op=ALU.add),
                      reads=["imp2", "t12"], writes=["imp2"])
                sc.op("dve", lambda e: e.memset(imp2[:, 0:1], 1e4), reads=["imp2"], writes=["imp2"])
                sc.op("dve", lambda e: e.max(out=m8[:, 0:8], in_=imp2[:, :]), reads=["imp2"], writes=["m8a"])
                sc.op("dve", lambda e: e.match_replace(out=imp3[:, :], in_to_replace=m8[:, 0:8], in_values=imp2[:, :], imm_value=-1e30),
                      reads=["imp2", "m8a"], writes=["imp3"])
                sc.op("dve", lambda e: e.max(out=m8[:, 8:16], in_=imp3[:, :]), reads=["imp3"], writes=["m8b"])
                sc.op("dve", lambda e: e.tensor_scalar(out=selb[:, :], in0=imp2[:, :], scalar1=m8[:, 15:16], scalar2=None, op0=ALU.is_ge),
                      reads=["imp2", "m8b"], writes=["selb"])
                k0 = max(0, qt - 4)
                for kt in range(k0, qt + 1):
                    ktok = slice(kt * 128, (kt + 1) * 128)
                    if kt == qt:
                        mf = bmask(lambda: tri[:, :].unsqueeze(1).broadcast_to([128, 4, 128]), ["c3"])
                    elif kt == qt - 4:
                        mf = bmask(lambda: tric[:, :].unsqueeze(1).broadcast_to([128, 4, 128]), ["c3"])
                    else:
                        mf = None
                    push(KB[64:128, ktok], qr_rhs, qreads, mf,
                         V1[:, kt, 1, :], [("V1", kt), "V1ones"], PS_OW, kt == k0, kt == qt)
                mode[0] = "x"
                pe_mode("full")
                waits = sc._collect("pe", ["selb", "c3"], ["psb"])
                sc._emit_waits("pe", waits)
                inst = nc.tensor.transpose(psb[:, 0:128], selb[:, :], ident[:, :])
                sc.pe_inc(inst)
                sc._record(("pe", sc.cnt["pe"], "pe"), ["selb", "c3"], ["psb"])
                sc.op("act", lambda e: e.activation(out=selT[:, :], in_=psb[:, 0:128], func=AF.Copy), reads=["psb"], writes=["selT"])
                for kt in range(qt + 1):
                    kb = kt % 4
                    if kb == 0:
                        nk = min(4, qt + 1 - kt)
                        pe_mode("full")
                        mkc[0] += 1
                        mb = MKB[mkc[0] % 2]
                        mkey = ("mkb", mkc[0] % 2)
                        for u_ in range(nk):
                            sc.mm(ps[mb][:, u_ * 128:(u_ + 1) * 128],
                                  [(ext[:, (kt + u_) * 128:(kt + u_ + 1) * 128], selT[:, :])],
                                  reads=["ext", "selT"], writes=[mkey])
                        mks_ = mks[mkc[0] % 2]
                        skey = ("mks", mkc[0] % 2)
                        sc.op("act", lambda e: e.activation(out=mks_[:, 0:nk * 128], in_=ps[mb][:, 0:nk * 128], func=AF.Copy),
                              reads=[mkey], writes=[skey])
                    ktok = slice(kt * 128, (kt + 1) * 128)
                    if kt == qt:
                        sc.op("dve", lambda e: e.tensor_tensor(out=mkd[:, :], in0=mks_[:, kb * 128:(kb + 1) * 128], in1=tri[:, :],
                                                               op=ALU.mult), reads=[skey, "c3"], writes=["mkd"])
                        mf = bmask(lambda: mkd[:, :].unsqueeze(1).broadcast_to([128, 4, 128]), ["mkd"])
                    else:
                        mf = bmask(lambda kb=kb, mks_=mks_: mks_[:, kb * 128:(kb + 1) * 128].unsqueeze(1).broadcast_to([128, 4, 128]),
                                   [skey])
                    push(KA[64:128, ktok], qr_rhs, qreads, mf,
                         V1[:, kt, 0, :], [("V1", kt), "V1ones"], PS_OS, kt == 0, kt == qt)
                flush()
                for bi, po in ((1, PS_OS), (2, PS_OW)):
                    sc.op("dve", lambda e: e.tensor_scalar_max(
                        out=rz[:, bi, :], in0=ps[po][:, 0:260].rearrange("p (r c) -> p r c", r=4)[:, :, 64], scalar1=1e-30),
                        reads=[("ps", po)], writes=[("rz", bi)])
                    sc.op("dve", lambda e: e.reciprocal(out=rz[:, bi, :], in_=rz[:, bi, :]), reads=[("rz", bi)], writes=[("rz", bi)])
                sc.op("dve", lambda e: e.tensor_tensor(out=cf[:, :, :], in0=rz[:, :, :],
                                                       in1=G[:, qt, :].rearrange("p (r b) -> p b r", b=3), op=ALU.mult),
                      reads=[("rz", 0), ("rz", 1), ("rz", 2), ("G", qt)], writes=["cf"])
                ob = osb[qt % 2]
                for bi, po in ((0, PS_OC), (1, PS_OS), (2, PS_OW)):
                    dst = ob if bi == 0 else otm
                    sc.op("dve", lambda e: e.tensor_tensor(
                        out=dst[:, :, :], in0=ps[po][:, 0:260].rearrange("p (r c) -> p r c", r=4)[:, :, 0:64],
                        in1=cf[:, bi, :].unsqueeze(2).broadcast_to([128, 4, 64]), op=ALU.mult),
                        reads=[("ps", po), "cf"], writes=[("osb", qt % 2) if bi == 0 else "otm"])
                    if bi > 0:
                        sc.op("dve", lambda e: e.tensor_tensor(out=ob[:, :, :], in0=ob[:, :, :], in1=otm[:, :, :], op=ALU.add),
                              reads=[("osb", qt % 2), "otm"], writes=[("osb", qt % 2)])
                sc.dma("sp", "oo%d" % (qt % 2), [(oout[qtok, :], ob[:, :, :].rearrange("p r c -> p (r c)"))],
                       reads=[("osb", qt % 2)])
            sc.barrier()
        sc.final_wait("sp")
        ok, stuck = sc.check_deadlock()
        if not ok:
            raise RuntimeError("deadlock in semaphore program: %r" % (stuck,))
    return nc


def nsa_host_inputs(P, j, li, xfull):
    consts = nsa_consts()
    w_in = np.asarray(P["nsa_w_in"][j], np.float32)
    offs = np.cumsum([0, 1024, 256, 256, 256, 256, 256, 256])
    oq, okc, ovc, oks, ovs, okw, ovw, ogl = [int(v) for v in offs]
    in_maps = []
    for c in range(NCORES):
        b, g = divmod(c, 4)
        cols = []
        for r in range(4):
            qc = w_in[:, oq + (g * 4 + r) * 64: oq + (g * 4 + r + 1) * 64]
            cols += [qc, qc]
        sl = lambda o: w_in[:, o + g * 64:o + (g + 1) * 64]
        cols += [sl(okc), sl(oks), sl(ovc), sl(okw)]
        wfm = np.ascontiguousarray(np.concatenate(cols, axis=1))
        wtm = np.ascontiguousarray(np.concatenate([sl(ovs), sl(ovw), w_in[:, ogl + g * 12: ogl + (g + 1) * 12]], axis=1))
        cvec = np.ones((128, 16), np.float32)
        cvec[:, 0:8] = fm(P["mix_norm_g"][li], 8)
        qg = np.asarray(P["nsa_q_norm"][j], np.float32)
        for r in range(4):
            cvec[0:64, 8 + r] = qg
            cvec[64:128, 8 + r] = qg
        cvec[64:128, 12] = np.asarray(P["nsa_ks_norm"][j], np.float32)
        cvec[64:128, 13] = np.asarray(P["nsa_kw_norm"][j], np.float32)
        cvec[0:64, 14] = np.asarray(P["nsa_kc_norm"][j], np.float32)
        m = {"xT": np.ascontiguousarray(xfull[b].T), "wfm": wfm, "wtm": wtm, "cvec": cvec,
             "pekT": np.ascontiguousarray(np.asarray(P["nsa_pe_k"][j], np.float32).T),
             "pevT": np.ascontiguousarray(np.asarray(P["nsa_pe_v"][j], np.float32).T),
             "ck_w1": np.ascontiguousarray(P["nsa_ck_w1"][j], np.float32), "ck_w2": np.ascontiguousarray(P["nsa_ck_w2"][j], np.float32),
             "cv_w1": np.ascontiguousarray(P["nsa_cv_w1"][j], np.float32), "cv_w2": np.ascontiguousarray(P["nsa_cv_w2"][j], np.float32)}
        m.update(consts)
        in_maps.append(m)
    return in_maps


def run_nsa(P, j, li, xfull):
    if "nsa" not in _NC_CACHE:
        _NC_CACHE["nsa"] = build_nsa()
    nc = _NC_CACHE["nsa"]
    in_maps = nsa_host_inputs(P, j, li, xfull)
    res = run_bass_kernel_spmd(nc, in_maps, core_ids=list(range(NCORES)))
    o = np.zeros((B, S, D), np.float32)
    for c in range(NCORES):
        b, g = divmod(c, 4)
        o[b, :, g * 256:(g + 1) * 256] = res.results[c]["o"]
    return o


def kernel(**inputs):
    P = {k: np.asarray(v) for k, v in inputs.items()}
    x = np.ascontiguousarray(P["x"], np.float32)
    x = run_dense([("conv", (0, 0)), ("ffn", 0)], P, x)
    o = run_nsa(P, 0, 1, x)
    x = run_dense([("oproj", 0), ("ffn", 1), ("conv", (2, 1)), ("ffn", 2)], P, x, ofull=o)
    o = run_nsa(P, 1, 3, x)
    x = run_dense([("oproj", 1), ("ffn", 3)], P, x, ofull=o)
    return x.astype(np.float32)
```

```python
import numpy as np
from contextlib import ExitStack
import concourse.bass as bass
import concourse.mybir as mybir
from concourse.bass_utils import run_bass_kernel_spmd

F32 = mybir.dt.float32
BF16 = mybir.dt.bfloat16
ALU = mybir.AluOpType
AF = mybir.ActivationFunctionType
AX = mybir.AxisListType

D = 1024
B = 2
S = 8192
DEPTH = 4
NCORES = 8
TOK = 2048
HALO = 52
TT = TOK + HALO
TW = 420
NT = TT // TW
DFF = 2816
NFC = DFF // 128
CW = 31
EPS = 1e-6
GSZ = 4
GROUPS = [(j, min(GSZ, NFC - j)) for j in range(0, NFC, GSZ)]

HD = 64
NKV = 4
NH = 16
PROJ = 2608


class Sched:
    def __init__(self, nc, es):
        self.nc = nc
        self.es = es
        self.E = {"pe": nc.tensor, "act": nc.scalar, "dve": nc.vector, "pool": nc.gpsimd, "sp": nc.sync}
        self.semh = {}
        self.cnt = {}
        for e in self.E:
            self.semh[e] = es.enter_context(nc.semaphore("s_" + e))
            self.cnt[e] = 0
        self.waited = {e: {} for e in self.E}
        self.res = {}
        self.pending_noinc = {e: False for e in self.E}
        self.log = {e: [] for e in self.E}

    def dsem(self, name):
        k = "d_" + name
        if k not in self.semh:
            self.semh[k] = self.es.enter_context(self.nc.semaphore(k))
            self.cnt[k] = 0
        return k

    def _collect(self, eng, reads, writes):
        waits = {}

        def need(dep, same_ok):
            semk, val, deng = dep
            if deng == eng and (not same_ok or eng == "pe"):
                return
            if self.waited[eng].get(semk, 0) >= val:
                return
            if waits.get(semk, 0) < val:
                waits[semk] = val

        for r in reads:
            st = self.res.get(r)
            if st is not None and st[0] is not None:
                need(st[0], True)
        for w in writes:
            st = self.res.get(w)
            if st is not None:
                if st[0] is not None:
                    need(st[0], True)
                for rd in st[1].values():
                    need(rd, False)
        return waits

    def _emit_waits(self, eng, waits):
        E = self.E[eng]
        for semk, val in waits.items():
            E.wait_ge(self.semh[semk], val)
            self.waited[eng][semk] = val
            self.log[eng].append(("w", semk, val))

    def _record(self, me, reads, writes):
        for r in reads:
            st = self.res.setdefault(r, [None, {}])
            st[1][me[2]] = me
        for w in writes:
            st = self.res.setdefault(w, [None, {}])
            st[0] = me
            st[1] = {}

    def op(self, eng, fn, reads=(), writes=(), inc=True):
        waits = self._collect(eng, reads, writes)
        self._emit_waits(eng, waits)
        inst = fn(self.E[eng])
        if inc:
            self.cnt[eng] += 1
            inst.then_inc(self.semh[eng], 1)
            self.log[eng].append(("i", eng, 1))
            me = (eng, self.cnt[eng], eng)
            self.pending_noinc[eng] = False
        else:
            me = (eng, self.cnt[eng] + 1, eng)
            self.pending_noinc[eng] = True
        self._record(me, reads, writes)
        return inst

    def pe_inc(self, inst):
        self.cnt["pe"] += 1
        inst.then_inc(self.semh["pe"], 1)
        self.log["pe"].append(("i", "pe", 1))

    def mm(self, out, pairs, reads, writes):
        waits = self._collect("pe", reads, writes)
        self._emit_waits("pe", waits)
        n = len(pairs)
        inst = None
        for i, (lhsT, rhs) in enumerate(pairs):
            inst = self.nc.tensor.matmul(out, lhsT=lhsT, rhs=rhs, start=(i == 0), stop=(i == n - 1))
        self.pe_inc(inst)
        me = ("pe", self.cnt["pe"], "pe")
        self._record(me, reads, writes)

    def mm_part(self, out, lhsT, rhs, start, stop, reads, writes, last):
        waits = self._collect("pe", reads, writes)
        self._emit_waits("pe", waits)
        inst = self.nc.tensor.matmul(out, lhsT=lhsT, rhs=rhs, start=start, stop=stop)
        if last:
            self.pe_inc(inst)
            me = ("pe", self.cnt["pe"], "pe")
        else:
            me = ("pe", self.cnt["pe"] + 1, "pe")
        self._record(me, reads, writes)

    def dma(self, q, dname, items, reads=(), writes=()):
        semk = self.dsem(dname)
        waits = self._collect(q, reads, writes)
        if self.cnt[semk] > 0 and self.waited[q].get(semk, 0) < self.cnt[semk]:
            waits[semk] = self.cnt[semk]
        self._emit_waits(q, waits)
        for (o, i) in items:
            inst = self.E[q].dma_start(out=o, in_=i)
            self.cnt[semk] += 16
            inst.then_inc(self.semh[semk], 16)
            self.log[q].append(("i", semk, 16))
        me = (semk, self.cnt[semk], "dma:" + semk)
        self._record(me, reads, writes)

    def check_deadlock(self):
        pos = {e: 0 for e in self.log}
        val = {}
        progress = True
        while progress:
            progress = False
            for e, lg in self.log.items():
                while pos[e] < len(lg):
                    kind, semk, v = lg[pos[e]]
                    if kind == "w":
                        if val.get(semk, 0) >= v:
                            pos[e] += 1; progress = True
                        else:
                            break
                    else:
                        val[semk] = val.get(semk, 0) + v
                        pos[e] += 1; progress = True
        stuck = {e: (pos[e], len(lg), lg[pos[e]] if pos[e] < len(lg) else None, val.get(lg[pos[e]][1]) if pos[e] < len(lg) else None)
                 for e, lg in self.log.items()}
        return all(pos[e] == len(lg) for e, lg in self.log.items()), stuck

    def barrier(self):
        comp = ["pe", "act", "dve", "pool", "sp"]
        for e in comp:
            waits = {}
            for k, v in self.cnt.items():
                if k == e or v == 0:
                    continue
                if self.waited[e].get(k, 0) < v:
                    waits[k] = v
            self._emit_waits(e, waits)
        self.res = {}

    def final_wait(self, eng="sp"):
        waits = {}
        for k, v in self.cnt.items():
            if k == eng or v == 0:
                continue
            if self.waited[eng].get(k, 0) < v:
                waits[k] = v
        self._emit_waits(eng, waits)


def fm(vec, nch):
    return np.ascontiguousarray(np.asarray(vec, np.float32).reshape(nch, 128).T)


class VecPack:
    def __init__(self):
        self.off = {}
        self.n = 0
        self.arrs = []

    def add(self, name, arr):
        arr = np.asarray(arr, np.float32)
        assert arr.shape[0] == 128
        arr = arr.reshape(128, -1)
        self.off[name] = (self.n, arr.shape[1])
        self.n += arr.shape[1]
        self.arrs.append(arr)

    def array(self):
        return np.ascontiguousarray(np.concatenate(self.arrs, axis=1))


def dense_vec_layout(phases):
    off = {}
    n = 0

    def add(name, w):
        nonlocal n
        off[name] = (n, w)
        n += w

    for pi, (kind, _) in enumerate(phases):
        p = "p%d_" % pi
        if kind == "conv":
            add(p + "g", 8); add(p + "b1", 16); add(p + "wdw", 8 * CW); add(p + "bdw", 8)
            add(p + "lng", 8); add(p + "lnb", 8); add(p + "b2", 8)
        elif kind == "ffn":
            add(p + "g", 8); add(p + "wdw", NFC * 3); add(p + "bdw", NFC)
        elif kind == "oproj":
            pass
    return off, n


def build_dense(phases):
    nc = bass.Bass("TRN2", target_bir_lowering=False)
    voff, nv = dense_vec_layout(phases)
    xin = nc.dram_tensor("xT", [D, TT], F32, kind="ExternalInput").ap()
    hm_in = nc.dram_tensor("hmask", [128, HALO], F32, kind="ExternalInput").ap()
    cv_in = nc.dram_tensor("cvec", [128, max(nv, 1)], F32, kind="ExternalInput").ap()
    yout = nc.dram_tensor("yT", [D, TOK], F32, kind="ExternalOutput").ap()
    wd = {}
    for pi, (kind, _) in enumerate(phases):
        p = "p%d_" % pi
        if kind == "conv":
            wd[p + "w1"] = nc.dram_tensor(p + "w1", [D, 2 * D], F32, kind="ExternalInput").ap()
            wd[p + "w2"] = nc.dram_tensor(p + "w2", [D, D], F32, kind="ExternalInput").ap()
        elif kind == "ffn":
            wd[p + "wup"] = nc.dram_tensor(p + "wup", [D, 2 * DFF], F32, kind="ExternalInput").ap()
            wd[p + "wdn"] = nc.dram_tensor(p + "wdn", [DFF, D], F32, kind="ExternalInput").ap()
        elif kind == "oproj":
            wd[p + "wo"] = nc.dram_tensor(p + "wo", [D, D], F32, kind="ExternalInput").ap()
            wd[p + "oT"] = nc.dram_tensor(p + "oT", [D, TT], F32, kind="ExternalInput").ap()

    with ExitStack() as es:
        sc = Sched(nc, es)
        xT = es.enter_context(nc.sbuf_tensor("xTs", [128, 8, TT], F32))
        cv = es.enter_context(nc.sbuf_tensor("cv", [128, max(nv, 1)], F32))
        hmk = es.enter_context(nc.sbuf_tensor("hmk", [128, HALO], F32))
        onesm = es.enter_context(nc.sbuf_tensor("onesm", [128, 128], BF16))
        epsb = es.enter_context(nc.sbuf_tensor("epsb", [128, 1], F32))
        ps = [es.enter_context(nc.psum_tensor("ps%d" % i, [128, 512], F32)) for i in range(8)]

        def V(name, a=0, b=None):
            o, w = voff[name]
            if b is None:
                b = w
            return cv[:, o + a:o + b]

        xin_v = xin.rearrange("(kc p) t -> p kc t", p=128)
        for t in range(NT):
            sc.dma("sp", "xin%d" % t, [(xT[:, :, t * TW:(t + 1) * TW], xin_v[:, :, t * TW:(t + 1) * TW])],
                   writes=[("x", t)])
        sc.dma("sp", "cvl", [(cv[:, :], cv_in[:, :]), (hmk[:, :], hm_in[:, :])], writes=["cv"])
        sc.op("dve", lambda e: e.memset(onesm[:, :], 1.0 / 1024.0), writes=["ones"])
        sc.op("dve", lambda e: e.memset(epsb[:, :], EPS), writes=["eps"])

        def rms_stats(x_ap, reads_x, sq, rs, k, psb):
            sc.op("act", lambda e: e.activation(out=sq[:, :, :], in_=x_ap, func=AF.Square),
                  reads=reads_x, writes=[("sq", k)])
            sc.mm(ps[psb][:, :TW], [(onesm[:, :], sq[:, kc, :]) for kc in range(8)],
                  reads=[("sq", k), "ones"], writes=[("ps", psb)])
            sc.op("act", lambda e: e.activation(out=rs[:, :], in_=ps[psb][:, :TW], func=AF.Sqrt,
                                                bias=epsb[:, 0:1], scale=1.0),
                  reads=[("ps", psb), "eps"], writes=[("rs", k)])
            sc.op("dve", lambda e: e.reciprocal(out=rs[:, :], in_=rs[:, :]), reads=[("rs", k)], writes=[("rs", k)])

        for pi, (kind, _) in enumerate(phases):
            p = "p%d_" % pi
            with ExitStack() as ph:
                if kind == "conv":
                    w1 = ph.enter_context(nc.sbuf_tensor(p + "w1s", [128, 8, 2 * D], BF16))
                    w2 = ph.enter_context(nc.sbuf_tensor(p + "w2s", [128, 8, D], BF16))
                    sqb = [ph.enter_context(nc.sbuf_tensor(p + "sq%d" % i, [128, 8, TW], BF16)) for i in range(2)]
                    rsb = [ph.enter_context(nc.sbuf_tensor(p + "rs%d" % i, [128, TW], F32)) for i in range(2)]
                    hb = [ph.enter_context(nc.sbuf_tensor(p + "hb%d" % i, [128, 8, TW], BF16)) for i in range(2)]
                    uG = ph.enter_context(nc.sbuf_tensor(p + "uG", [128, 8, TW + CW - 1], F32))
                    yb = ph.enter_context(nc.sbuf_tensor(p + "yb", [128, 8, TW], F32))
                    sgb = [ph.enter_context(nc.sbuf_tensor(p + "sg%d" % i, [128, TW], F32)) for i in range(2)]
                    ybf = [ph.enter_context(nc.sbuf_tensor(p + "ybf%d" % i, [128, TW], BF16)) for i in range(2)]
                    ysq = [ph.enter_context(nc.sbuf_tensor(p + "ysq%d" % i, [128, TW], BF16)) for i in range(2)]
                    zb = ph.enter_context(nc.sbuf_tensor(p + "zb", [128, 8, TW], BF16))
                    mu = ph.enter_context(nc.sbuf_tensor(p + "mu", [128, TW], F32))
                    var = ph.enter_context(nc.sbuf_tensor(p + "var", [128, TW], F32))
                    nb = ph.enter_context(nc.sbuf_tensor(p + "nb", [128, TW], F32))
                    ntm = [ph.enter_context(nc.sbuf_tensor(p + "ntm%d" % i, [128, TW], F32)) for i in range(2)]
                    w1v = wd[p + "w1"].rearrange("(kc p) n -> p kc n", p=128)
                    w2v = wd[p + "w2"].rearrange("(kc p) n -> p kc n", p=128)
                    sc.dma("pool", "w1a", [(w1[:, 0:4, :], w1v[:, 0:4, :])], writes=["w1a"])
                    sc.dma("pool", "w1b", [(w1[:, 4:8, :], w1v[:, 4:8, :])], writes=["w1b"])
                    sc.dma("pool", "w2", [(w2[:, :, :], w2v[:, :, :])], writes=["w2"])
                    sc.op("dve", lambda e: e.memset(uG[:, :, 0:CW - 1], 0.0), writes=[("uG", c) for c in range(8)])
                    PS_ST, PS_A, PS_G, PS_MU, PS_E2 = 0, (1, 2), (3, 4), 5, 6
                    for t in range(NT):
                        cols = slice(t * TW, (t + 1) * TW)
                        k = t % 2
                        rms_stats(xT[:, :, cols], [("x", t)], sqb[k], rsb[k], k, PS_ST)
                        for kc in range(8):
                            sc.op("dve", lambda e: e.scalar_tensor_tensor(
                                out=hb[k][:, kc, :], in0=xT[:, kc, cols], scalar=V(p + "g", kc, kc + 1),
                                in1=rsb[k][:, :], op0=ALU.mult, op1=ALU.mult),
                                reads=[("x", t), ("rs", k), "cv"], writes=[("hb", k)])
                        for c2 in range(0, 8, 2):
                            for c in (c2, c2 + 1):
                                pa, pg = PS_A[c % 2], PS_G[c % 2]
                                sc.mm(ps[pa][:, :TW], [(w1[:, kc, c * 128:(c + 1) * 128], hb[k][:, kc, :]) for kc in range(8)],
                                      reads=[("hb", k), "w1a", "w1b"], writes=[("ps", pa)])
                                sc.mm(ps[pg][:, :TW], [(w1[:, kc, D + c * 128:D + (c + 1) * 128], hb[k][:, kc, :]) for kc in range(8)],
                                      reads=[("hb", k), "w1a", "w1b"], writes=[("ps", pg)])
                                sc.op("act", lambda e: e.activation(out=sgb[c % 2][:, :], in_=ps[pg][:, :TW], func=AF.Sigmoid,
                                                                    bias=V(p + "b1", 8 + c, 9 + c), scale=1.0),
                                      reads=[("ps", pg), "cv"], writes=[("sg", c % 2)])
                                sc.op("dve", lambda e: e.scalar_tensor_tensor(
                                    out=uG[:, c, CW - 1:], in0=ps[pa][:, :TW], scalar=V(p + "b1", c, c + 1),
                                    in1=sgb[c % 2][:, :], op0=ALU.add, op1=ALU.mult),
                                    reads=[("ps", pa), ("sg", c % 2), "cv"], writes=[("uG", c)])
                                if t == 0:
                                    sc.op("dve", lambda e: e.tensor_tensor(
                                        out=uG[:, c, CW - 1:CW - 1 + HALO], in0=uG[:, c, CW - 1:CW - 1 + HALO],
                                        in1=hmk[:, :], op=ALU.mult), reads=[("uG", c), "cv"], writes=[("uG", c)])
                            for kk in range(CW):
                                for c in (c2, c2 + 1):
                                    wk = V(p + "wdw", c * CW + kk, c * CW + kk + 1)
                                    if kk == 0:
                                        sc.op("dve", lambda e: e.tensor_scalar(
                                            out=yb[:, c, :], in0=uG[:, c, 0:TW], scalar1=wk, scalar2=V(p + "bdw", c, c + 1),
                                            op0=ALU.mult, op1=ALU.add), reads=[("uG", c), "cv"], writes=[("y", c)])
                                    else:
                                        sc.op("dve", lambda e: e.scalar_tensor_tensor(
                                            out=yb[:, c, :], in0=uG[:, c, kk:kk + TW], scalar=wk, in1=yb[:, c, :],
                                            op0=ALU.mult, op1=ALU.add), reads=[("uG", c), ("y", c), "cv"], writes=[("y", c)])
                            for c in (c2, c2 + 1):
                                sc.op("act", lambda e: e.activation(out=ybf[c % 2][:, :], in_=yb[:, c, :], func=AF.Copy),
                                      reads=[("y", c)], writes=[("ybf", c % 2)])
                                sc.op("act", lambda e: e.activation(out=ysq[c % 2][:, :], in_=yb[:, c, :], func=AF.Square),
                                      reads=[("y", c)], writes=[("ysq", c % 2)])
                                sc.mm_part(ps[PS_MU][:, :TW], onesm[:, :], ybf[c % 2][:, :], c == 0, c == 7,
                                           reads=[("ybf", c % 2), "ones"], writes=[("ps", PS_MU)], last=True)
                                sc.mm_part(ps[PS_E2][:, :TW], onesm[:, :], ysq[c % 2][:, :], c == 0, c == 7,
                                           reads=[("ysq", c % 2), "ones"], writes=[("ps", PS_E2)], last=True)
                        sc.op("dve", lambda e: e.tensor_copy(out=uG[:, :, 0:CW - 1], in_=uG[:, :, TW:TW + CW - 1]),
                              reads=[("uG", c) for c in range(8)], writes=[("uG", c) for c in range(8)])
                        sc.op("act", lambda e: e.activation(out=mu[:, :], in_=ps[PS_MU][:, :TW], func=AF.Copy),
                              reads=[("ps", PS_MU)], writes=["mu"])
                        sc.op("dve", lambda e: e.tensor_tensor(out=var[:, :], in0=mu[:, :], in1=mu[:, :], op=ALU.mult),
                              reads=["mu"], writes=["var"])
                        sc.op("dve", lambda e: e.tensor_tensor(out=var[:, :], in0=ps[PS_E2][:, :TW], in1=var[:, :], op=ALU.subtract),
                              reads=[("ps", PS_E2), "var"], writes=["var"])
                        sc.op("dve", lambda e: e.tensor_scalar_max(out=var[:, :], in0=var[:, :], scalar1=0.0),
                              reads=["var"], writes=["var"])
                        sc.op("act", lambda e: e.activation(out=var[:, :], in_=var[:, :], func=AF.Sqrt, bias=epsb[:, 0:1], scale=1.0),
                              reads=["var", "eps"], writes=["var"])
                        sc.op("dve", lambda e: e.reciprocal(out=var[:, :], in_=var[:, :]), reads=["var"], writes=["var"])
                        sc.op("dve", lambda e: e.scalar_tensor_tensor(out=nb[:, :], in0=mu[:, :], scalar=-1.0, in1=var[:, :],
                                                                      op0=ALU.mult, op1=ALU.mult),
                              reads=["mu", "var"], writes=["nb"])
                        for c in range(8):
                            nt_ = ntm[c % 2]
                            sc.op("dve", lambda e: e.tensor_tensor(out=nt_[:, :], in0=yb[:, c, :], in1=var[:, :], op=ALU.mult),
                                  reads=[("y", c), "var"], writes=[("ntm", c % 2)])
                            sc.op("pool", lambda e: e.tensor_tensor(out=nt_[:, :], in0=nt_[:, :], in1=nb[:, :], op=ALU.add),
                                  reads=[("ntm", c % 2), "nb"], writes=[("ntm", c % 2)])
                            sc.op("act", lambda e: e.activation(out=zb[:, c, :], in_=nt_[:, :], func=AF.Silu,
                                                                bias=V(p + "lnb", c, c + 1), scale=V(p + "lng", c, c + 1)),
                                  reads=[("ntm", c % 2), "cv"], writes=[("z", c)])
                        for m in range(8):
                            po = PS_A[m % 2]
                            sc.mm(ps[po][:, :TW], [(w2[:, c, m * 128:(m + 1) * 128], zb[:, c, :]) for c in range(8)],
                                  reads=[("z", c) for c in range(8)] + ["w2"], writes=[("ps", po)])
                            sc.op("dve", lambda e: e.scalar_tensor_tensor(
                                out=xT[:, m, cols], in0=ps[po][:, :TW], scalar=V(p + "b2", m, m + 1), in1=xT[:, m, cols],
                                op0=ALU.add, op1=ALU.add), reads=[("ps", po), ("x", t), "cv"], writes=[("x", t)])

                elif kind == "oproj":
                    wo = ph.enter_context(nc.sbuf_tensor(p + "wos", [128, 8, D], BF16))
                    oT = ph.enter_context(nc.sbuf_tensor(p + "oTs", [128, 8, TT], BF16))
                    wov = wd[p + "wo"].rearrange("(kc p) n -> p kc n", p=128)
                    oTv = wd[p + "oT"].rearrange("(kc p) t -> p kc t", p=128)
                    sc.dma("pool", "wo", [(wo[:, :, :], wov[:, :, :])], writes=["wo"])
                    for t in range(NT):
                        cols = slice(t * TW, (t + 1) * TW)
                        sc.dma("pool", "oT%d" % t, [(oT[:, :, cols], oTv[:, :, cols])], writes=[("oT", t)])
                    for t in range(NT):
                        cols = slice(t * TW, (t + 1) * TW)
                        for m in range(8):
                            po = 1 + (m % 2)
                            sc.mm(ps[po][:, :TW], [(wo[:, kc, m * 128:(m + 1) * 128], oT[:, kc, cols]) for kc in range(8)],
                                  reads=[("oT", t), "wo"], writes=[("ps", po)])
                            sc.op("dve", lambda e: e.tensor_tensor(out=xT[:, m, cols], in0=ps[po][:, :TW], in1=xT[:, m, cols],
                                                                   op=ALU.add),
                                  reads=[("ps", po), ("x", t)], writes=[("x", t)])

                elif kind == "ffn":
                    hall = ph.enter_context(nc.sbuf_tensor(p + "hall", [128, 8, TT + 2], BF16))
                    wa = [ph.enter_context(nc.sbuf_tensor(p + "wa%d" % i, [128, 8, GSZ * 128], BF16)) for i in range(2)]
                    wv = [ph.enter_context(nc.sbuf_tensor(p + "wv%d" % i, [128, 8, GSZ * 128], BF16)) for i in range(2)]
                    wdn = [ph.enter_context(nc.sbuf_tensor(p + "wdn%d" % i, [128, GSZ, D], BF16)) for i in range(2)]
                    sqb = [ph.enter_context(nc.sbuf_tensor(p + "sq%d" % i, [128, 8, TW], BF16)) for i in range(2)]
                    rsb = [ph.enter_context(nc.sbuf_tensor(p + "rs%d" % i, [128, TW], F32)) for i in range(2)]
                    acc = [ph.enter_context(nc.sbuf_tensor(p + "acc%d" % i, [128, TW], F32)) for i in range(2)]
                    sil = [ph.enter_context(nc.sbuf_tensor(p + "sil%d" % i, [128, TW], F32)) for i in range(2)]
                    gb = [ph.enter_context(nc.sbuf_tensor(p + "gb%d" % i, [128, GSZ, TW], BF16)) for i in range(2)]
                    wupv = wd[p + "wup"].rearrange("(kc p) n -> p kc n", p=128)
                    wdnv = wd[p + "wdn"].rearrange("(jc p) n -> p jc n", p=128)

                    def load_group(gi):
                        j0, G = GROUPS[gi]
                        s = gi % 2
                        sc.dma("pool", "wg%d" % s, [
                            (wa[s][:, :, 0:G * 128], wupv[:, :, j0 * 128:(j0 + G) * 128]),
                            (wv[s][:, :, 0:G * 128], wupv[:, :, DFF + j0 * 128:DFF + (j0 + G) * 128]),
                            (wdn[s][:, 0:G, :], wdnv[:, j0:j0 + G, :]),
                        ], writes=[("wg", s)])

                    load_group(0)
                    sc.op("dve", lambda e: e.memset(hall[:, :, 0:2], 0.0), writes=[("h", -1)])
                    for t in range(NT):
                        cols = slice(t * TW, (t + 1) * TW)
                        k = t % 2
                        rms_stats(xT[:, :, cols], [("x", t)], sqb[k], rsb[k], k, 0)
                        for kc in range(8):
                            sc.op("dve", lambda e: e.scalar_tensor_tensor(
                                out=hall[:, kc, 2 + t * TW:2 + (t + 1) * TW], in0=xT[:, kc, cols], scalar=V(p + "g", kc, kc + 1),
                                in1=rsb[k][:, :], op0=ALU.mult, op1=ALU.mult),
                                reads=[("x", t), ("rs", k), "cv"], writes=[("h", t)])
                        if t == 0:
                            sc.op("dve", lambda e: e.tensor_tensor(
                                out=hall[:, :, 2:2 + HALO], in0=hall[:, :, 2:2 + HALO],
                                in1=hmk[:, :].unsqueeze(1).broadcast_to([128, 8, HALO]), op=ALU.mult),
                                reads=[("h", 0), "cv"], writes=[("h", 0)])
                    PA, PV, PD = (1, 2), (3, 4), (5, 6)
                    it = 0
                    pend = None

                    def emit_down(gi, t, gk):
                        j0, G = GROUPS[gi]
                        s = gi % 2
                        cols = slice(t * TW, (t + 1) * TW)
                        for m in range(8):
                            pd = PD[m % 2]
                            sc.mm(ps[pd][:, :TW], [(wdn[s][:, jj, m * 128:(m + 1) * 128], gb[gk][:, jj, :]) for jj in range(G)],
                                  reads=[("g", gk), ("wg", s)], writes=[("ps", pd)])
                            sc.op("dve", lambda e: e.tensor_tensor(out=xT[:, m, cols], in0=ps[pd][:, :TW], in1=xT[:, m, cols],
                                                                   op=ALU.add),
                                  reads=[("ps", pd), ("x", t)], writes=[("x", t)])

                    for gi, (j0, G) in enumerate(GROUPS):
                        s = gi % 2
                        for t in range(NT):
                            gk = it % 2
                            hreads = [("h", t), ("h", t - 1)]
                            for jj in range(G):
                                j = j0 + jj
                                pa, pv = PA[jj % 2], PV[jj % 2]
                                sc.mm(ps[pa][:, :TW + 2], [(wa[s][:, kc, jj * 128:(jj + 1) * 128], hall[:, kc, t * TW:t * TW + TW + 2])
                                                            for kc in range(8)],
                                      reads=hreads + [("wg", s)], writes=[("ps", pa)])
                                sc.mm(ps[pv][:, :TW], [(wv[s][:, kc, jj * 128:(jj + 1) * 128], hall[:, kc, 2 + t * TW:2 + (t + 1) * TW])
                                                       for kc in range(8)],
                                      reads=hreads + [("wg", s)], writes=[("ps", pv)])
                                a_ = acc[jj % 2]
                                sc.op("dve", lambda e: e.tensor_scalar(
                                    out=a_[:, :], in0=ps[pa][:, 2:TW + 2], scalar1=V(p + "wdw", j * 3 + 2, j * 3 + 3),
                                    scalar2=V(p + "bdw", j, j + 1), op0=ALU.mult, op1=ALU.add),
                                    reads=[("ps", pa), "cv"], writes=[("acc", jj % 2)])
                                sc.op("dve", lambda e: e.scalar_tensor_tensor(
                                    out=a_[:, :], in0=ps[pa][:, 1:TW + 1], scalar=V(p + "wdw", j * 3 + 1, j * 3 + 2), in1=a_[:, :],
                                    op0=ALU.mult, op1=ALU.add), reads=[("ps", pa), ("acc", jj % 2), "cv"], writes=[("acc", jj % 2)])
                                sc.op("dve", lambda e: e.scalar_tensor_tensor(
                                    out=a_[:, :], in0=ps[pa][:, 0:TW], scalar=V(p + "wdw", j * 3, j * 3 + 1), in1=a_[:, :],
                                    op0=ALU.mult, op1=ALU.add), reads=[("ps", pa), ("acc", jj % 2), "cv"], writes=[("acc", jj % 2)])
                                sc.op("act", lambda e: e.activation(out=sil[jj % 2][:, :], in_=a_[:, :], func=AF.Silu),
                                      reads=[("acc", jj % 2)], writes=[("sil", jj % 2)])
                                sc.op("dve", lambda e: e.tensor_tensor(out=gb[gk][:, jj, :], in0=ps[pv][:, :TW], in1=sil[jj % 2][:, :],
                                                                       op=ALU.mult),
                                      reads=[("ps", pv), ("sil", jj % 2)], writes=[("g", gk)])
                            if pend is not None:
                                emit_down(*pend)
                            pend = (gi, t, gk)
                            it += 1
                            if t == 0 and gi + 1 < len(GROUPS):
                                if pend is not None and pend[0] != gi:
                                    pass
                                load_group(gi + 1)
                    emit_down(*pend)
                sc.barrier()

        yv = yout.rearrange("(kc p) t -> p kc t", p=128)
        for t in range(NT):
            lo = max(t * TW, HALO)
            hi = (t + 1) * TW
            sc.dma("sp", "yout%d" % t, [(yv[:, :, lo - HALO:hi - HALO], xT[:, :, lo:hi])], reads=[("x", t)])
        sc.final_wait("sp")
    return nc


def shard_xT(xfull):
    outs = []
    for c in range(NCORES):
        b, q = divmod(c, 4)
        t0 = q * TOK
        buf = np.zeros((TT, xfull.shape[2]), np.float32)
        lo = t0 - HALO
        if lo < 0:
            buf[-lo:] = xfull[b, 0:t0 + TOK]
        else:
            buf[:] = xfull[b, lo:t0 + TOK]
        outs.append(np.ascontiguousarray(buf.T))
    return outs


def unshard_yT(ys):
    out = np.zeros((B, S, D), np.float32)
    for c in range(NCORES):
        b, q = divmod(c, 4)
        out[b, q * TOK:(q + 1) * TOK] = ys[c].T
    return out


def dense_host_inputs(phases, P, xfull, ofull=None):
    vp = VecPack()
    wmaps = {}
    for pi, (kind, idx) in enumerate(phases):
        p = "p%d_" % pi
        if kind == "conv":
            li, j = idx
            vp.add(p + "g", fm(P["mix_norm_g"][li], 8))
            vp.add(p + "b1", fm(P["conv_b_pw1"][j], 16))
            wdw = np.asarray(P["conv_w_dw"][j], np.float32)
            vp.add(p + "wdw", np.ascontiguousarray(wdw.reshape(CW, 8, 128).transpose(2, 1, 0)).reshape(128, 8 * CW))
            vp.add(p + "bdw", fm(P["conv_b_dw"][j], 8))
            vp.add(p + "lng", fm(P["conv_ln_g"][j], 8))
            vp.add(p + "lnb", fm(P["conv_ln_b"][j], 8))
            vp.add(p + "b2", fm(P["conv_b_pw2"][j], 8))
            wmaps[p + "w1"] = np.ascontiguousarray(P["conv_w_pw1"][j], np.float32)
            wmaps[p + "w2"] = np.ascontiguousarray(P["conv_w_pw2"][j], np.float32)
        elif kind == "ffn":
            li = idx
            vp.add(p + "g", fm(P["ffn_norm_g"][li], 8))
            wdw = np.asarray(P["ffn_w_dw"][li], np.float32)
            vp.add(p + "wdw", np.ascontiguousarray(wdw.reshape(3, NFC, 128).transpose(2, 1, 0)).reshape(128, NFC * 3))
            vp.add(p + "bdw", fm(P["ffn_b_dw"][li], NFC))
            wmaps[p + "wup"] = np.ascontiguousarray(P["ffn_w_up"][li], np.float32)
            wmaps[p + "wdn"] = np.ascontiguousarray(P["ffn_w_down"][li], np.float32)
        elif kind == "oproj":
            j = idx
            wmaps[p + "wo"] = np.ascontiguousarray(P["nsa_w_out"][j], np.float32)
    voff, nv = dense_vec_layout(phases)
    assert voff == vp.off, (voff, vp.off)
    cvec = vp.array() if vp.n > 0 else np.zeros((128, 1), np.float32)
    xs = shard_xT(xfull)
    oTs = shard_xT(ofull) if ofull is not None else None
    in_maps = []
    for c in range(NCORES):
        q = c % 4
        m = {"xT": xs[c], "cvec": cvec,
             "hmask": np.full((128, HALO), 0.0 if q == 0 else 1.0, np.float32)}
        m.update(wmaps)
        for pi, (kind, idx) in enumerate(phases):
            if kind == "oproj":
                m["p%d_oT" % pi] = oTs[c]
        in_maps.append(m)
    return in_maps


_NC_CACHE = {}


def run_dense(phases, P, xfull, ofull=None):
    key = ("dense", tuple(k for k, _ in phases))
    if key not in _NC_CACHE:
        _NC_CACHE[key] = build_dense(phases)
    nc = _NC_CACHE[key]
    in_maps = dense_host_inputs(phases, P, xfull, ofull)
    res = run_bass_kernel_spmd(nc, in_maps, core_ids=list(range(NCORES)))
    return unshard_yT([r["yT"] for r in res.results])


NTK = 512
NTT = S // NTK
NQT = S // 128
NCMP = 511
ROT = 16


def nsa_consts():
    c = {}
    half = ROT // 2
    inv_freq = (np.float32(500000.0) ** (-np.arange(half, dtype=np.float32) * np.float32(2.0 / ROT))).astype(np.float32)
    ang = (np.arange(S, dtype=np.float32)[None, :] * inv_freq[:, None]).astype(np.float32)
    cos = np.cos(ang).astype(np.float32)
    sin = np.sin(ang).astype(np.float32)
    c["ropeC"] = np.ascontiguousarray(np.concatenate([cos, cos], 0))
    c["ropeS"] = np.ascontiguousarray(np.concatenate([sin, sin], 0))
    prot = np.zeros((128, 128), np.float32)
    for i in range(half):
        prot[64 + i + half, 64 + i] = -1.0
        prot[64 + i, 64 + i + half] = 1.0
    c["prot"] = prot
    c["ident"] = np.eye(128, dtype=np.float32)
    p = np.arange(128)[:, None]
    t = np.arange(128)[None, :]
    c["tri"] = (p <= t).astype(np.float32)
    c["tric"] = (p > t).astype(np.float32)
    cm = np.zeros((17, 128, 128), np.float32)
    for mi in range(17):
        m = 8 * mi
        cm[mi] = (16 * (p - m) + 31 <= t).astype(np.float32)
    c["cmask"] = np.ascontiguousarray(cm.transpose(1, 0, 2))
    n = np.arange(512)[:, None] * 16
    s_ = np.arange(128)[None, :] * 64
    ov = np.minimum(n + 32, s_ + 64) - np.maximum(n, s_)
    cmap = np.maximum(ov, 0).astype(np.float32) / 32.0
    cmap[511] = 0.0
    c["cmap"] = np.ascontiguousarray(cmap.reshape(4, 128, 128).transpose(1, 0, 2))
    ext = np.zeros((128, S), np.float32)
    ext[np.arange(S) // 64, np.arange(S)] = 1.0
    c["ext"] = ext
    d = np.arange(-127, 129)[None, :]
    tt = np.arange(128)[:, None]
    lo = (tt < 64)
    t1 = np.where(lo, (d <= -2), (d <= -1)).astype(np.float32)
    t2 = np.where(lo,
                  np.where((d == -1) | (d == 0), 1e4, np.where(d > 0, -1.0, 0.0)),
                  np.where((d == 0) | (d == 1), 1e4, np.where(d > 1, -1.0, 0.0))).astype(np.float32)
    c["t1"] = np.ascontiguousarray(t1 * np.ones((128, 1), np.float32))
    c["t2"] = np.ascontiguousarray(t2)
    return c


def build_nsa():
    nc = bass.Bass("TRN2", target_bir_lowering=False)

    def din(name, shape):
        return nc.dram_tensor(name, shape, F32, kind="ExternalInput").ap()

    xin = din("xT", [D, S])
    wfm_in = din("wfm", [D, 768])
    wtm_in = din("wtm", [D, 140])
    cv_in = din("cvec", [128, 8 + 6 + 2])
    pek_in = din("pekT", [64, 32]); pev_in = din("pevT", [64, 32])
    ckw1_in = din("ck_w1", [2048, 256]); ckw2_in = din("ck_w2", [256, 64])
    cvw1_in = din("cv_w1", [2048, 256]); cvw2_in = din("cv_w2", [256, 64])
    ropeC_in = din("ropeC", [16, S]); ropeS_in = din("ropeS", [16, S])
    prot_in = din("prot", [128, 128]); ident_in = din("ident", [128, 128])
    tri_in = din("tri", [128, 128]); tric_in = din("tric", [128, 128])
    cmask_in = din("cmask", [128, 17, 128]); cmap_in = din("cmap", [128, 4, 128])
    ext_in = din("ext", [128, S]); t1_in = din("t1", [128, 256]); t2_in = din("t2", [128, 256])
    oout = nc.dram_tensor("o", [S, 256], F32, kind="ExternalOutput").ap()

    with ExitStack() as es:
        sc = Sched(nc, es)
        SB = lambda name, shape, dt: es.enter_context(nc.sbuf_tensor("sb_" + name, shape, dt))
        Q = SB("Q", [128, 4, S], BF16)
        KA = SB("KA", [128, S], BF16)
        KB = SB("KB", [128, S], BF16)
        V1 = SB("V1", [128, NQT, 2, 65], BF16)
        G = SB("G", [128, NQT, 12], F32)
        KCMP = SB("KCMP", [128, 512], BF16)
        VC1 = SB("VC1", [128, 4, 65], BF16)
        cv = SB("cv", [128, 16], F32)
        onesm = SB("onesm", [128, 128], BF16)
        BD = SB("BD", [128, 128], BF16)
        epsb = SB("epsb", [128, 1], F32)
        prot = SB("prot", [128, 128], F32)
        ps = [es.enter_context(nc.psum_tensor("ps%d" % i, [128, 512], F32)) for i in range(7)]
        psb = es.enter_context(nc.psum_tensor("psb", [128, 1024], BF16))

        sc.dma("sp", "c0", [(cv[:, :], cv_in[:, :]), (prot[:, :], prot_in[:, :])], writes=["cv", "prot"])
        sc.op("dve", lambda e: e.memset(onesm[:, :], 1.0 / 1024.0), writes=["ones"])
        sc.op("dve", lambda e: e.memset(epsb[:, :], EPS), writes=["eps"])
        sc.op("dve", lambda e: e.memset(BD[:, :], 0.0), writes=["BD"])
        sc.op("dve", lambda e: e.memset(BD[0:64, 0:64], 1.0 / 64.0), reads=["BD"], writes=["BD"])
        sc.op("dve", lambda e: e.memset(BD[64:128, 64:128], 1.0 / 64.0), reads=["BD"], writes=["BD"])
        sc.op("dve", lambda e: e.memset(V1[:, :, :, 64:65], 1.0), writes=["V1ones"])
        sc.op("dve", lambda e: e.memset(VC1[:, :, 64:65], 1.0), writes=["VC1ones"])

        with ExitStack() as ph:
            PB = lambda name, shape, dt: ph.enter_context(nc.sbuf_tensor("sb_" + name, shape, dt))
            wfm = PB("wfm", [128, 8, 768], BF16)
            wtm = PB("wtm", [128, 8, 140], BF16)
            xt_ = PB("xt_", [128, 8, NTK], F32)
            xt = [xt_, xt_]
            sq1_ = PB("sq_", [128, 8, NTK], BF16)
            sq = [sq1_, sq1_]
            rs = [PB("rs%d" % i, [128, NTK], F32) for i in range(2)]
            hb = [PB("hb%d" % i, [128, 8, NTK], BF16) for i in range(2)]
            csC = [PB("csC%d" % i, [128, NTK], F32) for i in range(2)]
            csS = [PB("csS%d" % i, [128, NTK], F32) for i in range(2)]
            sq2 = [PB("sq2%d" % i, [128, NTK], BF16) for i in range(2)]
            rt = [PB("rt%d" % i, [128, NTK], F32) for i in range(2)]
            qn = [PB("qn%d" % i, [128, NTK], F32) for i in range(2)]
            r1_ = PB("r1_", [128, NTK], F32)
            r2_ = PB("r2_", [128, NTK], F32)
            r1 = [r1_, r1_]
            r2 = [r2_, r2_]
            xv = xin.rearrange("(kc p) t -> p kc t", p=128)
            sc.dma("pool", "wfm", [(wfm[:, :, :], wfm_in.rearrange("(kc p) n -> p kc n", p=128)),
                                   (wtm[:, :, :], wtm_in.rearrange("(kc p) n -> p kc n", p=128))], writes=["wfm"])

            def load_x(tt):
                k = tt % 2
                tok = slice(tt * NTK, (tt + 1) * NTK)
                sc.dma("sp", "xt0", [(xt[k][:, :, :], xv[:, :, tok])], writes=[("xt", 0)])
                sc.dma("sp", "cs%d" % k, [(csC[k][64:80, :], ropeC_in[:, tok]), (csS[k][64:80, :], ropeS_in[:, tok])],
                       writes=[("cs", k)])

            load_x(0)
            cbi = 0
            for tt in range(NTT):
                k = tt % 2
                tok = slice(tt * NTK, (tt + 1) * NTK)
                sc.op("act", lambda e: e.activation(out=sq[k][:, :, :], in_=xt[k][:, :, :], func=AF.Square),
                      reads=[("xt", 0)], writes=[("sq", 0)])
                sc.mm(ps[0][:, :], [(onesm[:, :], sq[k][:, kc, :]) for kc in range(8)],
                      reads=[("sq", 0), "ones"], writes=[("ps", 0)])
                sc.op("act", lambda e: e.activation(out=rs[k][:, :], in_=ps[0][:, :], func=AF.Sqrt, bias=epsb[:, 0:1], scale=1.0),
                      reads=[("ps", 0), "eps"], writes=[("rs", k)])
                sc.op("dve", lambda e: e.reciprocal(out=rs[k][:, :], in_=rs[k][:, :]), reads=[("rs", k)], writes=[("rs", k)])
                for kc in range(8):
                    sc.op("dve", lambda e: e.scalar_tensor_tensor(
                        out=hb[k][:, kc, :], in0=xt[k][:, kc, :], scalar=cv[:, kc:kc + 1], in1=rs[k][:, :],
                        op0=ALU.mult, op1=ALU.mult), reads=[("xt", 0), ("rs", k), "cv"], writes=[("hb", k)])
                if tt + 1 < NTT:
                    load_x(tt + 1)
                for sub in range(4):
                    kt = tt * 4 + sub
                    pt = 1 + sub % 2
                    sc.mm(ps[pt][:, 0:140], [(hb[k][:, kc, sub * 128:(sub + 1) * 128], wtm[:, kc, :]) for kc in range(8)],
                          reads=[("hb", k), "wfm"], writes=[("ps", pt)])
                    sc.op("act", lambda e: e.activation(out=V1[:, kt, :, 0:64],
                                                        in_=ps[pt][:, 0:128].rearrange("p (a b) -> p a b", a=2), func=AF.Copy),
                          reads=[("ps", pt)], writes=[("V1", kt)])
                    sc.op("act", lambda e: e.activation(out=G[:, kt, :], in_=ps[pt][:, 128:140], func=AF.Sigmoid),
                          reads=[("ps", pt)], writes=[("G", kt)])
                for cb in range(6):
                    j = cbi % 2
                    cbi += 1
                    P1, P2 = 3 + j, 5 + j
                    sc.mm(ps[P1][:, :], [(wfm[:, kc, cb * 128:(cb + 1) * 128], hb[k][:, kc, :]) for kc in range(8)],
                          reads=[("hb", k), "wfm"], writes=[("ps", P1)])
                    sc.op("act", lambda e: e.activation(out=sq2[j][:, :], in_=ps[P1][:, :], func=AF.Square),
                          reads=[("ps", P1)], writes=[("sq2", j)])
                    sc.mm(ps[P2][:, :], [(BD[:, :], sq2[j][:, :])], reads=[("sq2", j), "BD"], writes=[("ps", P2)])
                    sc.op("act", lambda e: e.activation(out=rt[j][:, :], in_=ps[P2][:, :], func=AF.Sqrt, bias=epsb[:, 0:1], scale=1.0),
                          reads=[("ps", P2), "eps"], writes=[("rt", j)])
                    sc.op("dve", lambda e: e.reciprocal(out=rt[j][:, :], in_=rt[j][:, :]), reads=[("rt", j)], writes=[("rt", j)])
                    gcol = cv[:, 8 + cb:9 + cb]
                    if cb < 4:
                        sc.op("dve", lambda e: e.scalar_tensor_tensor(
                            out=qn[j][:, :], in0=ps[P1][:, :], scalar=gcol, in1=rt[j][:, :], op0=ALU.mult, op1=ALU.mult),
                            reads=[("ps", P1), ("rt", j), "cv"], writes=[("qn", j)])
                    else:
                        dst = KA if cb == 4 else KB
                        sc.op("act", lambda e: e.activation(out=dst[0:64, tok], in_=ps[P1][0:64, :], func=AF.Copy),
                              reads=[("ps", P1)], writes=[("KAB", cb, tt, 0)])
                        sc.op("dve", lambda e: e.memset(qn[j][0:64, :], 0.0), writes=[("qn", j)], reads=[("qn", j)])
                        sc.op("dve", lambda e: e.scalar_tensor_tensor(
                            out=qn[j][64:128, :], in0=ps[P1][64:128, :], scalar=gcol[64:128, :], in1=rt[j][64:128, :],
                            op0=ALU.mult, op1=ALU.mult),
                            reads=[("ps", P1), ("rt", j), "cv", ("qn", j)], writes=[("qn", j)])
                    PR = 0
                    sc.mm(ps[PR][:, :], [(prot[:, :], qn[j][:, :])], reads=[("qn", j), "prot"], writes=[("ps", PR)])
                    sc.op("dve", lambda e: e.tensor_tensor(out=r1[j][64:80, :], in0=qn[j][64:80, :], in1=csC[k][64:80, :], op=ALU.mult),
                          reads=[("qn", j), ("cs", k)], writes=[("r1", 0)])
                    sc.op("dve", lambda e: e.tensor_tensor(out=r2[j][64:80, :], in0=ps[PR][64:80, :], in1=csS[k][64:80, :], op=ALU.mult),
                          reads=[("ps", PR), ("cs", k)], writes=[("r2", 0)])
                    sc.op("dve", lambda e: e.tensor_tensor(out=qn[j][64:80, :], in0=r1[j][64:80, :], in1=r2[j][64:80, :], op=ALU.add),
                          reads=[("r1", 0), ("r2", 0), ("qn", j), ("ps", PR)], writes=[("qn", j)])
                    if cb < 4:
                        sc.op("act", lambda e: e.activation(out=Q[0:64, cb, tok], in_=qn[j][0:64, :], func=AF.Copy),
                              reads=[("qn", j)], writes=[("Q", cb, tt, 0)])
                        sc.op("pool", lambda e: e.tensor_copy(out=Q[64:128, cb, tok], in_=qn[j][64:128, :]),
                              reads=[("qn", j)], writes=[("Q", cb, tt, 1)])
                    else:
                        dst = KA if cb == 4 else KB
                        sc.op("pool", lambda e: e.tensor_copy(out=dst[64:128, tok], in_=qn[j][64:128, :]),
                              reads=[("qn", j)], writes=[("KAB", cb, tt, 1)])
            sc.barrier()

        with ExitStack() as ph:
            PB = lambda name, shape, dt: ph.enter_context(nc.sbuf_tensor("sb_" + name, shape, dt))
            w1 = [PB("cw1%d" % i, [64, 32, 256], BF16) for i in range(2)]
            w2 = [PB("cw2%d" % i, [128, 2, 64], BF16) for i in range(2)]
            peT = [PB("peT%d" % i, [64, 32], BF16) for i in range(2)]
            hid = [PB("hid%d" % i, [128, 2, 512], BF16) for i in range(2)]
            bia = [PB("bia%d" % i, [128, 2], F32) for i in range(2)]
            ksq = PB("ksq", [64, 512], BF16)
            krt = PB("krt", [64, 512], F32)
            for i, (a1, a2, ap_) in enumerate([(ckw1_in, ckw2_in, pek_in), (cvw1_in, cvw2_in, pev_in)]):
                sc.dma("pool", "cw%d" % i, [
                    (w1[i][:, :, :], a1.rearrange("(l d) h -> d l h", d=64)),
                    (w2[i][:, :, :], a2.rearrange("(hc p) d -> p hc d", p=128)),
                    (peT[i][:, :], ap_[:, :])], writes=[("cw", i)])
                sc.op("dve", lambda e: e.memset(hid[i][:, :, 511:512], 0.0), writes=[("hid", i)])
            for i in range(2):
                src = KA if i == 0 else KB
                for hc in range(2):
                    sc.mm(ps[0][:, hc:hc + 1], [(w1[i][:, l, hc * 128:(hc + 1) * 128], peT[i][:, l:l + 1]) for l in range(32)],
                          reads=[("cw", i)], writes=[("ps", 0, hc)])
                    sc.op("act", lambda e: e.activation(out=bia[i][:, hc:hc + 1], in_=ps[0][:, hc:hc + 1], func=AF.Copy),
                          reads=[("ps", 0, hc)], writes=[("bia", i, hc)])
                    pb_ = 1 + hc
                    sc.mm(ps[pb_][:, 0:NCMP],
                          [(w1[i][:, l, hc * 128:(hc + 1) * 128], src[0:64, l:l + 16 * (NCMP - 1) + 1:16]) for l in range(32)],
                          reads=[("cw", i)], writes=[("ps", pb_)])
                    sc.op("act", lambda e: e.activation(out=hid[i][:, hc, 0:NCMP], in_=ps[pb_][:, 0:NCMP], func=AF.Silu,
                                                        bias=bia[i][:, hc:hc + 1], scale=1.0),
                          reads=[("ps", pb_), ("bia", i, hc), ("hid", i)], writes=[("hid", i)])
            sc.mm(ps[3][0:64, :], [(w2[0][:, hc, :], hid[0][:, hc, :]) for hc in range(2)],
                  reads=[("hid", 0), ("cw", 0)], writes=[("ps", 3)])
            sc.op("act", lambda e: e.activation(out=ksq[:, :], in_=ps[3][0:64, :], func=AF.Square), reads=[("ps", 3)], writes=["ksq"])
            sc.mm(ps[4][0:64, :], [(BD[0:64, 0:64], ksq[:, :])], reads=["ksq", "BD"], writes=[("ps", 4)])
            sc.op("act", lambda e: e.activation(out=krt[:, :], in_=ps[4][0:64, :], func=AF.Sqrt, bias=epsb[0:64, 0:1], scale=1.0),
                  reads=[("ps", 4), "eps"], writes=["krt"])
            sc.op("dve", lambda e: e.reciprocal(out=krt[:, :], in_=krt[:, :]), reads=["krt"], writes=["krt"])
            sc.op("dve", lambda e: e.scalar_tensor_tensor(out=KCMP[0:64, :], in0=ps[3][0:64, :], scalar=cv[0:64, 14:15], in1=krt[:, :],
                                                          op0=ALU.mult, op1=ALU.mult),
                  reads=[("ps", 3), "krt", "cv"], writes=["KCMP"])
            for cc in range(4):
                pb_ = 5 + cc % 2
                sc.mm(ps[pb_][:, 0:64], [(hid[1][:, hc, cc * 128:(cc + 1) * 128], w2[1][:, hc, :]) for hc in range(2)],
                      reads=[("hid", 1), ("cw", 1)], writes=[("ps", pb_)])
                sc.op("act", lambda e: e.activation(out=VC1[:, cc, 0:64], in_=ps[pb_][:, 0:64], func=AF.Copy),
                      reads=[("ps", pb_)], writes=[("VC1", cc)])
            sc.barrier()

        with ExitStack() as ph:
            PB = lambda name, shape, dt: ph.enter_context(nc.sbuf_tensor("sb_" + name, shape, dt))
            ident = PB("ident", [128, 128], BF16)
            tri = PB("tri", [128, 128], BF16)
            tric = PB("tric", [128, 128], BF16)
            cmask = PB("cmask", [128, 17, 128], BF16)
            cmap = PB("cmap", [128, 4, 128], BF16)
            ext = PB("ext", [128, S], BF16)
            t1 = PB("t1", [128, 256], F32)
            t2 = PB("t2", [128, 256], F32)
            NE = 8
            Eb = [PB("E%d" % i, [128, 512], BF16) for i in range(NE)]
            impb = PB("impb", [128, 128], F32)
            imp2 = PB("imp2", [128, 128], F32)
            imp3 = PB("imp3", [128, 128], F32)
            m8 = PB("m8", [128, 16], F32)
            selb = PB("selb", [128, 128], BF16)
            selT = PB("selT", [128, 128], BF16)
            mkd = PB("mkd", [128, 128], BF16)
            mks = [PB("mks%d" % i, [128, 512], BF16) for i in range(2)]
            rz = PB("rz", [128, 3, 4], F32)
            cf = PB("cf", [128, 3, 4], F32)
            osb = [PB("osb%d" % i, [128, 4, 64], F32) for i in range(2)]
            otm = PB("otm", [128, 4, 64], F32)
            sc.dma("pool", "c3", [(ident[:, :], ident_in[:, :]), (tri[:, :], tri_in[:, :]), (tric[:, :], tric_in[:, :]),
                                  (cmask[:, :, :], cmask_in[:, :, :]), (cmap[:, :, :], cmap_in[:, :, :])], writes=["c3"])
            sc.dma("pool", "c3b", [(ext[:, 0:4096], ext_in[:, 0:4096]), (ext[:, 4096:S], ext_in[:, 4096:S])], writes=["ext"])
            sc.dma("sp", "c3c", [(t1[:, :], t1_in[:, :]), (t2[:, :], t2_in[:, :])], writes=["t12"])
            PS_S = (0, 1)
            PS_OC, PS_OS, PS_OW, PS_IM, PS_MK = 2, 3, 4, 5, 6
            ecnt = [0]
            scnt = [0]
            mkc = [0]
            MKB = (6, 5)

            SBATCH = 3
            pend = []

            def emitSE(u):
                lhsT, q_rhs, kreads, mask_fn, vrhs, vreads, po, first, last_ = u["a"]
                pS = PS_S[scnt[0] % 2]; scnt[0] += 1
                ei = ecnt[0] % NE; ecnt[0] += 1
                E = Eb[ei]
                u["E"] = E; u["ei"] = ei
                sc.mm(ps[pS][:, :], [(lhsT, q_rhs)], reads=kreads, writes=[("ps", pS)])
                sc.op("act", lambda e: e.activation(out=E[:, :], in_=ps[pS][:, :], func=AF.Exp, scale=0.125),
                      reads=[("ps", pS)], writes=[("E", ei)])
                if mask_fn is not None:
                    mask_fn(E, ei)

            def emitPV(u):
                lhsT, q_rhs, kreads, mask_fn, vrhs, vreads, po, first, last_ = u["a"]
                E, ei = u["E"], u["ei"]
                waits = sc._collect("pe", [("E", ei)] + vreads, [("ps", po)])
                sc._emit_waits("pe", waits)
                for r in range(4):
                    inst = nc.tensor.matmul(ps[po][:, r * 65:(r + 1) * 65], lhsT=E[:, r * 128:(r + 1) * 128], rhs=vrhs,
                                            start=(first and r == 0), stop=(last_ and r == 3), skip_group_check=True)
                sc.pe_inc(inst)
                sc._record(("pe", sc.cnt["pe"], "pe"), [("E", ei)] + vreads, [("ps", po)])

            mode = ["full"]
            sb = [0]

            def pe_mode(m):
                if mode[0] != m:
                    if sc.waited["pe"].get("pe", 0) < sc.cnt["pe"]:
                        sc._emit_waits("pe", {"pe": sc.cnt["pe"]})
                    mode[0] = m

            def push(*a):
                u = {"a": a}
                pe_mode("tile")
                emitSE(u)
                pend.append(u)
                sb[0] += 1
                if sb[0] >= SBATCH:
                    sb[0] = 0
                    if len(pend) > SBATCH:
                        pe_mode("full")
                        while len(pend) > SBATCH:
                            emitPV(pend.pop(0))
                return u

            def flush():
                sb[0] = 0
                if pend:
                    pe_mode("full")
                while pend:
                    emitPV(pend.pop(0))

            def bmask(mask_ap_fn, mreads):
                def f(E, ei):
                    sc.op("dve", lambda e: e.tensor_tensor(
                        out=E[:, :].rearrange("p (r t) -> p r t", r=4), in0=E[:, :].rearrange("p (r t) -> p r t", r=4),
                        in1=mask_ap_fn(), op=ALU.mult), reads=[("E", ei)] + mreads, writes=[("E", ei)])
                return f

            for qt in range(NQT):
                qtok = slice(qt * 128, (qt + 1) * 128)
                qn_rhs = Q[0:64, :, qtok]
                qr_rhs = Q[64:128, :, qtok]
                qreads = []
                ncn = min(4, (8 * qt + 7 + 127) // 128)
                Es = []
                for cc in range(ncn):
                    m = 8 * qt - 128 * cc
                    mf = None
                    if 0 <= m <= 128:
                        mi = m // 8
                        mf = bmask(lambda mi=mi: cmask[:, mi, :].unsqueeze(1).broadcast_to([128, 4, 128]), ["c3"])
                    u = push(KCMP[0:64, cc * 128:(cc + 1) * 128], qn_rhs, ["KCMP"] + qreads, mf,
                             VC1[:, cc, :], [("VC1", c_) for c_ in range(4)] + ["VC1ones"], PS_OC, cc == 0, cc == ncn - 1)
                    Es.append((u, cc))
                flush()
                pe_mode("full")
                nmm = len(Es) * 4
                i_ = 0
                for (u, cc) in Es:
                    E, ei = u["E"], u["ei"]
                    for r in range(4):
                        waits = sc._collect("pe", [("E", ei), "c3"], [("mkb", 1)])
                        sc._emit_waits("pe", waits)
                        inst = nc.tensor.matmul(ps[PS_IM][:, r * 128:(r + 1) * 128], lhsT=E[:, r * 128:(r + 1) * 128],
                                                rhs=cmap[:, cc, :], start=(i_ == 0), stop=(i_ == nmm - 1), skip_group_check=True)
                        i_ += 1
                        sc._record(("pe", sc.cnt["pe"] + 1, "pe"), [("E", ei), "c3"], [("mkb", 1)])
                sc.pe_inc(inst)
                sc.op("dve", lambda e: e.tensor_scalar_max(
                    out=rz[:, 0, :], in0=ps[PS_OC][:, 0:260].rearrange("p (r c) -> p r c", r=4)[:, :, 64], scalar1=1e-30),
                    reads=[("ps", PS_OC)], writes=[("rz", 0)])
                sc.op("dve", lambda e: e.reciprocal(out=rz[:, 0, :], in_=rz[:, 0, :]), reads=[("rz", 0)], writes=[("rz", 0)])
                for r in range(4):
                    if r == 0:
                        sc.op("dve", lambda e: e.tensor_scalar(out=impb[:, :], in0=ps[PS_IM][:, 0:128], scalar1=rz[:, 0, 0:1],
                                                               scalar2=None, op0=ALU.mult),
                              reads=[("mkb", 1), ("rz", 0)], writes=["impb"])
                    else:
                        sc.op("dve", lambda e: e.scalar_tensor_tensor(
                            out=impb[:, :], in0=ps[PS_IM][:, r * 128:(r + 1) * 128], scalar=rz[:, 0, r:r + 1], in1=impb[:, :],
                            op0=ALU.mult, op1=ALU.add), reads=[("mkb", 1), ("rz", 0), "impb"], writes=["impb"])
                off = 127 - 2 * qt
                sc.op("dve", lambda e: e.tensor_tensor(out=imp2[:, :], in0=impb[:, :], in1=t1[:, off:off + 128], op=ALU.mult),
                      reads=["impb", "t12"], writes=["imp2"])
                sc.op("dve", lambda e: e.tensor_tensor(out=imp2[:, :], in0=imp2[:, :], in1=t2[:, off:off + 128], op=ALU.add),
                      reads=["imp2", "t12"], writes=["imp2"])
                sc.op("dve", lambda e: e.memset(imp2[:, 0:1], 1e4), reads=["imp2"], writes=["imp2"])
                sc.op("dve", lambda e: e.max(out=m8[:, 0:8], in_=imp2[:, :]), reads=["imp2"], writes=["m8a"])
                sc.op("dve", lambda e: e.match_replace(out=imp3[:, :], in_to_replace=m8[:, 0:8], in_values=imp2[:, :], imm_value=-1e30),
                      reads=["imp2", "m8a"], writes=["imp3"])
                sc.op("dve", lambda e: e.max(out=m8[:, 8:16], in_=imp3[:, :]), reads=["imp3"], writes=["m8b"])
                sc.op("dve", lambda e: e.tensor_scalar(out=selb[:, :], in0=imp2[:, :], scalar1=m8[:, 15:16], scalar2=None, op0=ALU.is_ge),
                      reads=["imp2", "m8b"], writes=["selb"])
                k0 = max(0, qt - 4)
                for kt in range(k0, qt + 1):
                    ktok = slice(kt * 128, (kt + 1) * 128)
                    if kt == qt:
                        mf = bmask(lambda: tri[:, :].unsqueeze(1).broadcast_to([128, 4, 128]), ["c3"])
                    elif kt == qt - 4:
                        mf = bmask(lambda: tric[:, :].unsqueeze(1).broadcast_to([128, 4, 128]), ["c3"])
                    else:
                        mf = None
                    push(KB[64:128, ktok], qr_rhs, qreads, mf,
                         V1[:, kt, 1, :], [("V1", kt), "V1ones"], PS_OW, kt == k0, kt == qt)
                mode[0] = "x"
                pe_mode("full")
                waits = sc._collect("pe", ["selb", "c3"], ["psb"])
                sc._emit_waits("pe", waits)
                inst = nc.tensor.transpose(psb[:, 0:128], selb[:, :], ident[:, :])
                sc.pe_inc(inst)
                sc._record(("pe", sc.cnt["pe"], "pe"), ["selb", "c3"], ["psb"])
                sc.op("act", lambda e: e.activation(out=selT[:, :], in_=psb[:, 0:128], func=AF.Copy), reads=["psb"], writes=["selT"])
                for kt in range(qt + 1):
                    kb = kt % 4
                    if kb == 0:
                        nk = min(4, qt + 1 - kt)
                        pe_mode("full")
                        mkc[0] += 1
                        mb = MKB[mkc[0] % 2]
                        mkey = ("mkb", mkc[0] % 2)
                        for u_ in range(nk):
                            sc.mm(ps[mb][:, u_ * 128:(u_ + 1) * 128],
                                  [(ext[:, (kt + u_) * 128:(kt + u_ + 1) * 128], selT[:, :])],
                                  reads=["ext", "selT"], writes=[mkey])
                        mks_ = mks[mkc[0] % 2]
                        skey = ("mks", mkc[0] % 2)
                        sc.op("act", lambda e: e.activation(out=mks_[:, 0:nk * 128], in_=ps[mb][:, 0:nk * 128], func=AF.Copy),
                              reads=[mkey], writes=[skey])
                    ktok = slice(kt * 128, (kt + 1) * 128)
                    if kt == qt:
                        sc.op("dve", lambda e: e.tensor_tensor(out=mkd[:, :], in0=mks_[:, kb * 128:(kb + 1) * 128], in1=tri[:, :],
                                                               op=ALU.mult), reads=[skey, "c3"], writes=["mkd"])
                        mf = bmask(lambda: mkd[:, :].unsqueeze(1).broadcast_to([128, 4, 128]), ["mkd"])
                    else:
                        mf = bmask(lambda kb=kb, mks_=mks_: mks_[:, kb * 128:(kb + 1) * 128].unsqueeze(1).broadcast_to([128, 4, 128]),
                                   [skey])
                    push(KA[64:128, ktok], qr_rhs, qreads, mf,
                         V1[:, kt, 0, :], [("V1", kt), "V1ones"], PS_OS, kt == 0, kt == qt)
                flush()
                for bi, po in ((1, PS_OS), (2, PS_OW)):
                    sc.op("dve", lambda e: e.tensor_scalar_max(
                        out=rz[:, bi, :], in0=ps[po][:, 0:260].rearrange("p (r c) -> p r c", r=4)[:, :, 64], scalar1=1e-30),
                        reads=[("ps", po)], writes=[("rz", bi)])
                    sc.op("dve", lambda e: e.reciprocal(out=rz[:, bi, :], in_=rz[:, bi, :]), reads=[("rz", bi)], writes=[("rz", bi)])
                sc.op("dve", lambda e: e.tensor_tensor(out=cf[:, :, :], in0=rz[:, :, :],
                                                       in1=G[:, qt, :].rearrange("p (r b) -> p b r", b=3), op=ALU.mult),
                      reads=[("rz", 0), ("rz", 1), ("rz", 2), ("G", qt)], writes=["cf"])
                ob = osb[qt % 2]
                for bi, po in ((0, PS_OC), (1, PS_OS), (2, PS_OW)):
                    dst = ob if bi == 0 else otm
                    sc.op("dve", lambda e: e.tensor_tensor(
                        out=dst[:, :, :], in0=ps[po][:, 0:260].rearrange("p (r c) -> p r c", r=4)[:, :, 0:64],
                        in1=cf[:, bi, :].unsqueeze(2).broadcast_to([128, 4, 64]), op=ALU.mult),
                        reads=[("ps", po), "cf"], writes=[("osb", qt % 2) if bi == 0 else "otm"])
                    if bi > 0:
                        sc.op("dve", lambda e: e.tensor_tensor(out=ob[:, :, :], in0=ob[:, :, :], in1=otm[:, :, :], op=ALU.add),
                              reads=[("osb", qt % 2), "otm"], writes=[("osb", qt % 2)])
                sc.dma("sp", "oo%d" % (qt % 2), [(oout[qtok, :], ob[:, :, :].rearrange("p r c -> p (r c)"))],
                       reads=[("osb", qt % 2)])
            sc.barrier()
        sc.final_wait("sp")
        ok, stuck = sc.check_deadlock()
        if not ok:
            raise RuntimeError("deadlock in semaphore program: %r" % (stuck,))
    return nc


def nsa_host_inputs(P, j, li, xfull):
    consts = nsa_consts()
    w_in = np.asarray(P["nsa_w_in"][j], np.float32)
    offs = np.cumsum([0, 1024, 256, 256, 256, 256, 256, 256])
    oq, okc, ovc, oks, ovs, okw, ovw, ogl = [int(v) for v in offs]
    in_maps = []
    for c in range(NCORES):
        b, g = divmod(c, 4)
        cols = []
        for r in range(4):
            qc = w_in[:, oq + (g * 4 + r) * 64: oq + (g * 4 + r + 1) * 64]
            cols += [qc, qc]
        sl = lambda o: w_in[:, o + g * 64:o + (g + 1) * 64]
        cols += [sl(okc), sl(oks), sl(ovc), sl(okw)]
        wfm = np.ascontiguousarray(np.concatenate(cols, axis=1))
        wtm = np.ascontiguousarray(np.concatenate([sl(ovs), sl(ovw), w_in[:, ogl + g * 12: ogl + (g + 1) * 12]], axis=1))
        cvec = np.ones((128, 16), np.float32)
        cvec[:, 0:8] = fm(P["mix_norm_g"][li], 8)
        qg = np.asarray(P["nsa_q_norm"][j], np.float32)
        for r in range(4):
            cvec[0:64, 8 + r] = qg
            cvec[64:128, 8 + r] = qg
        cvec[64:128, 12] = np.asarray(P["nsa_ks_norm"][j], np.float32)
        cvec[64:128, 13] = np.asarray(P["nsa_kw_norm"][j], np.float32)
        cvec[0:64, 14] = np.asarray(P["nsa_kc_norm"][j], np.float32)
        m = {"xT": np.ascontiguousarray(xfull[b].T), "wfm": wfm, "wtm": wtm, "cvec": cvec,
             "pekT": np.ascontiguousarray(np.asarray(P["nsa_pe_k"][j], np.float32).T),
             "pevT": np.ascontiguousarray(np.asarray(P["nsa_pe_v"][j], np.float32).T),
             "ck_w1": np.ascontiguousarray(P["nsa_ck_w1"][j], np.float32), "ck_w2": np.ascontiguousarray(P["nsa_ck_w2"][j], np.float32),
             "cv_w1": np.ascontiguousarray(P["nsa_cv_w1"][j], np.float32), "cv_w2": np.ascontiguousarray(P["nsa_cv_w2"][j], np.float32)}
        m.update(consts)
        in_maps.append(m)
    return in_maps


def run_nsa(P, j, li, xfull):
    if "nsa" not in _NC_CACHE:
        _NC_CACHE["nsa"] = build_nsa()
    nc = _NC_CACHE["nsa"]
    in_maps = nsa_host_inputs(P, j, li, xfull)
    res = run_bass_kernel_spmd(nc, in_maps, core_ids=list(range(NCORES)))
    o = np.zeros((B, S, D), np.float32)
    for c in range(NCORES):
        b, g = divmod(c, 4)
        o[b, :, g * 256:(g + 1) * 256] = res.results[c]["o"]
    return o


def kernel(**inputs):
    P = {k: np.asarray(v) for k, v in inputs.items()}
    x = np.ascontiguousarray(P["x"], np.float32)
    x = run_dense([("conv", (0, 0)), ("ffn", 0)], P, x)
    o = run_nsa(P, 0, 1, x)
    x = run_dense([("oproj", 0), ("ffn", 1), ("conv", (2, 1)), ("ffn", 2)], P, x, ofull=o)
    o = run_nsa(P, 1, 3, x)
    x = run_dense([("oproj", 1), ("ffn", 3)], P, x, ofull=o)
    return x.astype(np.float32)
```
